# Optimizing a Trainium2 kernel written in Bass

```python
import math
import jax, jax.numpy as jnp
from jax import lax
import numpy as np

D_MODEL = 1024
BATCH = 2
SEQ = 8192
DEPTH = 1

CHUNK = 64
Q_BLOCK = 128
MEM_LEN = 256

DIFF_HEADS = 4
DIFF_QK_DIM = 64
DIFF_V_DIM = 2 * DIFF_QK_DIM
RET_HEADS = 4
RET_QK_DIM = 64
RET_V_DIM = 128
MEM_HEADS = 4
MEM_HEAD_DIM = 128

N_BRANCHES = 3
FFN_DIM = 2816
CONV_WIDTH = 3
LN_EPS = 1e-5
DEEPNORM_ALPHA = (2.0 * DEPTH) ** 0.25
DEEPNORM_BETA = (8.0 * DEPTH) ** -0.25

DIFF_QK_W = DIFF_HEADS * 2 * DIFF_QK_DIM
DIFF_V_W = DIFF_HEADS * DIFF_V_DIM
RET_QK_W = RET_HEADS * RET_QK_DIM
RET_V_W = RET_HEADS * RET_V_DIM
MEM_W = MEM_HEADS * MEM_HEAD_DIM
IN_WIDTHS = (DIFF_QK_W, DIFF_QK_W, DIFF_V_W, RET_QK_W, RET_QK_W, RET_V_W, RET_V_W, MEM_W)
IN_DIM = sum(IN_WIDTHS)
IN_OFFSETS = tuple(int(v) for v in np.cumsum(IN_WIDTHS)[:-1])

kernel_name = "hybrid_diffattn_retention_memxattn_convffn_deepnorm"


def layer_norm(x, g, b):
    xf = x.astype(jnp.float32)
    mu = jnp.mean(xf, axis=-1, keepdims=True)
    var = jnp.mean(jnp.square(xf - mu), axis=-1, keepdims=True)
    return ((xf - mu) * lax.rsqrt(var + LN_EPS) * g + b).astype(x.dtype)


def head_rmsnorm(y, g):
    B, S, H, E = y.shape
    yf = y.astype(jnp.float32)
    yf = yf * lax.rsqrt(jnp.mean(jnp.square(yf), axis=-1, keepdims=True) + LN_EPS)
    return (yf * g.reshape(H, E)).astype(y.dtype).reshape(B, S, H * E)


def head_groupnorm(y, g):
    B, S, H, E = y.shape
    yf = y.astype(jnp.float32)
    mu = jnp.mean(yf, axis=-1, keepdims=True)
    var = jnp.mean(jnp.square(yf - mu), axis=-1, keepdims=True)
    return ((yf - mu) * lax.rsqrt(var + LN_EPS) * g.reshape(H, E)).astype(y.dtype).reshape(B, S, H * E)


def alibi_slopes(n_heads):
    start = 2.0 ** (-8.0 / n_heads)
    return jnp.asarray([start ** (i + 1) for i in range(n_heads)], dtype=jnp.float32)


def diff_attention(q, k, v, lam):
    B, S, H, _, d = q.shape
    n_qb = S // Q_BLOCK
    scale = 1.0 / math.sqrt(d)
    slopes = alibi_slopes(H)
    pos = jnp.arange(S, dtype=jnp.int32)
    key_chunk = pos // CHUNK
    q_blocks = q.reshape(B, n_qb, Q_BLOCK, H, 2, d).transpose(1, 0, 2, 3, 4, 5)
    q_pos = pos.reshape(n_qb, Q_BLOCK)

    def one_block(args):
        qb, qp = args
        s = jnp.einsum("bqhid,bkhid->bhiqk", qb, k).astype(jnp.float32) * scale
        dist = jnp.abs(qp[:, None] - pos[None, :]).astype(jnp.float32)
        bias = -slopes[:, None, None] * dist
        allowed = key_chunk[None, :] <= (qp // CHUNK)[:, None]
        s = jnp.where(allowed, s + bias[None, :, None], -jnp.inf)
        p = jax.nn.softmax(s, axis=-1)
        a = p[:, :, 0] - lam * p[:, :, 1]
        return jnp.einsum("bhqk,bkhe->bqhe", a.astype(v.dtype), v)

    out = lax.map(one_block, (q_blocks, q_pos))
    return out.transpose(1, 0, 2, 3, 4).reshape(B, S, H, v.shape[-1])


def retention(q, k, v):
    B, S, H, dk = q.shape
    dv = v.shape[-1]
    nc = S // CHUNK
    lg = jnp.log(1.0 - 2.0 ** (-5.0 - jnp.arange(H, dtype=jnp.float32)))
    q = q.reshape(B, nc, CHUNK, H, dk)
    k = k.reshape(B, nc, CHUNK, H, dk) * (dk ** -0.5)
    v = v.reshape(B, nc, CHUNK, H, dv)
    idx = jnp.arange(CHUNK, dtype=jnp.float32)
    n_minus_m = idx[:, None] - idx[None, :]
    intra_decay = jnp.where(n_minus_m >= 0,
                            jnp.exp(lg[:, None, None] * jnp.maximum(n_minus_m, 0.0)), 0.0)
    s = jnp.einsum("bcnhd,bcmhd->bchnm", q, k) * intra_decay
    intra = jnp.einsum("bchnm,bcmhe->bcnhe", s, v)
    k_decay = jnp.exp(lg[None, :] * (CHUNK - 1.0 - idx)[:, None])
    kv = jnp.einsum("bcmhd,bcmhe->cbhde", k * k_decay[:, :, None], v)
    chunk_decay = jnp.exp(lg * CHUNK)[None, :, None, None]

    def step(state, kv_c):
        return state * chunk_decay + kv_c, state

    _, state_prev = lax.scan(step, jnp.zeros(kv.shape[1:], kv.dtype), kv)
    q_decay = jnp.exp(lg[None, :] * (idx + 1.0)[:, None])
    cross = jnp.einsum("bcnhd,cbhde->bcnhe", q * q_decay[:, :, None], state_prev)
    return (intra + cross).reshape(B, S, H, dv)


def memory_attention(q, k, v):
    s = jnp.einsum("bshe,bmhe->bhsm", q, k).astype(jnp.float32) * (q.shape[-1] ** -0.5)
    p = jax.nn.softmax(s, axis=-1)
    return jnp.einsum("bhsm,bmhe->bshe", p.astype(v.dtype), v)


def token_mixers(u, mem, w_in, lam_p, subln_g, ret_g, w_mem_kv, w_diff_o, w_ret_o, w_mem_o,
                 w_gate, b_gate, w_out, lambda_init):
    B, S, _ = u.shape
    proj = u @ w_in
    dq, dk, dv, rq, rk, rv, rg, mq = jnp.split(proj, IN_OFFSETS, axis=-1)

    lpf = lam_p.astype(jnp.float32)
    lam = jnp.exp(jnp.sum(lpf[0] * lpf[1])) - jnp.exp(jnp.sum(lpf[2] * lpf[3])) + lambda_init
    a = diff_attention(dq.reshape(B, S, DIFF_HEADS, 2, DIFF_QK_DIM),
                       dk.reshape(B, S, DIFF_HEADS, 2, DIFF_QK_DIM),
                       dv.reshape(B, S, DIFF_HEADS, DIFF_V_DIM), lam)
    a = head_rmsnorm(a, subln_g) * (1.0 - lambda_init)

    r = retention(rq.reshape(B, S, RET_HEADS, RET_QK_DIM),
                  rk.reshape(B, S, RET_HEADS, RET_QK_DIM),
                  rv.reshape(B, S, RET_HEADS, RET_V_DIM))
    r = head_groupnorm(r, ret_g) * jax.nn.silu(rg)

    mk, mv = jnp.split(mem @ w_mem_kv, 2, axis=-1)
    M = mem.shape[1]
    m = memory_attention(mq.reshape(B, S, MEM_HEADS, MEM_HEAD_DIM),
                         mk.reshape(B, M, MEM_HEADS, MEM_HEAD_DIM),
                         mv.reshape(B, M, MEM_HEADS, MEM_HEAD_DIM)).reshape(B, S, MEM_W)

    gates = jax.nn.sigmoid(u @ w_gate + b_gate).reshape(B, S, N_BRANCHES, D_MODEL)
    merged = (gates[:, :, 0] * (a @ w_diff_o)
              + gates[:, :, 1] * (r @ w_ret_o)
              + gates[:, :, 2] * (m @ w_mem_o))
    return merged @ w_out


def conv_ffn(u, w_up, conv_w, conv_b, w_down):
    h = u @ w_up
    C = h.shape[-1]
    h = lax.conv_general_dilated(h, conv_w.reshape(CONV_WIDTH, 1, C).astype(h.dtype),
                                 window_strides=(1,), padding=((CONV_WIDTH - 1, 0),),
                                 dimension_numbers=("NWC", "WIO", "NWC"),
                                 feature_group_count=C) + conv_b
    g, val = jnp.split(h, 2, axis=-1)
    return (jax.nn.silu(g) * val) @ w_down


def setup_inputs(seed: int = 0) -> dict:
    key = jax.random.key(seed)
    ks = jax.random.split(key, 21)
    f32 = jnp.float32
    L, D, F2 = DEPTH, D_MODEL, 2 * FFN_DIM

    def nrm(k, shape, scale):
        return jax.random.normal(k, shape, f32) * scale

    return {
        "x": nrm(ks[0], (BATCH, SEQ, D), 1.0),
        "mem": nrm(ks[1], (BATCH, MEM_LEN, D), 1.0),
        "w_in": nrm(ks[2], (L, D, IN_DIM), D ** -0.5),
        "diff_lambda": nrm(ks[3], (L, 4, DIFF_QK_DIM), 0.1),
        "diff_subln_g": 1.0 + nrm(ks[4], (L, DIFF_V_W), 0.02),
        "ret_norm_g": 1.0 + nrm(ks[5], (L, RET_V_W), 0.02),
        "w_mem_kv": nrm(ks[6], (L, D, 2 * MEM_W), D ** -0.5),
        "w_diff_o": nrm(ks[7], (L, DIFF_V_W, D), DIFF_V_W ** -0.5 * DEEPNORM_BETA),
        "w_ret_o": nrm(ks[8], (L, RET_V_W, D), RET_V_W ** -0.5 * DEEPNORM_BETA),
        "w_mem_o": nrm(ks[9], (L, MEM_W, D), MEM_W ** -0.5 * DEEPNORM_BETA),
        "w_gate": nrm(ks[10], (L, D, N_BRANCHES * D), D ** -0.5),
        "b_gate": nrm(ks[11], (L, N_BRANCHES * D), 0.01),
        "w_mix_out": nrm(ks[12], (L, D, D), D ** -0.5 * DEEPNORM_BETA),
        "ln1_g": 1.0 + nrm(ks[13], (L, D), 0.02),
        "ln1_b": nrm(ks[14], (L, D), 0.02),
        "w_up": nrm(ks[15], (L, D, F2), D ** -0.5 * DEEPNORM_BETA),
        "conv_w": nrm(ks[16], (L, CONV_WIDTH, F2), CONV_WIDTH ** -0.5),
        "conv_b": nrm(ks[17], (L, F2), 0.01),
        "w_down": nrm(ks[18], (L, FFN_DIM, D), FFN_DIM ** -0.5 * DEEPNORM_BETA),
        "ln2_g": 1.0 + nrm(ks[19], (L, D), 0.02),
        "ln2_b": nrm(ks[20], (L, D), 0.02),
    }


def reference(x, mem, w_in, diff_lambda, diff_subln_g, ret_norm_g, w_mem_kv, w_diff_o, w_ret_o,
              w_mem_o, w_gate, b_gate, w_mix_out, ln1_g, ln1_b, w_up, conv_w, conv_b, w_down,
              ln2_g, ln2_b):
    h = x
    for l in range(DEPTH):
        lambda_init = 0.8 - 0.6 * math.exp(-0.3 * l)
        mix = token_mixers(h, mem, w_in[l], diff_lambda[l], diff_subln_g[l], ret_norm_g[l],
                           w_mem_kv[l], w_diff_o[l], w_ret_o[l], w_mem_o[l], w_gate[l], b_gate[l],
                           w_mix_out[l], lambda_init)
        h = layer_norm(DEEPNORM_ALPHA * h + mix, ln1_g[l], ln1_b[l])
        ffn = conv_ffn(h, w_up[l], conv_w[l], conv_b[l], w_down[l])
        h = layer_norm(DEEPNORM_ALPHA * h + ffn, ln2_g[l], ln2_b[l])
    return h
```

```python
import math
import numpy as np
import concourse.bass as bass
import concourse.mybir as mybir
from concourse.bass_utils import run_bass_kernel_spmd

F32 = mybir.dt.float32
BF16 = mybir.dt.bfloat16
AF = mybir.ActivationFunctionType
ALU = mybir.AluOpType

ALPHA = 2.0 ** 0.25
LAMBDA_INIT = 0.2
EPS = 1e-5
BIGC = 1.0e7
NCORES = 8
SLOPES = [0.25 ** (i + 1) for i in range(4)]
SEM_ROT = 12000


class Tk:
    __slots__ = ("name", "w", "r")

    def __init__(self, name):
        self.name = name
        self.w = None
        self.r = {}


class Prog:
    ENG = ("pe", "act", "dve", "pool", "sp")

    def __init__(self, nc, sem_pool):
        self.nc = nc
        self.sem_pool = sem_pool
        self.ops = {e: [] for e in self.ENG}
        self.cur = {}
        self.cnt = {}
        self.known = {e: {} for e in self.ENG}
        for e in ("pe", "act", "dve", "pool"):
            self.cur[e] = self.sem_pool.pop()
            self.cnt[e] = 0
        self.dma_sem = {}
        self.dma_cnt = {}
        self.stop = False

    def _deps(self, eng, reads, writes):
        need = {}

        def add(tok):
            if tok is None:
                return
            s, v = tok
            if id(s) not in need or need[id(s)][1] < v:
                need[id(s)] = (s, v)

        for t in reads:
            add(t.w)
        for t in writes:
            add(t.w)
            for tok in t.r.values():
                add(tok)
        for s, v in need.values():
            if eng == "pe" and s is self.cur.get("pe"):
                continue
            k = self.known[eng].get(id(s), 0)
            if k >= v:
                continue
            self.known[eng][id(s)] = v
            self.ops[eng].append(lambda e, s=s, v=v: e.wait_ge(s, v))

    def _mark(self, tok, reads, writes):
        for t in reads:
            t.r[id(tok[0])] = tok
        for t in writes:
            t.w = tok
            t.r = {}

    def op(self, eng, fn, reads=(), writes=()):
        if self.stop:
            return
        self._deps(eng, reads, writes)
        if self.cnt[eng] >= SEM_ROT:
            self.cur[eng] = self.sem_pool.pop()
            self.cnt[eng] = 0
        self.cnt[eng] += 1
        s = self.cur[eng]
        v = self.cnt[eng]
        self.ops[eng].append(lambda e, s=s: fn(e).then_inc(s, 1))
        self._mark((s, v), reads, writes)

    def dma(self, q, out, in_, reads=(), writes=(), key=None):
        if self.stop:
            return
        self._deps(q, reads, writes)
        key = key if key is not None else (writes[0].name if writes else reads[0].name)
        if key not in self.dma_sem:
            self.dma_sem[key] = self.sem_pool.pop()
            self.dma_cnt[key] = 0
        self.dma_cnt[key] += 16
        s = self.dma_sem[key]
        v = self.dma_cnt[key]
        self.ops[q].append(lambda e, s=s: e.dma_start(out=out, in_=in_).then_inc(s, 16))
        self._mark((s, v), reads, writes)

    def barrier(self):
        if self.stop:
            return
        toks = [(self.cur[e], self.cnt[e]) for e in ("pe", "act", "dve", "pool") if self.cnt[e] > 0]
        toks += [(self.dma_sem[k], self.dma_cnt[k]) for k in self.dma_sem]
        for eng in self.ENG:
            for s, v in toks:
                if eng == "pe" and s is self.cur.get("pe"):
                    continue
                if self.known[eng].get(id(s), 0) >= v:
                    continue
                self.known[eng][id(s)] = v
                self.ops[eng].append(lambda e, s=s, v=v: e.wait_ge(s, v))

    def final_wait(self, eng, tiles):
        self._deps(eng, tiles, [])


C_BG, C_L1G, C_L1B, C_L2G, C_L2B, C_CW, C_CB, C_SG, C_RG = 0, 24, 32, 40, 48, 56, 188, 232, 236
NCONST = 240
NEGM = -30000.0


def _gammas():
    return [1.0 - 2.0 ** (-5.0 - h) for h in range(4)]


def shared_tables():
    C = np.zeros((128, 4, 512))
    n = np.arange(512)
    for k in range(4):
        s = 128 * k + np.arange(128)
        cs = (s // 64)[:, None]
        cn = (n // 64)[None, :]
        diff = (s[:, None] - n[None, :]).astype(np.float64)
        C[:, k, :] = np.where(cs > cn, -BIGC, np.where((cs == cn) & (diff > 0), -16.0 * diff, 0.0))
    lg = [math.log(x) for x in _gammas()]
    Tt = np.zeros((128, 4, 896))
    c = np.arange(896)
    for h in range(4):
        j = c[None, :] - 384 - np.arange(128)[:, None]
        Tt[:, h, :] = np.where(j >= 0, np.exp(lg[h] * np.maximum(j, 0)) / 8.0, 0.0)
    Gq = np.zeros((128, 4, 512))
    for h in range(4):
        Gq[:, h, :] = np.exp(lg[h] * (n + 1.0))[None, :]
    gk = np.zeros((128, 32))
    for tb in range(4):
        m = 128 * tb + np.arange(128)
        for h in range(4):
            gk[:, tb * 8 + h] = np.exp(lg[h] * (511.0 - m)) / 8.0
            gk[:, tb * 8 + 4 + h] = np.where(m <= 510, np.exp(lg[h] * np.maximum(510.0 - m, 0.0)) / 8.0, 0.0)
    return C.astype(np.float32), Tt.astype(np.float32), Gq.astype(np.float32), gk.astype(np.float32)


def core_tables(j, NSB):
    NSLOT = NSB // 4
    NB = NSB * 4
    p = np.arange(128, dtype=np.float64)
    kbW = np.zeros((128, 4, NSLOT, NB))
    acol = np.zeros((128, 4, NSLOT, 4))
    for k in range(NSLOT):
        sb = 4 * k + j
        for blk in range(NB):
            i = blk // 4
            for h in range(4):
                if i > sb:
                    kbW[:, h, k, blk] = NEGM
                else:
                    kbW[:, h, k, blk] = SLOPES[h] * (128.0 * blk + p - 512.0 * sb - 256.0)
        for ii in range(4):
            for h in range(4):
                acol[:, h, k, ii] = SLOPES[h] if (4 * k + ii == sb) else 0.0
    Bh = np.zeros((128, NB, 8))
    hz = np.zeros((128, 8))
    for k in range(NSLOT):
        sb = 4 * k + j
        t0 = 512 * sb
        if sb == 0:
            continue
        for r in range(2):
            tq = t0 - 2 + r
            e = 2 * k + r
            hz[:, e] = 1.0
            for blk in range(NB):
                s = 128.0 * blk + p
                Bh[:, blk, e] = np.where(s < t0, -8.0 * np.abs(tq - s), -BIGC)
    sel = np.zeros((128, 16))
    for k in range(NSLOT):
        sel[:, 4 * k + j] = 1.0
    pc = np.zeros((128, 64 + 16 + 8), np.float32)
    for h in range(4):
        for k in range(NSLOT):
            for ii in range(4):
                pc[:, h * 16 + k * 4 + ii] = acol[:, h, k, ii]
    pc[:, 64:80] = sel
    pc[:, 80:88] = hz
    return kbW.reshape(128, -1).astype(np.float32), Bh.astype(np.float32), pc


PC_ACOL, PC_SEL, PC_HZ = 0, 64, 80
NPC = 88


def fm(w, kc):
    return np.ascontiguousarray(w.reshape(kc, 128, -1).transpose(1, 0, 2))


def vec_fm(v, nch):
    return np.ascontiguousarray(v.reshape(nch, 128).T)


def build(NSB=16, debug=False, upto=9):
    NSLOT = NSB // 4
    S = NSB * 512
    NB = NSB * 4
    NQ = NSLOT * 512
    NQT = NQ + 8
    nc = bass.Bass("TRN2", target_bir_lowering=False)

    def din(name, shape, dt=F32):
        return nc.dram_tensor(name, list(shape), dt, kind="ExternalInput").ap()

    xT = din("xT", [128, 8, S])
    xq = din("xq", [128, 8, NQT])
    memT = din("memT", [128, 8, 256])
    w_in = din("w_in", [128, 8, 3584])
    w_mkv = din("w_mkv", [128, 8, 1024])
    w_br = din("w_br", [128, 12, 1024])
    w_gate = din("w_gate", [128, 8, 3072])
    w_out = din("w_out", [128, 8, 1024])
    w_up = din("w_up", [128, 8, 5632])
    w_dn = din("w_dn", [128, 22, 1024])
    consts_d = din("consts", [128, NCONST])
    dl_d = din("dl", [128, 256])
    kbW_d = din("kbW", [128, 4 * NSLOT * NB])
    Bh_d = din("Bh", [128, NB, 8])
    pc_d = din("pc", [128, NPC])
    gk_d = din("gk", [128, 32])
    gkT_d = din("gkT", [128, 2048])
    C_d = din("Ctab", [128, 4, 512])
    I_d = din("ident", [128, 128])
    Tt_d = din("Ttab", [128, 4, 896])
    Gq_d = din("Gq", [128, 4, 512])
    outT = nc.dram_tensor("outT", [128, 8, NQ], F32, kind="ExternalOutput").ap()
    skind = "ExternalOutput" if debug else "Internal"
    br_d = nc.dram_tensor("br_d", [128, 12, NQT], BF16, kind=skind).ap()
    aT_d, rT_d, mT_d = br_d[:, 0:4, :], br_d[:, 4:8, :], br_d[:, 8:12, :]
    h1_d = nc.dram_tensor("h1_d", [128, 8, NQT], F32, kind=skind).ap()
    wu_b = nc.dram_tensor("wu_b", [128, 8, 5632], BF16, kind="Internal").ap()
    wd_b = nc.dram_tensor("wd_b", [128, 22, 1024], BF16, kind="Internal").ap()
    t_wub, t_wdb = Tk("wub"), Tk("wdb")

    from contextlib import ExitStack
    es = ExitStack()
    big = es.enter_context(nc.sbuf_tensor("big", [128, 53200], F32))
    TOTAL = 53200 * 4
    bigb = big.bitcast(BF16)
    pspair = [es.enter_context(nc.psum_tensor(f"psp{i}", [128, 1024], F32)) for i in range(4)]
    psb = [pspair[i // 2][:, (i % 2) * 512:(i % 2 + 1) * 512] for i in range(8)]
    sem_pool = [es.enter_context(nc.semaphore(f"s{i}")) for i in range(72)]
    P = Prog(nc, sem_pool)

    class Alloc:
        def __init__(self, base):
            self.off = base

        def f32(self, *shape):
            n = int(np.prod(shape))
            o = (self.off + 3) // 4
            self.off = (o + n) * 4
            assert self.off <= TOTAL, f"SBUF overflow {self.off}"
            return self._shape(big[:, o:o + n], shape)

        def b16(self, *shape):
            n = int(np.prod(shape))
            o = (self.off + 3) // 4 * 2
            self.off = ((o + n) * 2 + 3) // 4 * 4
            assert self.off <= TOTAL, f"SBUF overflow {self.off}"
            return self._shape(bigb[:, o:o + n], shape)

        @staticmethod
        def _shape(v, shape):
            if len(shape) == 1:
                return v
            if len(shape) == 2:
                return v.rearrange("p (a b) -> p a b", b=shape[1])
            return v.rearrange("p (a b c) -> p a b c", b=shape[1], c=shape[2])

    pa = Alloc(0)
    cst = pa.f32(NCONST)
    pc = pa.f32(NPC)
    gk = pa.f32(32)
    dl = pa.f32(256)
    ones_bf = pa.b16(128)
    onesb128 = pa.b16(128)
    onesb1024 = pa.b16(128)
    epsc = pa.f32(1)
    neglam = pa.f32(1)
    lamtmp = pa.f32(128)
    lam2 = pa.f32(2)
    gs = pa.f32(4)
    mkT = pa.b16(4, 256)
    mv = pa.b16(2, 512)
    o_snap = pa.off
    snapS = pa.b16(max(NSLOT, 4) * 4, 128)
    snapS2 = pa.b16(max(NSLOT, 4) * 4, 128)
    h1Hb = pa.b16(8, 8)
    uH = pa.f32(44, 8)
    PBASE = pa.off

    t_cst, t_misc, t_mk, t_mv = Tk("cst"), Tk("misc"), Tk("mkT"), Tk("mv")
    t_snap, t_h1H, t_uH = Tk("snap"), Tk("h1H"), Tk("uH")
    t_ps = [Tk(f"ps{i}") for i in range(8)]

    def col(off, i=0):
        return cst[:, off + i:off + i + 1]

    def pcol(off, i=0):
        return pc[:, off + i:off + i + 1]

    def mm(ps_i, out_ap, pairs, reads, start=True, stop=True):
        n = len(pairs)

        def fn(e):
            ins = None
            for i, (l, r) in enumerate(pairs):
                ins = e.matmul(out_ap, l, r, start=(start and i == 0), stop=(stop and i == n - 1))
            return ins
        P.op("pe", fn, reads=reads, writes=[t_ps[ps_i]])

    def mml(ps_is, items, reads):
        def fn(e):
            ins = None
            for (o, l, r, st, sp_) in items:
                ins = e.matmul(o, l, r, start=st, stop=sp_)
            return ins
        P.op("pe", fn, reads=reads, writes=[t_ps[i] for i in ps_is])

    def act(out, in_, func, reads, writes, bias=None, scale=None):
        kw = {}
        if bias is not None:
            kw["bias"] = bias
        if scale is not None:
            kw["scale"] = scale
        P.op("act", lambda e: e.activation(out, in_, func, **kw), reads=reads, writes=writes)

    def tt(out, a, b, op, reads, writes, eng="dve"):
        P.op(eng, lambda e: e.tensor_tensor(out, a, b, op), reads=reads, writes=writes)

    def stt(out, a, sc, b, op0, op1, reads, writes):
        P.op("dve", lambda e: e.scalar_tensor_tensor(out, a, sc, b, op0, op1), reads=reads, writes=writes)

    def ts(out, a, s1, s2, op0, op1, reads, writes, eng="dve"):
        if op1 is None:
            P.op(eng, lambda e: e.tensor_scalar(out, a, s1, None, op0), reads=reads, writes=writes)
        else:
            P.op(eng, lambda e: e.tensor_scalar(out, a, s1, s2, op0, op1), reads=reads, writes=writes)

    def cp(eng, out, in_, reads, writes):
        if eng == "act":
            P.op("act", lambda e: e.copy(out, in_), reads=reads, writes=writes)
        else:
            P.op(eng, lambda e: e.tensor_copy(out, in_), reads=reads, writes=writes)

    def recip(out, in_, reads, writes):
        act(out, in_, AF.Ln, reads, writes)
        act(out, out, AF.Exp, writes, writes, scale=-1.0)

    def recip_exact(out, in_, reads, writes):
        P.op("dve", lambda e: e.reciprocal(out, in_), reads=reads, writes=writes)

    def rstd_from(ps_i, ps_ap, tmp, t_tmp, out, t_out):
        act(tmp, ps_ap, AF.Ln, [t_ps[ps_i], t_misc], [t_tmp], bias=epsc[:, 0:1], scale=1.0)
        act(out, tmp, AF.Exp, [t_tmp], [t_out], scale=-0.5)

    rot = [0]

    def nextbank():
        rot[0] = (rot[0] + 1) % 4
        return 4 + rot[0]

    P.dma("sp", cst, consts_d, writes=[t_cst])
    P.dma("sp", pc, pc_d, writes=[t_cst], key="cst")
    P.dma("sp", gk, gk_d, writes=[t_cst], key="cst")
    P.dma("sp", dl, dl_d, writes=[t_misc], key="miscld")
    P.op("pool", lambda e: e.memset(ones_bf, 1.0), writes=[t_misc])
    P.op("pool", lambda e: e.memset(onesb128, 1.0 / 128.0), writes=[t_misc])
    P.op("pool", lambda e: e.memset(onesb1024, 1.0 / 1024.0), writes=[t_misc])
    P.op("pool", lambda e: e.memset(epsc, EPS), writes=[t_misc])
    P.op("pool", lambda e: e.memset(snapS, 0.0), writes=[t_snap])
    P.op("pool", lambda e: e.memset(snapS2, 0.0), writes=[t_snap])
    dl4 = dl.rearrange("p (a b c) -> p a b c", b=2, c=64)
    lt = lamtmp.rearrange("p (a c) -> p a c", c=64)
    t_lam = Tk("lam")
    tt(lt, dl4[:, :, 0, :], dl4[:, :, 1, :], ALU.mult, [t_misc], [t_lam])
    P.op("dve", lambda e: e.tensor_reduce(lam2, lt, mybir.AxisListType.X, ALU.add), reads=[t_lam], writes=[t_lam])
    act(lam2, lam2, AF.Exp, [t_lam], [t_lam])
    tt(neglam, lam2[:, 1:2], lam2[:, 0:1], ALU.subtract, [t_lam], [t_lam])
    ts(neglam, neglam, -LAMBDA_INIT, None, ALU.add, None, [t_lam], [t_lam])
    ts(gs, cst[:, C_SG:C_SG + 4], 1.0 - LAMBDA_INIT, None, ALU.mult, None, [t_cst], [t_lam])

    a1 = Alloc(PBASE)
    KT = a1.b16(4, S)
    Vt = a1.b16(NB, 512)
    xb = [a1.b16(8, 512) for _ in range(2)]
    kbW = a1.f32(4 * NSLOT * NB)
    RB = a1.off
    t_KT = [[Tk(f"KT{h}_{s}") for s in range(NSB)] for h in range(4)]
    t_Vt = [Tk(f"Vt{b}") for b in range(NB)]
    t_xb = [Tk("xb0"), Tk("xb1")]
    t_tab = Tk("tab")
    P.dma("sp", kbW, kbW_d, writes=[t_tab], key="tab")

    r1 = Alloc(RB)
    W1 = r1.b16(8, 1792)
    rkd = [r1.b16(256) for _ in range(2)]
    rkd2 = [r1.b16(256) for _ in range(2)]
    rvt = [r1.b16(512) for _ in range(2)]
    Sf = r1.f32(4, 128)
    S2f = r1.f32(4, 128)
    GkT = r1.b16(8, 256)
    t_GkT = Tk("GkT")
    t_W1 = Tk("W1")
    t_rkd = [Tk("rkd0"), Tk("rkd1")]
    t_rvt = [Tk("rvt0"), Tk("rvt1")]
    t_Sf, t_S2f = Tk("Sf"), Tk("S2f")
    P.dma("pool", W1[:, :, 0:1024], w_in[:, :, 512:1536], writes=[t_W1], key="W1")
    P.dma("pool", W1[:, :, 1024:1792], w_in[:, :, 1792:2560], writes=[t_W1], key="W1")
    P.dma("pool", GkT, gkT_d, writes=[t_GkT])
    P.op("pool", lambda e: e.memset(Sf, 0.0), writes=[t_Sf])
    P.op("pool", lambda e: e.memset(S2f, 0.0), writes=[t_S2f])
    P.dma("pool", xb[0], xT[:, :, 0:512], writes=[t_xb[0]])
    a0 = Alloc(PBASE)
    memb = a0.b16(8, 256)
    Wm = a0.b16(8, 1024)
    t_Wm, t_memb = Tk("Wm"), Tk("memb")
    P.dma("pool", memb, memT, writes=[t_memb])
    P.dma("pool", Wm, w_mkv, writes=[t_Wm])
    for h in range(4):
        b = 4 + h
        mm(b, psb[b][:, 0:256], [(Wm[:, kc, h * 128:(h + 1) * 128], memb[:, kc, :]) for kc in range(8)],
           [t_Wm, t_memb])
        cp("act", mkT[:, h, :], psb[b][:, 0:256], [t_ps[b]], [t_mk])
    for blk in range(2):
        b = 4 + blk
        mm(b, psb[b][:, :], [(memb[:, kc, blk * 128:(blk + 1) * 128], Wm[:, kc, 512:1024]) for kc in range(8)],
           [t_Wm, t_memb])
        cp("dve", mv[:, blk, :], psb[b][:, :], [t_ps[b]], [t_mv])
    P.barrier()
    if upto < 1:
        P.stop = True

    G512 = [math.exp(math.log(g_) * 512.0) for g_ in _gammas()]
    G511 = [math.exp(math.log(g_) * 511.0) for g_ in _gammas()]
    for s in range(NSB):
        xi = s % 2
        X, tX = xb[xi], t_xb[xi]
        if s > 0:
            P.dma("pool", X, xT[:, :, s * 512:(s + 1) * 512], writes=[tX])
        for h in range(4):
            b = nextbank()
            mm(b, psb[b][:, :], [(W1[:, kc, h * 128:(h + 1) * 128], X[:, kc, :]) for kc in range(8)], [t_W1, tX])
            cp("act", KT[:, h, s * 512:(s + 1) * 512], psb[b][:, :], [t_ps[b]], [t_KT[h][s]])
        for tb in range(4):
            b = nextbank()
            mm(b, psb[b][:, :], [(X[:, kc, tb * 128:(tb + 1) * 128], W1[:, kc, 512:1024]) for kc in range(8)],
               [t_W1, tX])
            cp("dve", Vt[:, s * 4 + tb, :], psb[b][:, :], [t_ps[b]], [t_Vt[s * 4 + tb]])
        k = s // 4
        if k < NSLOT:
            stt(snapS[0:64, k * 4:(k + 1) * 4, :], Sf[0:64, :, :], pcol(PC_SEL, s)[0:64, :],
                snapS[0:64, k * 4:(k + 1) * 4, :], ALU.mult, ALU.add, [t_Sf, t_cst, t_snap], [t_snap])
            stt(snapS2[0:64, k * 4:(k + 1) * 4, :], S2f[0:64, :, :], pcol(PC_SEL, s)[0:64, :],
                snapS2[0:64, k * 4:(k + 1) * 4, :], ALU.mult, ALU.add, [t_S2f, t_cst, t_snap], [t_snap])
        if s < NSB - 1:
            for tb in range(4):
                ti = tb % 2
                b = nextbank()
                mm(b, psb[b][:, 0:256], [(X[:, kc, tb * 128:(tb + 1) * 128], W1[:, kc, 1024:1280]) for kc in range(8)],
                   [t_W1, tX])
                tt(rkd[ti], psb[b][:, 0:256], GkT[:, tb * 2, :], ALU.mult, [t_ps[b], t_GkT], [t_rkd[ti]])
                tt(rkd2[ti], psb[b][:, 0:256], GkT[:, tb * 2 + 1, :], ALU.mult, [t_ps[b], t_GkT], [t_rkd[ti]])
                b = nextbank()
                mm(b, psb[b][:, :], [(X[:, kc, tb * 128:(tb + 1) * 128], W1[:, kc, 1280:1792]) for kc in range(8)],
                   [t_W1, tX])
                cp("dve", rvt[ti], psb[b][:, :], [t_ps[b]], [t_rvt[ti]])
                items = []
                for h in range(4):
                    items.append((psb[0][0:64, h * 128:(h + 1) * 128], rkd[ti][:, h * 64:(h + 1) * 64],
                                  rvt[ti][:, h * 128:(h + 1) * 128], (tb == 0 and h == 0), (tb == 3 and h == 3)))
                for h in range(4):
                    items.append((psb[1][0:64, h * 128:(h + 1) * 128], rkd2[ti][:, h * 64:(h + 1) * 64],
                                  rvt[ti][:, h * 128:(h + 1) * 128], (tb == 0 and h == 0), (tb == 3 and h == 3)))
                mml([0, 1], items, [t_rkd[ti], t_rvt[ti]])
            for h in range(4):
                stt(S2f[0:64, h, :], Sf[0:64, h, :], float(G511[h]), psb[1][0:64, h * 128:(h + 1) * 128], ALU.mult,
                    ALU.add, [t_Sf, t_ps[1]], [t_S2f])
            for h in range(4):
                stt(Sf[0:64, h, :], Sf[0:64, h, :], float(G512[h]), psb[0][0:64, h * 128:(h + 1) * 128], ALU.mult,
                    ALU.add, [t_Sf, t_ps[0]], [t_Sf])
    P.barrier()

    if upto < 2:
        P.stop = True
    r2 = Alloc(RB)
    Wq = r2.b16(8, 512)
    QT = r2.b16(4, 512)
    PTp = [r2.b16(1024) for _ in range(2)]
    PT = [[PTp[jj][:, m * 512:(m + 1) * 512] for jj in range(2)] for m in range(2)]
    sqb = r2.b16(512)
    OLc = [r2.f32(512) for _ in range(4)]
    Bf = OLc[3]
    R1 = r2.f32(512)
    Af = r2.f32(512)
    aw = r2.b16(4, 512)
    Cb = r2.b16(4, 512)
    BhT = r2.f32(NB, 8)
    identb = r2.b16(128)
    selI = [r2.b16(128) for _ in range(2)]
    t_ctab, t_bh, t_selI = Tk("ctab"), Tk("bh"), [Tk("selI0"), Tk("selI1")]
    selcnt = [0]
    t_Wq, t_QT = Tk("Wq"), Tk("QT")
    t_PT = [[Tk(f"PT{m}{j}") for j in range(2)] for m in range(2)]
    t_sqb = Tk("sqb")
    t_OLc = [Tk(f"OLc{i}") for i in range(4)]
    t_Bf = t_OLc[3]
    t_R1, t_Af, t_aw = Tk("R1"), Tk("Af"), Tk("aw")
    t_aTd, t_rTd, t_mTd, t_h1d, t_out = Tk("aTd"), Tk("rTd"), Tk("mTd"), Tk("h1d"), Tk("outd")
    P.dma("pool", Wq, w_in[:, :, 0:512], writes=[t_Wq])
    P.dma("pool", Cb, C_d, writes=[t_ctab], key="ctab")
    P.dma("pool", identb, I_d, writes=[t_ctab], key="ctab")
    P.dma("sp", BhT, Bh_d, writes=[t_bh])

    def post_evac(N):
        for i in range(4):
            cp("act" if i % 2 == 0 else "dve", OLc[i][:, 0:N], psb[i][:, 0:N], [t_ps[i]], [t_OLc[i]])

    def post_a(N):
        rc = recip_exact if N == 512 else recip
        rc(R1[:, 0:N], OLc[2][:, 0:N], [t_OLc[2]], [t_R1])
        tt(Af[:, 0:N], OLc[0][:, 0:N], R1[:, 0:N], ALU.mult, [t_OLc[0], t_R1], [t_Af])
        rc(R1[:, 0:N], OLc[3][:, 0:N], [t_OLc[3]], [t_R1])
        tt(Bf[:, 0:N], OLc[1][:, 0:N], R1[:, 0:N], ALU.mult, [t_OLc[1], t_R1], [t_Bf])
        stt(Af[:, 0:N], Bf[:, 0:N], neglam[:, 0:1], Af[:, 0:N], ALU.mult, ALU.add, [t_Bf, t_Af, t_lam], [t_Af])
        tt(sqb[:, 0:N], Af[:, 0:N], Af[:, 0:N], ALU.mult, [t_Af], [t_sqb])

    def post_b(N, h):
        mm(4, psb[4][:, 0:N], [(onesb128, sqb[:, 0:N])], [t_misc, t_sqb])
        rstd_from(4, psb[4][:, 0:N], R1[:, 0:N], t_R1, R1[:, 0:N], t_R1)
        tt(Af[:, 0:N], Af[:, 0:N], R1[:, 0:N], ALU.mult, [t_Af, t_R1], [t_Af])
        ts(aw[:, h, 0:N], Af[:, 0:N], gs[:, h:h + 1], None, ALU.mult, None, [t_Af, t_lam], [t_aw])

    def diff_post(N, h):
        post_evac(N)
        post_a(N)
        post_b(N, h)

    pend_a, pend_b = [], []

    for k in range(NSLOT):
        xi = k % 2
        X, tX = xb[xi], t_xb[xi]
        P.dma("pool", X, xq[:, :, k * 512:(k + 1) * 512], writes=[tX])
        if k == 1:
            for i in range(4):
                P.dma("pool", wu_b[:, :, i * 1408:(i + 1) * 1408], w_up[:, :, i * 1408:(i + 1) * 1408],
                      writes=[t_wub], key="wub")
            P.dma("pool", wd_b, w_dn, writes=[t_wdb], key="wdb")
        for h in range(4):
            b = nextbank()
            mm(b, psb[b][:, :], [(Wq[:, kc, h * 128:(h + 1) * 128], X[:, kc, :]) for kc in range(8)], [t_Wq, tX])
            cp("act", QT[:, h, :], psb[b][:, :], [t_ps[b]], [t_QT])
        for h in range(4):
            blocks = [(i, kb) for i in range(4 * k + 4) for kb in range(4)]
            n = len(blocks)

            def qk(j):
                i, kb = blocks[j]
                blk = i * 4 + kb
                pA, pB = 4 + 2 * (j % 2), 5 + 2 * (j % 2)
                ks = slice(blk * 128, (blk + 1) * 128)
                if i >= 4 * k:
                    if kb == 0:
                        selcnt[0] += 1
                        q_ = selcnt[0] % 2
                        ts(selI[q_], identb, pcol(PC_ACOL, h * 16 + k * 4 + (i - 4 * k)), None, ALU.mult, None,
                           [t_ctab, t_cst], [t_selI[q_]])
                    q_ = selcnt[0] % 2
                    mml([pA, pB], [(psb[pA][:, :], KT[0:64, h, ks], QT[0:64, h, :], True, False),
                                   (psb[pA][:, :], selI[q_], Cb[:, kb, :], False, True),
                                   (psb[pB][:, :], KT[64:128, h, ks], QT[64:128, h, :], True, False),
                                   (psb[pB][:, :], selI[q_], Cb[:, kb, :], False, True)],
                        [t_KT[h][i], t_QT, t_selI[q_], t_ctab])
                else:
                    mml([pA, pB], [(psb[pA][:, :], KT[0:64, h, ks], QT[0:64, h, :], True, True),
                                   (psb[pB][:, :], KT[64:128, h, ks], QT[64:128, h, :], True, True)],
                        [t_KT[h][i], t_QT])

            def ex(j):
                i, kb = blocks[j]
                blk = i * 4 + kb
                cidx = (h * NSLOT + k) * NB + blk
                kcol = kbW[:, cidx:cidx + 1]
                pA = 4 + 2 * (j % 2)
                act(PTp[j % 2], pspair[pA // 2][:, :], AF.Exp, [t_ps[pA], t_ps[pA + 1], t_tab],
                    [t_PT[0][j % 2], t_PT[1][j % 2]], bias=kcol, scale=0.125)

            def pv(j):
                i, kb = blocks[j]
                blk = i * 4 + kb
                vs = Vt[:, blk, h * 128:(h + 1) * 128]
                st, sp_ = (j == 0), (j == n - 1)
                items = []
                for m in range(2):
                    items.append((psb[m][:, :], vs, PT[m][j % 2], st, sp_))
                    items.append((psb[2 + m][:, :], ones_bf, PT[m][j % 2], st, sp_))
                mml([0, 1, 2, 3], items, [t_Vt[blk], t_PT[0][j % 2], t_PT[1][j % 2], t_misc])

            qk(0)
            for j in range(n):
                if j + 1 < n:
                    qk(j + 1)
                if j == 2 and pend_a:
                    pend_a.pop(0)()
                ex(j)
                if j == 12 and pend_b:
                    pend_b.pop(0)()
                pv(j)
            post_evac(512)
            pend_a.append(lambda: post_a(512))

            def fin(h=h, k=k):
                post_b(512, h)
                if h == 3:
                    P.dma("sp", aT_d[:, :, k * 512:(k + 1) * 512], aw, reads=[t_aw], writes=[t_aTd])
            pend_b.append(fin)
    while pend_a:
        pend_a.pop(0)()
    while pend_b:
        pend_b.pop(0)()

    X, tX = xb[0], t_xb[0]
    P.dma("pool", X[:, :, 0:8], xq[:, :, NQ:NQ + 8], writes=[tX])
    for h in range(4):
        b = nextbank()
        mm(b, psb[b][:, 0:8], [(Wq[:, kc, h * 128:(h + 1) * 128], X[:, kc, 0:8]) for kc in range(8)], [t_Wq, tX])
        cp("act", QT[:, h, 0:8], psb[b][:, 0:8], [t_ps[b]], [t_QT])
    NG = NB // 8
    for h in range(4):
        for g in range(NG):
            pA, pB = 4 + 2 * (g % 2), 5 + 2 * (g % 2)
            items = []
            rd = [t_QT]
            for bl in range(8):
                blk = g * 8 + bl
                ks = slice(blk * 128, (blk + 1) * 128)
                items.append((psb[pA][:, bl * 8:(bl + 1) * 8], KT[0:64, h, ks], QT[0:64, h, 0:8], True, True))
                items.append((psb[pB][:, bl * 8:(bl + 1) * 8], KT[64:128, h, ks], QT[64:128, h, 0:8], True, True))
                if t_KT[h][blk // 4] not in rd:
                    rd.append(t_KT[h][blk // 4])
            mml([pA, pB], items, rd)
            bh = BhT[:, g * 8:(g + 1) * 8, :]
            for m in range(2):
                pi = pA + m
                p3 = psb[pi][:, 0:64].rearrange("p (a b) -> p a b", b=8)
                stt(p3, bh, float(SLOPES[h]), p3, ALU.mult, ALU.add, [t_bh, t_ps[pi]], [t_ps[pi]])
                act(PT[m][g % 2][:, 0:64], psb[pi][:, 0:64], AF.Exp, [t_ps[pi]], [t_PT[m][g % 2]], scale=0.125)
            items = []
            rd = [t_PT[0][g % 2], t_PT[1][g % 2], t_misc]
            for bl in range(8):
                blk = g * 8 + bl
                st, sp_ = (g == 0 and bl == 0), (g == NG - 1 and bl == 7)
                for m in range(2):
                    rhs = PT[m][g % 2][:, bl * 8:(bl + 1) * 8]
                    items.append((psb[m][:, 0:8], Vt[:, blk, h * 128:(h + 1) * 128], rhs, st, sp_))
                    items.append((psb[2 + m][:, 0:8], ones_bf, rhs, st, sp_))
                rd.append(t_Vt[blk])
            mml([0, 1, 2, 3], items, rd)
        diff_post(8, h)
    P.dma("sp", aT_d[:, :, NQ:NQ + 8], aw[:, :, 0:8], reads=[t_aw], writes=[t_aTd])

    if upto < 3:
        P.stop = True
    P.barrier()
    a2 = Alloc(PBASE)
    Wp = a2.b16(8, 2048)
    Tt = a2.f32(4, 896)
    Gq = a2.f32(4, 512)
    xb2 = [a2.b16(8, 512)] * 2
    rqT = a2.b16(2, 512)
    rqd4 = a2.b16(4, 512)
    rkT = a2.b16(2, 512)
    rv = a2.b16(4, 512)
    sg = a2.b16(4, 512)
    mqT = a2.b16(4, 512)
    rw = a2.b16(4, 512)
    mw = a2.b16(4, 512)
    Sd = [a2.b16(512) for _ in range(2)]
    PT2 = [a2.b16(512) for _ in range(2)]
    rf = a2.f32(512)
    cen = a2.f32(512)
    rs = a2.f32(512)
    rfb = a2.b16(512)
    sqb2 = a2.b16(512)
    t_Wp4 = [Tk(f"Wp{i}") for i in range(4)]
    t_tab2 = Tk("tab2")
    t_xb2 = [Tk("xb20")] * 2
    t_rqT, t_rqd4, t_rkT, t_rv, t_sg, t_mqT = (Tk(x) for x in ("rqT", "rqd4", "rkT", "rv", "sg", "mqT"))
    t_rw, t_mw = Tk("rw"), Tk("mw")
    t_Sd = [Tk("Sd0"), Tk("Sd1")]
    t_PT2 = [Tk("PT20"), Tk("PT21")]
    t_rf, t_cen, t_rs = Tk("rf"), Tk("cen"), Tk("rs")
    t_rfb, t_sqb2 = Tk("rfb"), Tk("sqb2")
    for i in range(4):
        P.dma("pool", Wp[:, :, i * 512:(i + 1) * 512], w_in[:, :, 1536 + i * 512:1536 + (i + 1) * 512], writes=[t_Wp4[i]])
    P.dma("sp", Tt, Tt_d, writes=[t_tab2], key="tab2")
    P.dma("sp", Gq, Gq_d, writes=[t_tab2], key="tab2")
    TOP3 = TOTAL - (8 * 3072 + 12 * 1024 + 8 * 1024) * 2
    assert a2.off <= TOP3, f"P3a region {a2.off} overlaps P3b weight prefetch area {TOP3}"
    aw3 = Alloc(TOP3)
    Wg = aw3.b16(8, 3072)
    Wbr = aw3.b16(12, 1024)
    Wo = aw3.b16(8, 1024)
    t_Wg2 = [Tk("Wg0"), Tk("Wg1")]
    t_Wbr, t_Wo = Tk("Wbr"), Tk("Wo")
    Wg4 = Wg.rearrange("p k (j c) -> p k j c", j=3)
    wg4 = w_gate.rearrange("p k (j c) -> p k j c", j=3)
    prefetch3 = [lambda i=i: P.dma("pool", Wg4[:, :, :, i * 512:(i + 1) * 512], wg4[:, :, :, i * 512:(i + 1) * 512],
                                   writes=[t_Wg2[i]]) for i in range(2)]
    prefetch3.append(lambda: P.dma("pool", Wbr, w_br, writes=[t_Wbr]))
    prefetch3.append(lambda: P.dma("pool", Wo, w_out, writes=[t_Wo]))

    def group_norm_gate(N, src, h_or_none):
        cp("act", rfb[:, 0:N], src, [t_rf], [t_rfb])
        mm(1, psb[1][:, 0:N], [(onesb128, rfb[:, 0:N])], [t_misc, t_rfb])
        tt(cen[:, 0:N], src, psb[1][:, 0:N], ALU.subtract, [t_rf, t_ps[1]], [t_cen])
        tt(sqb2[:, 0:N], cen[:, 0:N], cen[:, 0:N], ALU.mult, [t_cen], [t_sqb2])
        mm(1, psb[1][:, 0:N], [(onesb128, sqb2[:, 0:N])], [t_misc, t_sqb2])
        rstd_from(1, psb[1][:, 0:N], rs[:, 0:N], t_rs, rs[:, 0:N], t_rs)
        tt(cen[:, 0:N], cen[:, 0:N], rs[:, 0:N], ALU.mult, [t_cen, t_rs], [t_cen])

    def mem_core(N, h):
        for blk in range(2):
            b = 4 + blk
            mm(b, psb[b][:, 0:N], [(mkT[:, h, blk * 128:(blk + 1) * 128], mqT[:, h, 0:N])], [t_mk, t_mqT])
            act(PT2[blk][:, 0:N], psb[b][:, 0:N], AF.Exp, [t_ps[b]], [t_PT2[blk]], scale=1.0 / math.sqrt(128.0))
            mml([6, 7], [(psb[6][:, 0:N], mv[:, blk, h * 128:(h + 1) * 128], PT2[blk][:, 0:N], blk == 0, blk == 1),
                         (psb[7][:, 0:N], ones_bf, PT2[blk][:, 0:N], blk == 0, blk == 1)],
                [t_mv, t_PT2[blk], t_misc])

    def mem_finish(N, h):
        recip(rs[:, 0:N], psb[7][:, 0:N], [t_ps[7]], [t_rs])
        tt(mw[:, h, 0:N], psb[6][:, 0:N], rs[:, 0:N], ALU.mult, [t_ps[6], t_rs], [t_mw])

    def mem_attn(N, h):
        mem_core(N, h)
        mem_finish(N, h)

    for k in range(NSLOT + 1):
        halo = (k == NSLOT)
        N = 8 if halo else 512
        c0 = NQ if halo else k * 512
        xi = k % 2
        X, tX = xb2[xi], t_xb2[xi]
        P.dma("pool", X[:, :, 0:N], xq[:, :, c0:c0 + N], writes=[tX])
        for _ in range(2):
            if prefetch3:
                prefetch3.pop(0)()
        if not halo:
            for c in range(2):
                b = nextbank()
                mm(b, psb[b][:, :], [(Wp[:, kc, c * 128:(c + 1) * 128], X[:, kc, :]) for kc in range(8)], [t_Wp4[0], tX])
                cp("act", rqT[:, c, :], psb[b][:, :], [t_ps[b]], [t_rqT])
            for c in range(2):
                b = nextbank()
                mm(b, psb[b][:, :], [(Wp[:, kc, 256 + c * 128:256 + (c + 1) * 128], X[:, kc, :]) for kc in range(8)],
                   [t_Wp4[0], tX])
                cp("act", rkT[:, c, :], psb[b][:, :], [t_ps[b]], [t_rkT])
            for tb in range(4):
                b = nextbank()
                mm(b, psb[b][:, :], [(X[:, kc, tb * 128:(tb + 1) * 128], Wp[:, kc, 512:1024]) for kc in range(8)],
                   [t_Wp4[1], tX])
                cp("act", rv[:, tb, :], psb[b][:, :], [t_ps[b]], [t_rv])
        for h in range(4):
            b = nextbank()
            mm(b, psb[b][0:64, 0:N], [(Wp[:, kc, h * 64:(h + 1) * 64], X[:, kc, 0:N]) for kc in range(8)], [t_Wp4[0], tX])
            if halo:
                cp("dve", rqd4[0:64, h, 0:N], psb[b][0:64, 0:N], [t_ps[b]], [t_rqd4])
            else:
                tt(rqd4[0:64, h, :], psb[b][0:64, :], Gq[0:64, h, :], ALU.mult, [t_ps[b], t_tab2], [t_rqd4])
        for h in range(4):
            b = nextbank()
            mm(b, psb[b][:, 0:N], [(Wp[:, kc, 1024 + h * 128:1024 + (h + 1) * 128], X[:, kc, 0:N]) for kc in range(8)],
               [t_Wp4[2], tX])
            act(sg[:, h, 0:N], psb[b][:, 0:N], AF.Silu, [t_ps[b]], [t_sg])
        for h in range(4):
            b = nextbank()
            mm(b, psb[b][:, 0:N], [(Wp[:, kc, 1536 + h * 128:1536 + (h + 1) * 128], X[:, kc, 0:N]) for kc in range(8)],
               [t_Wp4[3], tX])
            cp("act", mqT[:, h, 0:N], psb[b][:, 0:N], [t_ps[b]], [t_mqT])
        if not halo:
            def ret_core(h):
                c, r0 = h // 2, (h % 2) * 64
                rr = slice(r0, r0 + 64)
                mm(3, psb[3][:, :], [(snapS[0:64, k * 4 + h, :], rqd4[0:64, h, :])], [t_snap, t_rqd4])
                for kk in range(4):
                    b = 4 + (kk % 2)
                    mm(b, psb[b][:, :], [(rkT[rr, c, kk * 128:(kk + 1) * 128], rqT[rr, c, :])], [t_rkT, t_rqT])
                    tt(Sd[kk % 2], psb[b][:, :], Tt[:, h, 384 - 128 * kk:896 - 128 * kk], ALU.mult,
                       [t_ps[b], t_tab2], [t_Sd[kk % 2]])
                    mm(0, psb[0][:, :], [(rv[:, kk, h * 128:(h + 1) * 128], Sd[kk % 2])], [t_rv, t_Sd[kk % 2]],
                       start=(kk == 0), stop=(kk == 3))
                cp("act", rf, psb[0][:, :], [t_ps[0]], [t_rf])
                tt(rf, rf, psb[3][:, :], ALU.add, [t_rf, t_ps[3]], [t_rf])

            def ret_finish(h):
                group_norm_gate(512, rf, h)
                stt(rw[:, h, :], cen, col(C_RG, h), sg[:, h, :], ALU.mult, ALU.mult, [t_cen, t_cst, t_sg], [t_rw])
            for h in range(4):
                ret_core(h)
                if h > 0:
                    mem_finish(512, h - 1)
                mem_core(512, h)
                ret_finish(h)
            mem_finish(512, 3)
        else:
            items = []
            for h in range(4):
                for kq4 in range(4):
                    kq = min(kq4, NSLOT - 1)
                    for r in range(2):
                        e = 2 * kq4 + r
                        snap = snapS if r == 1 else snapS2
                        items.append((psb[3][:, h * 8 + e:h * 8 + e + 1], snap[0:64, kq * 4 + h, :],
                                      rqd4[0:64, h, e:e + 1], True, True))
            mml([3], items, [t_snap, t_rqd4])
            cp("act", rf[:, 0:32], psb[3][:, 0:32], [t_ps[3]], [t_rf])
            group_norm_gate(32, rf[:, 0:32], None)
            for h in range(4):
                stt(rw[:, h, 0:8], cen[:, h * 8:(h + 1) * 8], col(C_RG, h), sg[:, h, 0:8], ALU.mult, ALU.mult,
                    [t_cen, t_cst, t_sg], [t_rw])
        if halo:
            for h in range(4):
                mem_attn(N, h)
        P.dma("sp", rT_d[:, :, c0:c0 + N], rw[:, :, 0:N], reads=[t_rw], writes=[t_rTd])
        P.dma("sp", mT_d[:, :, c0:c0 + N], mw[:, :, 0:N], reads=[t_mw], writes=[t_mTd])

    while prefetch3:
        prefetch3.pop(0)()
    if upto < 4:
        P.stop = True
    P.barrier()
    a3 = Alloc(PBASE)
    xb3 = [a3.b16(8, 512) for _ in range(2)]
    xf = a3.f32(8, 512)
    br = a3.b16(12, 512)
    sig = [a3.f32(512) for _ in range(2)]
    acc = a3.f32(512)
    tmpm = a3.f32(512)
    mg = a3.b16(8, 512)
    y = a3.f32(8, 512)
    lnm = a3.f32(512)
    lnv = a3.f32(512)
    lnr = a3.f32(512)
    lnq = [a3.b16(512) for _ in range(2)]
    ybf = [a3.b16(512) for _ in range(2)]
    lnt = [a3.f32(512) for _ in range(2)]
    t_xb3 = [Tk("xb30"), Tk("xb31")]
    t_xf, t_br = Tk("xf"), Tk("br")
    t_sig = [Tk("sig0"), Tk("sig1")]
    t_acc, t_tmpm, t_mg, t_y = Tk("acc"), Tk("tmpm"), Tk("mg"), Tk("y")
    t_lnm, t_lnv, t_lnr = Tk("lnm"), Tk("lnv"), Tk("lnr")
    t_lnq = [Tk("lnq0"), Tk("lnq1")]
    t_ybf = [Tk("ybf0"), Tk("ybf1")]
    t_lnt = [Tk("lnt0"), Tk("lnt1")]
    assert a3.off <= TOP3, f"P3b buffers {a3.off} overlap weights {TOP3}"

    def layer_norm_steps(N, src, t_src, gofs, bofs, dst_fn, t_dst_list):
        steps = []

        def stat(c):
            q = c % 2
            cp("dve" if c % 2 == 0 else "act", ybf[q][:, 0:N], src[:, c, 0:N], [t_src], [t_ybf[q]])
            mm(0, psb[0][:, 0:N], [(onesb1024, ybf[q][:, 0:N])], [t_misc, t_ybf[q]], start=(c == 0), stop=(c == 7))
            act(lnq[q][:, 0:N], src[:, c, 0:N], AF.Square, [t_src], [t_lnq[q]])
            mm(1, psb[1][:, 0:N], [(onesb1024, lnq[q][:, 0:N])], [t_misc, t_lnq[q]], start=(c == 0), stop=(c == 7))

        def mid():
            cp("act", lnm[:, 0:N], psb[0][:, 0:N], [t_ps[0]], [t_lnm])
            tt(lnv[:, 0:N], lnm[:, 0:N], lnm[:, 0:N], ALU.mult, [t_lnm], [t_lnv])
            tt(lnv[:, 0:N], psb[1][:, 0:N], lnv[:, 0:N], ALU.subtract, [t_ps[1], t_lnv], [t_lnv])
            act(lnr[:, 0:N], lnv[:, 0:N], AF.Ln, [t_lnv, t_misc], [t_lnr], bias=epsc[:, 0:1], scale=1.0)
            act(lnr[:, 0:N], lnr[:, 0:N], AF.Exp, [t_lnr], [t_lnr], scale=-0.5)

        def norm(c):
            q = c % 2
            tt(lnt[q][:, 0:N], src[:, c, 0:N], lnm[:, 0:N], ALU.subtract, [t_src, t_lnm], [t_lnt[q]])
            tt(lnt[q][:, 0:N], lnt[q][:, 0:N], lnr[:, 0:N], ALU.mult, [t_lnt[q], t_lnr], [t_lnt[q]])
            act(dst_fn(c), lnt[q][:, 0:N], AF.Identity, [t_lnt[q], t_cst], t_dst_list, bias=col(bofs, c),
                scale=col(gofs, c))
        for c in range(8):
            steps.append(lambda c=c: stat(c))
        steps.append(mid)
        for c in range(8):
            steps.append(lambda c=c: norm(c))
        return steps

    def layer_norm(N, src, t_src, gofs, bofs, dst_fn, t_dst_list):
        for st in layer_norm_steps(N, src, t_src, gofs, bofs, dst_fn, t_dst_list):
            st()

    pending3 = []
    for k in range(NSLOT + 1):
        halo = (k == NSLOT)
        N = 8 if halo else 512
        c0 = NQ if halo else k * 512
        ws = slice(c0, c0 + N)
        xi = k % 2
        X, tX = xb3[xi], t_xb3[xi]

        def loads_a(kk):
            hal = (kk == NSLOT)
            n_ = 8 if hal else 512
            c_ = NQ if hal else kk * 512
            P.dma("pool", xb3[kk % 2][:, :, 0:n_], xq[:, :, c_:c_ + n_], writes=[t_xb3[kk % 2]])
            P.dma("sp", br[:, :, 0:n_], br_d[:, :, c_:c_ + n_], reads=[t_aTd, t_rTd, t_mTd], writes=[t_br], key="br")

        def loads_b(kk):
            hal = (kk == NSLOT)
            n_ = 8 if hal else 512
            c_ = NQ if hal else kk * 512
            P.dma("sp", xf[:, :, 0:n_], xq[:, :, c_:c_ + n_], writes=[t_xf])
        if k == 0:
            loads_a(0)
            loads_b(0)
        for c in range(8):
            for j in range(3):
                if (c, j) >= (1, 0) and pending3:
                    pending3.pop(0)()
                q = (c * 3 + j) % 2
                bG = 4 + q
                bB = 6 + q
                mm(bG, psb[bG][:, 0:N],
                   [(Wg[:, kc, j * 1024 + c * 128:j * 1024 + (c + 1) * 128], X[:, kc, 0:N]) for kc in range(8)],
                   [t_Wg2[c // 4], tX])
                act(sig[q][:, 0:N], psb[bG][:, 0:N], AF.Sigmoid, [t_ps[bG], t_cst], [t_sig[q]],
                    bias=col(C_BG, j * 8 + c), scale=1.0)
                mm(bB, psb[bB][:, 0:N],
                   [(Wbr[:, j * 4 + kc, c * 128:(c + 1) * 128], br[:, j * 4 + kc, 0:N]) for kc in range(4)],
                   [t_Wbr, t_br])
                if j == 0:
                    tt(acc[:, 0:N], sig[q][:, 0:N], psb[bB][:, 0:N], ALU.mult, [t_sig[q], t_ps[bB]], [t_acc])
                else:
                    tt(tmpm[:, 0:N], sig[q][:, 0:N], psb[bB][:, 0:N], ALU.mult, [t_sig[q], t_ps[bB]], [t_tmpm])
                    if j == 1:
                        tt(acc[:, 0:N], acc[:, 0:N], tmpm[:, 0:N], ALU.add, [t_acc, t_tmpm], [t_acc])
                    else:
                        tt(mg[:, c, 0:N], acc[:, 0:N], tmpm[:, 0:N], ALU.add, [t_acc, t_tmpm], [t_mg])
        if k + 1 <= NSLOT:
            loads_a(k + 1)
        for c in range(8):
            b = 2 + (c % 2)
            mm(b, psb[b][:, 0:N], [(Wo[:, kc, c * 128:(c + 1) * 128], mg[:, kc, 0:N]) for kc in range(8)],
               [t_Wo, t_mg])
            stt(y[:, c, 0:N], xf[:, c, 0:N], float(ALPHA), psb[b][:, 0:N], ALU.mult, ALU.add, [t_xf, t_ps[b]], [t_y])
        if k + 1 <= NSLOT:
            loads_b(k + 1)
        pending3 += layer_norm_steps(N, y, t_y, C_L1G, C_L1B, lambda c, N=N: y[:, c, 0:N], [t_y])

        def store(N=N, ws=ws, halo=halo):
            if halo:
                cp("dve", h1Hb, y[:, :, 0:8], [t_y], [t_h1H])
            P.dma("sp", h1_d[:, :, ws], y[:, :, 0:N], reads=[t_y], writes=[t_h1d])
        pending3.append(store)
    while pending3:
        pending3.pop(0)()

    if upto < 5:
        P.stop = True
    P.barrier()
    a4 = Alloc(PBASE)
    Wu = a4.b16(8, 5632)
    Wd = a4.b16(22, 1024)
    hb = [a4.b16(8, 512) for _ in range(2)]
    hf = a4.f32(8, 512)
    adead = Alloc(o_snap)
    tg = [a4.f32(512), adead.f32(512)]
    tv = [a4.f32(512), adead.f32(512)]
    bnd = adead.f32(44, 2)
    btmp = adead.f32(44)
    assert adead.off <= o_snap + 2 * max(NSLOT, 4) * 4 * 128 * 2, "P4 scratch overflows the dead snapshot area"
    t_bnd = Tk("bnd")
    o_actb = a4.off
    actb = a4.b16(22, 512)
    a4b = Alloc(o_actb)
    lnm = a4b.f32(512)
    lnv = a4b.f32(512)
    lnr = a4b.f32(512)
    lnq = [a4b.b16(512) for _ in range(2)]
    ybf = [a4b.b16(512) for _ in range(2)]
    lnt = [tg[0], tv[0]]
    t_Wu4 = [Tk(f"Wu{i}") for i in range(4)]
    t_Wd = Tk("Wd")
    t_hb = [Tk("hb0"), Tk("hb1")]
    t_hf = Tk("hf")
    t_tg = [Tk("tg0"), Tk("tg1")]
    t_tv = [Tk("tv0"), Tk("tv1")]
    t_actb = Tk("actb")
    t_lnm, t_lnv, t_lnr = Tk("lnm4"), Tk("lnv4"), Tk("lnr4")
    t_lnq = [Tk("lnq40"), Tk("lnq41")]
    t_ybf = [Tk("ybf40"), Tk("ybf41")]
    t_lnt = [t_tg[0], t_tv[0]]
    alias4 = [t_lnm, t_lnv, t_lnr] + t_lnq + t_ybf

    def inherit(dst, src):
        dst.w = src.w
        dst.r = dict(src.r)

    def absorb(dst, srcs):
        for t in srcs:
            for tok in ([t.w] if t.w else []) + list(t.r.values()):
                kk = id(tok[0])
                if kk not in dst.r or dst.r[kk][1] < tok[1]:
                    dst.r[kk] = tok
    for i in range(4):
        P.dma("sp", Wu[:, :, i * 1408:(i + 1) * 1408], wu_b[:, :, i * 1408:(i + 1) * 1408], reads=[t_wub],
              writes=[t_Wu4[i]])
    P.dma("sp", Wd, wd_b, reads=[t_wdb], writes=[t_Wd])
    for ch in range(44):
        b = 4 + (ch % 4)
        mm(b, psb[b][:, 0:8], [(Wu[:, kc, ch * 128:(ch + 1) * 128], h1Hb[:, kc, :]) for kc in range(8)],
           [t_Wu4[ch // 11], t_h1H])
        tt(uH[:, ch, :], psb[b][:, 0:8], pc[:, PC_HZ:PC_HZ + 8], ALU.mult, [t_ps[b], t_cst], [t_uH])
    P.dma("pool", hb[0], h1_d[:, :, 0:512], reads=[t_h1d], writes=[t_hb[0]])
    for k in range(NSLOT):
        xi = k % 2
        H, tH = hb[xi], t_hb[xi]
        ws = slice(k * 512, (k + 1) * 512)
        if k + 1 < NSLOT:
            P.dma("pool", hb[1 - xi], h1_d[:, :, (k + 1) * 512:(k + 2) * 512], reads=[t_h1d], writes=[t_hb[1 - xi]])
        P.dma("sp", hf, h1_d[:, :, ws], reads=[t_h1d], writes=[t_hf])
        w0v, w1v = cst[:, C_CW:C_CW + 44], cst[:, C_CW + 44:C_CW + 88]
        tt(bnd[:, :, 1], uH[:, :, 2 * k + 1], w0v, ALU.mult, [t_uH, t_cst], [t_bnd])
        tt(bnd[:, :, 0], uH[:, :, 2 * k], w0v, ALU.mult, [t_uH, t_cst], [t_bnd])
        tt(btmp, uH[:, :, 2 * k + 1], w1v, ALU.mult, [t_uH, t_cst], [t_bnd])
        tt(bnd[:, :, 0], bnd[:, :, 0], btmp, ALU.add, [t_bnd], [t_bnd])
        cbv = cst[:, C_CB:C_CB + 44]
        tt(bnd[:, :, 0], bnd[:, :, 0], cbv, ALU.add, [t_bnd, t_cst], [t_bnd])
        tt(bnd[:, :, 1], bnd[:, :, 1], cbv, ALU.add, [t_bnd, t_cst], [t_bnd])
        for c in range(22):
            ci = c % 2
            for br_i, (ch, T, tT) in enumerate(((c, tg[ci], t_tg[ci]), (22 + c, tv[ci], t_tv[ci]))):
                b = 4 + ((2 * c + br_i) % 4)
                w0, w1, w2 = (col(C_CW, t * 44 + ch) for t in range(3))
                mm(b, psb[b][:, :], [(Wu[:, kc, ch * 128:(ch + 1) * 128], H[:, kc, :]) for kc in range(8)],
                   [t_Wu4[ch // 11], tH])
                act(T[:, 2:512], psb[b][:, 2:512], AF.Identity, [t_ps[b], t_cst], [tT], bias=col(C_CB, ch), scale=w2)
                act(T[:, 0:1], psb[b][:, 0:1], AF.Identity, [t_ps[b], t_cst, t_bnd], [tT], bias=bnd[:, ch, 0:1], scale=w2)
                act(T[:, 1:2], psb[b][:, 1:2], AF.Identity, [t_ps[b], t_cst, t_bnd], [tT], bias=bnd[:, ch, 1:2], scale=w2)
                stt(T[:, 2:512], psb[b][:, 0:510], w0, T[:, 2:512], ALU.mult, ALU.add, [t_ps[b], t_cst, tT], [tT])
                stt(T[:, 1:512], psb[b][:, 0:511], w1, T[:, 1:512], ALU.mult, ALU.add, [t_ps[b], t_cst, tT], [tT])
            act(tg[ci], tg[ci], AF.Silu, [t_tg[ci]], [t_tg[ci]])
            tt(actb[:, c, :], tv[ci], tg[ci], ALU.mult, [t_tv[ci], t_tg[ci]], [t_actb], eng="pool")
        for c in range(8):
            b = 2 + (c % 2)
            mm(b, psb[b][:, :], [(Wd[:, kc, c * 128:(c + 1) * 128], actb[:, kc, :]) for kc in range(22)],
               [t_Wd, t_actb])
            stt(hf[:, c, :], hf[:, c, :], float(ALPHA), psb[b][:, :], ALU.mult, ALU.add, [t_hf, t_ps[b]], [t_hf])
        for t in alias4:
            inherit(t, t_actb)
        layer_norm(512, hf, t_hf, C_L2G, C_L2B, lambda c: hf[:, c, :], [t_hf])
        absorb(t_actb, alias4)
        P.dma("sp", outT[:, :, ws], hf, reads=[t_hf], writes=[t_out])
    P.final_wait("sp", [t_out, t_aTd, t_rTd, t_mTd, t_h1d])

    with nc.Block() as block:
        @block.tensor
        def _(e):
            for f in P.ops["pe"]:
                f(e)

        @block.scalar
        def _(e):
            for f in P.ops["act"]:
                f(e)

        @block.vector
        def _(e):
            for f in P.ops["dve"]:
                f(e)

        @block.gpsimd
        def _(e):
            for f in P.ops["pool"]:
                f(e)

        @block.sync
        def _(e):
            for f in P.ops["sp"]:
                f(e)
    es.close()
    return nc


def prep_shared(inp):
    f = np.float32
    d = {}
    d["w_in"] = fm(np.asarray(inp["w_in"], f)[0], 8)
    d["w_mkv"] = fm(np.asarray(inp["w_mem_kv"], f)[0], 8)
    d["w_br"] = np.ascontiguousarray(np.concatenate(
        [fm(np.asarray(inp[k], f)[0], 4) for k in ("w_diff_o", "w_ret_o", "w_mem_o")], axis=1))
    d["w_gate"] = fm(np.asarray(inp["w_gate"], f)[0], 8)
    d["w_out"] = fm(np.asarray(inp["w_mix_out"], f)[0], 8)
    d["w_up"] = fm(np.asarray(inp["w_up"], f)[0], 8)
    d["w_dn"] = fm(np.asarray(inp["w_down"], f)[0], 22)
    cs = np.zeros((128, NCONST), f)
    cs[:, C_BG:C_BG + 24] = vec_fm(np.asarray(inp["b_gate"], f)[0], 24)
    cs[:, C_L1G:C_L1G + 8] = vec_fm(np.asarray(inp["ln1_g"], f)[0], 8)
    cs[:, C_L1B:C_L1B + 8] = vec_fm(np.asarray(inp["ln1_b"], f)[0], 8)
    cs[:, C_L2G:C_L2G + 8] = vec_fm(np.asarray(inp["ln2_g"], f)[0], 8)
    cs[:, C_L2B:C_L2B + 8] = vec_fm(np.asarray(inp["ln2_b"], f)[0], 8)
    cw = np.asarray(inp["conv_w"], f)[0]
    for t in range(3):
        cs[:, C_CW + t * 44:C_CW + (t + 1) * 44] = vec_fm(cw[t], 44)
    cs[:, C_CB:C_CB + 44] = vec_fm(np.asarray(inp["conv_b"], f)[0], 44)
    cs[:, C_SG:C_SG + 4] = vec_fm(np.asarray(inp["diff_subln_g"], f)[0], 4)
    cs[:, C_RG:C_RG + 4] = vec_fm(np.asarray(inp["ret_norm_g"], f)[0], 4)
    d["consts"] = cs
    d["dl"] = np.ascontiguousarray(np.broadcast_to(np.asarray(inp["diff_lambda"], f)[0].reshape(1, 256), (128, 256)))
    d["Ctab"], d["Ttab"], d["Gq"], d["gk"] = shared_tables()
    gkT = np.zeros((128, 4, 2, 4, 64), np.float32)
    for tb in range(4):
        for v in range(2):
            for h in range(4):
                gkT[:, tb, v, h, :] = d["gk"][:, tb * 8 + v * 4 + h][:, None]
    d["gkT"] = gkT.reshape(128, 2048)
    d["ident"] = np.eye(128, dtype=np.float32)
    return d


def prep_core(inp, shared, b, j, NSB):
    NSLOT = NSB // 4
    S = NSB * 512
    f = np.float32
    x = np.asarray(inp["x"], f)[b, :S]
    d = dict(shared)
    xT = np.ascontiguousarray(x.T)
    d["xT"] = fm(xT, 8)
    cols = []
    for k in range(NSLOT):
        sb = 4 * k + j
        cols.append(np.arange(sb * 512, (sb + 1) * 512))
    hal = []
    for k in range(4):
        sb = 4 * k + j
        if k < NSLOT and sb > 0:
            hal += [sb * 512 - 2, sb * 512 - 1]
        else:
            hal += [0, 1]
    idx = np.concatenate(cols + [np.asarray(hal)])
    d["xq"] = fm(np.ascontiguousarray(xT[:, idx]), 8)
    d["memT"] = fm(np.ascontiguousarray(np.asarray(inp["mem"], f)[b].T), 8)
    d["kbW"], d["Bh"], d["pc"] = core_tables(j, NSB)
    return d


_NC_CACHE = {}


def run(inp, NSB=16, debug=False, upto=9):
    key = (NSB, debug, upto)
    if key not in _NC_CACHE:
        _NC_CACHE[key] = build(NSB, debug, upto)
    nc = _NC_CACHE[key]
    shared = prep_shared(inp)
    in_maps = [prep_core(inp, shared, c // 4, c % 4, NSB) for c in range(NCORES)]
    res = run_bass_kernel_spmd(nc, in_maps, core_ids=list(range(NCORES)))
    return res.results


def kernel(**inputs):
    NSB = 16
    NSLOT = NSB // 4
    S = NSB * 512
    r = run(inputs, NSB)
    out = np.empty((2, S, 1024), np.float32)
    for c in range(NCORES):
        b, j = c // 4, c % 4
        oT = np.asarray(r[c]["outT"])
        o = oT.transpose(2, 1, 0).reshape(NSLOT * 512, 1024)
        for k in range(NSLOT):
            sb = 4 * k + j
            out[b, sb * 512:(sb + 1) * 512] = o[k * 512:(k + 1) * 512]
    return out
```

```python
import math
import numpy as np
import concourse.bass as bass
import concourse.mybir as mybir
from concourse.bass_utils import run_bass_kernel_spmd

F32 = mybir.dt.float32
BF16 = mybir.dt.bfloat16
AF = mybir.ActivationFunctionType
ALU = mybir.AluOpType

ALPHA = 2.0 ** 0.25
LAMBDA_INIT = 0.2
EPS = 1e-5
BIGC = 1.0e7
NCORES = 8
SLOPES = [0.25 ** (i + 1) for i in range(4)]
SEM_ROT = 12000


class Tk:
    __slots__ = ("name", "w", "r")

    def __init__(self, name):
        self.name = name
        self.w = None
        self.r = {}


class Prog:
    ENG = ("pe", "act", "dve", "pool", "sp")

    def __init__(self, nc, sem_pool):
        self.nc = nc
        self.sem_pool = sem_pool
        self.ops = {e: [] for e in self.ENG}
        self.cur = {}
        self.cnt = {}
        self.known = {e: {} for e in self.ENG}
        for e in ("pe", "act", "dve", "pool"):
            self.cur[e] = self.sem_pool.pop()
            self.cnt[e] = 0
        self.dma_sem = {}
        self.dma_cnt = {}
        self.stop = False

    def _deps(self, eng, reads, writes):
        need = {}

        def add(tok):
            if tok is None:
                return
            s, v = tok
            if id(s) not in need or need[id(s)][1] < v:
                need[id(s)] = (s, v)

        for t in reads:
            add(t.w)
        for t in writes:
            add(t.w)
            for tok in t.r.values():
                add(tok)
        for s, v in need.values():
            if eng == "pe" and s is self.cur.get("pe"):
                continue
            k = self.known[eng].get(id(s), 0)
            if k >= v:
                continue
            self.known[eng][id(s)] = v
            self.ops[eng].append(lambda e, s=s, v=v: e.wait_ge(s, v))

    def _mark(self, tok, reads, writes):
        for t in reads:
            t.r[id(tok[0])] = tok
        for t in writes:
            t.w = tok
            t.r = {}

    def op(self, eng, fn, reads=(), writes=()):
        if self.stop:
            return
        self._deps(eng, reads, writes)
        if self.cnt[eng] >= SEM_ROT:
            self.cur[eng] = self.sem_pool.pop()
            self.cnt[eng] = 0
        self.cnt[eng] += 1
        s = self.cur[eng]
        v = self.cnt[eng]
        self.ops[eng].append(lambda e, s=s: fn(e).then_inc(s, 1))
        self._mark((s, v), reads, writes)

    def dma(self, q, out, in_, reads=(), writes=(), key=None):
        if self.stop:
            return
        self._deps(q, reads, writes)
        key = key if key is not None else (writes[0].name if writes else reads[0].name)
        if key not in self.dma_sem:
            self.dma_sem[key] = self.sem_pool.pop()
            self.dma_cnt[key] = 0
        self.dma_cnt[key] += 16
        s = self.dma_sem[key]
        v = self.dma_cnt[key]
        self.ops[q].append(lambda e, s=s: e.dma_start(out=out, in_=in_).then_inc(s, 16))
        self._mark((s, v), reads, writes)

    def barrier(self):
        if self.stop:
            return
        toks = [(self.cur[e], self.cnt[e]) for e in ("pe", "act", "dve", "pool") if self.cnt[e] > 0]
        toks += [(self.dma_sem[k], self.dma_cnt[k]) for k in self.dma_sem]
        for eng in self.ENG:
            for s, v in toks:
                if eng == "pe" and s is self.cur.get("pe"):
                    continue
                if self.known[eng].get(id(s), 0) >= v:
                    continue
                self.known[eng][id(s)] = v
                self.ops[eng].append(lambda e, s=s, v=v: e.wait_ge(s, v))

    def final_wait(self, eng, tiles):
        self._deps(eng, tiles, [])


C_BG, C_L1G, C_L1B, C_L2G, C_L2B, C_CW, C_CB, C_SG, C_RG = 0, 24, 32, 40, 48, 56, 188, 232, 236
NCONST = 240
NEGM = -30000.0


def _gammas():
    return [1.0 - 2.0 ** (-5.0 - h) for h in range(4)]


def shared_tables():
    C = np.zeros((128, 4, 512))
    n = np.arange(512)
    for k in range(4):
        s = 128 * k + np.arange(128)
        cs = (s // 64)[:, None]
        cn = (n // 64)[None, :]
        diff = (s[:, None] - n[None, :]).astype(np.float64)
        C[:, k, :] = np.where(cs > cn, -BIGC, np.where((cs == cn) & (diff > 0), -16.0 * diff, 0.0))
    lg = [math.log(x) for x in _gammas()]
    Tt = np.zeros((128, 4, 896))
    c = np.arange(896)
    for h in range(4):
        j = c[None, :] - 384 - np.arange(128)[:, None]
        Tt[:, h, :] = np.where(j >= 0, np.exp(lg[h] * np.maximum(j, 0)) / 8.0, 0.0)
    Gq = np.zeros((128, 4, 512))
    for h in range(4):
        Gq[:, h, :] = np.exp(lg[h] * (n + 1.0))[None, :]
    gk = np.zeros((128, 32))
    for tb in range(4):
        m = 128 * tb + np.arange(128)
        for h in range(4):
            gk[:, tb * 8 + h] = np.exp(lg[h] * (511.0 - m)) / 8.0
            gk[:, tb * 8 + 4 + h] = np.where(m <= 510, np.exp(lg[h] * np.maximum(510.0 - m, 0.0)) / 8.0, 0.0)
    return C.astype(np.float32), Tt.astype(np.float32), Gq.astype(np.float32), gk.astype(np.float32)


def core_tables(j, NSB):
    NSLOT = NSB // 4
    NB = NSB * 4
    p = np.arange(128, dtype=np.float64)
    kbW = np.zeros((128, 4, NSLOT, NB))
    acol = np.zeros((128, 4, NSLOT, 4))
    for k in range(NSLOT):
        sb = 4 * k + j
        for blk in range(NB):
            i = blk // 4
            for h in range(4):
                if i > sb:
                    kbW[:, h, k, blk] = NEGM
                else:
                    kbW[:, h, k, blk] = SLOPES[h] * (128.0 * blk + p - 512.0 * sb - 256.0)
        for ii in range(4):
            for h in range(4):
                acol[:, h, k, ii] = SLOPES[h] if (4 * k + ii == sb) else 0.0
    Bh = np.zeros((128, NB, 8))
    hz = np.zeros((128, 8))
    for k in range(NSLOT):
        sb = 4 * k + j
        t0 = 512 * sb
        if sb == 0:
            continue
        for r in range(2):
            tq = t0 - 2 + r
            e = 2 * k + r
            hz[:, e] = 1.0
            for blk in range(NB):
                s = 128.0 * blk + p
                Bh[:, blk, e] = np.where(s < t0, -8.0 * np.abs(tq - s), -BIGC)
    sel = np.zeros((128, 16))
    for k in range(NSLOT):
        sel[:, 4 * k + j] = 1.0
    pc = np.zeros((128, 64 + 16 + 8), np.float32)
    for h in range(4):
        for k in range(NSLOT):
            for ii in range(4):
                pc[:, h * 16 + k * 4 + ii] = acol[:, h, k, ii]
    pc[:, 64:80] = sel
    pc[:, 80:88] = hz
    return kbW.reshape(128, -1).astype(np.float32), Bh.astype(np.float32), pc


PC_ACOL, PC_SEL, PC_HZ = 0, 64, 80
NPC = 88


def fm(w, kc):
    return np.ascontiguousarray(w.reshape(kc, 128, -1).transpose(1, 0, 2))


def vec_fm(v, nch):
    return np.ascontiguousarray(v.reshape(nch, 128).T)


def build(NSB=16, debug=False, upto=9):
    NSLOT = NSB // 4
    S = NSB * 512
    NB = NSB * 4
    NQ = NSLOT * 512
    NQT = NQ + 8
    nc = bass.Bass("TRN2", target_bir_lowering=False)

    def din(name, shape, dt=F32):
        return nc.dram_tensor(name, list(shape), dt, kind="ExternalInput").ap()

    xT = din("xT", [128, 8, S])
    xq = din("xq", [128, 8, NQT])
    memT = din("memT", [128, 8, 256])
    w_in = din("w_in", [128, 8, 3584])
    w_mkv = din("w_mkv", [128, 8, 1024])
    w_br = din("w_br", [128, 12, 1024])
    w_gate = din("w_gate", [128, 8, 3072])
    w_out = din("w_out", [128, 8, 1024])
    w_up = din("w_up", [128, 8, 5632])
    w_dn = din("w_dn", [128, 22, 1024])
    consts_d = din("consts", [128, NCONST])
    dl_d = din("dl", [128, 256])
    kbW_d = din("kbW", [128, 4 * NSLOT * NB])
    Bh_d = din("Bh", [128, NB, 8])
    pc_d = din("pc", [128, NPC])
    gk_d = din("gk", [128, 32])
    gkT_d = din("gkT", [128, 2048])
    C_d = din("Ctab", [128, 4, 512])
    I_d = din("ident", [128, 128])
    Tt_d = din("Ttab", [128, 4, 896])
    Gq_d = din("Gq", [128, 4, 512])
    outT = nc.dram_tensor("outT", [128, 8, NQ], F32, kind="ExternalOutput").ap()
    skind = "ExternalOutput" if debug else "Internal"
    br_d = nc.dram_tensor("br_d", [128, 12, NQT], BF16, kind=skind).ap()
    aT_d, rT_d, mT_d = br_d[:, 0:4, :], br_d[:, 4:8, :], br_d[:, 8:12, :]
    h1_d = nc.dram_tensor("h1_d", [128, 8, NQT], F32, kind=skind).ap()
    wu_b = nc.dram_tensor("wu_b", [128, 8, 5632], BF16, kind="Internal").ap()
    wd_b = nc.dram_tensor("wd_b", [128, 22, 1024], BF16, kind="Internal").ap()
    t_wub, t_wdb = Tk("wub"), Tk("wdb")

    from contextlib import ExitStack
    es = ExitStack()
    big = es.enter_context(nc.sbuf_tensor("big", [128, 53200], F32))
    TOTAL = 53200 * 4
    bigb = big.bitcast(BF16)
    pspair = [es.enter_context(nc.psum_tensor(f"psp{i}", [128, 1024], F32)) for i in range(4)]
    psb = [pspair[i // 2][:, (i % 2) * 512:(i % 2 + 1) * 512] for i in range(8)]
    sem_pool = [es.enter_context(nc.semaphore(f"s{i}")) for i in range(72)]
    P = Prog(nc, sem_pool)

    class Alloc:
        def __init__(self, base):
            self.off = base

        def f32(self, *shape):
            n = int(np.prod(shape))
            o = (self.off + 3) // 4
            self.off = (o + n) * 4
            assert self.off <= TOTAL, f"SBUF overflow {self.off}"
            return self._shape(big[:, o:o + n], shape)

        def b16(self, *shape):
            n = int(np.prod(shape))
            o = (self.off + 3) // 4 * 2
            self.off = ((o + n) * 2 + 3) // 4 * 4
            assert self.off <= TOTAL, f"SBUF overflow {self.off}"
            return self._shape(bigb[:, o:o + n], shape)

        @staticmethod
        def _shape(v, shape):
            if len(shape) == 1:
                return v
            if len(shape) == 2:
                return v.rearrange("p (a b) -> p a b", b=shape[1])
            return v.rearrange("p (a b c) -> p a b c", b=shape[1], c=shape[2])

    pa = Alloc(0)
    cst = pa.f32(NCONST)
    pc = pa.f32(NPC)
    gk = pa.f32(32)
    dl = pa.f32(256)
    ones_bf = pa.b16(128)
    onesb128 = pa.b16(128)
    onesb1024 = pa.b16(128)
    epsc = pa.f32(1)
    neglam = pa.f32(1)
    lamtmp = pa.f32(128)
    lam2 = pa.f32(2)
    gs = pa.f32(4)
    mkT = pa.b16(4, 256)
    mv = pa.b16(2, 512)
    o_snap = pa.off
    snapS = pa.b16(max(NSLOT, 4) * 4, 128)
    snapS2 = pa.b16(max(NSLOT, 4) * 4, 128)
    h1Hb = pa.b16(8, 8)
    uH = pa.f32(44, 8)
    PBASE = pa.off

    t_cst, t_misc, t_mk, t_mv = Tk("cst"), Tk("misc"), Tk("mkT"), Tk("mv")
    t_snap, t_h1H, t_uH = Tk("snap"), Tk("h1H"), Tk("uH")
    t_ps = [Tk(f"ps{i}") for i in range(8)]

    def col(off, i=0):
        return cst[:, off + i:off + i + 1]

    def pcol(off, i=0):
        return pc[:, off + i:off + i + 1]

    def mm(ps_i, out_ap, pairs, reads, start=True, stop=True):
        n = len(pairs)

        def fn(e):
            ins = None
            for i, (l, r) in enumerate(pairs):
                ins = e.matmul(out_ap, l, r, start=(start and i == 0), stop=(stop and i == n - 1))
            return ins
        P.op("pe", fn, reads=reads, writes=[t_ps[ps_i]])

    def mml(ps_is, items, reads):
        def fn(e):
            ins = None
            for (o, l, r, st, sp_) in items:
                ins = e.matmul(o, l, r, start=st, stop=sp_)
            return ins
        P.op("pe", fn, reads=reads, writes=[t_ps[i] for i in ps_is])

    def act(out, in_, func, reads, writes, bias=None, scale=None):
        kw = {}
        if bias is not None:
            kw["bias"] = bias
        if scale is not None:
            kw["scale"] = scale
        P.op("act", lambda e: e.activation(out, in_, func, **kw), reads=reads, writes=writes)

    def tt(out, a, b, op, reads, writes, eng="dve"):
        P.op(eng, lambda e: e.tensor_tensor(out, a, b, op), reads=reads, writes=writes)

    def stt(out, a, sc, b, op0, op1, reads, writes):
        P.op("dve", lambda e: e.scalar_tensor_tensor(out, a, sc, b, op0, op1), reads=reads, writes=writes)

    def ts(out, a, s1, s2, op0, op1, reads, writes, eng="dve"):
        if op1 is None:
            P.op(eng, lambda e: e.tensor_scalar(out, a, s1, None, op0), reads=reads, writes=writes)
        else:
            P.op(eng, lambda e: e.tensor_scalar(out, a, s1, s2, op0, op1), reads=reads, writes=writes)

    def cp(eng, out, in_, reads, writes):
        if eng == "act":
            P.op("act", lambda e: e.copy(out, in_), reads=reads, writes=writes)
        else:
            P.op(eng, lambda e: e.tensor_copy(out, in_), reads=reads, writes=writes)

    def recip(out, in_, reads, writes):
        act(out, in_, AF.Ln, reads, writes)
        act(out, out, AF.Exp, writes, writes, scale=-1.0)

    def recip_exact(out, in_, reads, writes):
        P.op("dve", lambda e: e.reciprocal(out, in_), reads=reads, writes=writes)

    def rstd_from(ps_i, ps_ap, tmp, t_tmp, out, t_out):
        act(tmp, ps_ap, AF.Ln, [t_ps[ps_i], t_misc], [t_tmp], bias=epsc[:, 0:1], scale=1.0)
        act(out, tmp, AF.Exp, [t_tmp], [t_out], scale=-0.5)

    rot = [0]

    def nextbank():
        rot[0] = (rot[0] + 1) % 4
        return 4 + rot[0]

    P.dma("sp", cst, consts_d, writes=[t_cst])
    P.dma("sp", pc, pc_d, writes=[t_cst], key="cst")
    P.dma("sp", gk, gk_d, writes=[t_cst], key="cst")
    P.dma("sp", dl, dl_d, writes=[t_misc], key="miscld")
    P.op("pool", lambda e: e.memset(ones_bf, 1.0), writes=[t_misc])
    P.op("pool", lambda e: e.memset(onesb128, 1.0 / 128.0), writes=[t_misc])
    P.op("pool", lambda e: e.memset(onesb1024, 1.0 / 1024.0), writes=[t_misc])
    P.op("pool", lambda e: e.memset(epsc, EPS), writes=[t_misc])
    P.op("pool", lambda e: e.memset(snapS, 0.0), writes=[t_snap])
    P.op("pool", lambda e: e.memset(snapS2, 0.0), writes=[t_snap])
    dl4 = dl.rearrange("p (a b c) -> p a b c", b=2, c=64)
    lt = lamtmp.rearrange("p (a c) -> p a c", c=64)
    t_lam = Tk("lam")
    tt(lt, dl4[:, :, 0, :], dl4[:, :, 1, :], ALU.mult, [t_misc], [t_lam])
    P.op("dve", lambda e: e.tensor_reduce(lam2, lt, mybir.AxisListType.X, ALU.add), reads=[t_lam], writes=[t_lam])
    act(lam2, lam2, AF.Exp, [t_lam], [t_lam])
    tt(neglam, lam2[:, 1:2], lam2[:, 0:1], ALU.subtract, [t_lam], [t_lam])
    ts(neglam, neglam, -LAMBDA_INIT, None, ALU.add, None, [t_lam], [t_lam])
    ts(gs, cst[:, C_SG:C_SG + 4], 1.0 - LAMBDA_INIT, None, ALU.mult, None, [t_cst], [t_lam])

    a1 = Alloc(PBASE)
    KT = a1.b16(4, S)
    Vt = a1.b16(NB, 512)
    xb = [a1.b16(8, 512) for _ in range(2)]
    kbW = a1.f32(4 * NSLOT * NB)
    RB = a1.off
    t_KT = [[Tk(f"KT{h}_{s}") for s in range(NSB)] for h in range(4)]
    t_Vt = [Tk(f"Vt{b}") for b in range(NB)]
    t_xb = [Tk("xb0"), Tk("xb1")]
    t_tab = Tk("tab")
    P.dma("sp", kbW, kbW_d, writes=[t_tab], key="tab")

    r1 = Alloc(RB)
    W1 = r1.b16(8, 1792)
    rkd = [r1.b16(256) for _ in range(2)]
    rkd2 = [r1.b16(256) for _ in range(2)]
    rvt = [r1.b16(512) for _ in range(2)]
    Sf = r1.f32(4, 128)
    S2f = r1.f32(4, 128)
    GkT = r1.b16(8, 256)
    t_GkT = Tk("GkT")
    t_W1 = Tk("W1")
    t_rkd = [Tk("rkd0"), Tk("rkd1")]
    t_rvt = [Tk("rvt0"), Tk("rvt1")]
    t_Sf, t_S2f = Tk("Sf"), Tk("S2f")
    P.dma("pool", W1[:, :, 0:1024], w_in[:, :, 512:1536], writes=[t_W1], key="W1")
    P.dma("pool", W1[:, :, 1024:1792], w_in[:, :, 1792:2560], writes=[t_W1], key="W1")
    P.dma("pool", GkT, gkT_d, writes=[t_GkT])
    P.op("pool", lambda e: e.memset(Sf, 0.0), writes=[t_Sf])
    P.op("pool", lambda e: e.memset(S2f, 0.0), writes=[t_S2f])
    P.dma("pool", xb[0], xT[:, :, 0:512], writes=[t_xb[0]])
    a0 = Alloc(PBASE)
    memb = a0.b16(8, 256)
    Wm = a0.b16(8, 1024)
    t_Wm, t_memb = Tk("Wm"), Tk("memb")
    P.dma("pool", memb, memT, writes=[t_memb])
    P.dma("pool", Wm, w_mkv, writes=[t_Wm])
    for h in range(4):
        b = 4 + h
        mm(b, psb[b][:, 0:256], [(Wm[:, kc, h * 128:(h + 1) * 128], memb[:, kc, :]) for kc in range(8)],
           [t_Wm, t_memb])
        cp("act", mkT[:, h, :], psb[b][:, 0:256], [t_ps[b]], [t_mk])
    for blk in range(2):
        b = 4 + blk
        mm(b, psb[b][:, :], [(memb[:, kc, blk * 128:(blk + 1) * 128], Wm[:, kc, 512:1024]) for kc in range(8)],
           [t_Wm, t_memb])
        cp("dve", mv[:, blk, :], psb[b][:, :], [t_ps[b]], [t_mv])
    P.barrier()
    if upto < 1:
        P.stop = True

    G512 = [math.exp(math.log(g_) * 512.0) for g_ in _gammas()]
    G511 = [math.exp(math.log(g_) * 511.0) for g_ in _gammas()]
    rot1 = [0]

    def nextbank1():
        rot1[0] = (rot1[0] + 1) % 6
        return 2 + rot1[0]
    for s in range(NSB):
        xi = s % 2
        X, tX = xb[xi], t_xb[xi]
        if s > 0:
            P.dma("pool", X, xT[:, :, s * 512:(s + 1) * 512], writes=[tX])
        for h in range(4):
            b = nextbank1()
            mm(b, psb[b][:, :], [(W1[:, kc, h * 128:(h + 1) * 128], X[:, kc, :]) for kc in range(8)], [t_W1, tX])
            cp("act", KT[:, h, s * 512:(s + 1) * 512], psb[b][:, :], [t_ps[b]], [t_KT[h][s]])
        for tb in range(4):
            b = nextbank1()
            mm(b, psb[b][:, :], [(X[:, kc, tb * 128:(tb + 1) * 128], W1[:, kc, 512:1024]) for kc in range(8)],
               [t_W1, tX])
            cp("dve", Vt[:, s * 4 + tb, :], psb[b][:, :], [t_ps[b]], [t_Vt[s * 4 + tb]])
        k = s // 4
        if k < NSLOT:
            stt(snapS[0:64, k * 4:(k + 1) * 4, :], Sf[0:64, :, :], pcol(PC_SEL, s)[0:64, :],
                snapS[0:64, k * 4:(k + 1) * 4, :], ALU.mult, ALU.add, [t_Sf, t_cst, t_snap], [t_snap])
            stt(snapS2[0:64, k * 4:(k + 1) * 4, :], S2f[0:64, :, :], pcol(PC_SEL, s)[0:64, :],
                snapS2[0:64, k * 4:(k + 1) * 4, :], ALU.mult, ALU.add, [t_S2f, t_cst, t_snap], [t_snap])
        if s < NSB - 1:
            for tb in range(4):
                ti = tb % 2
                b = nextbank1()
                mm(b, psb[b][:, 0:256], [(X[:, kc, tb * 128:(tb + 1) * 128], W1[:, kc, 1024:1280]) for kc in range(8)],
                   [t_W1, tX])
                tt(rkd[ti], psb[b][:, 0:256], GkT[:, tb * 2, :], ALU.mult, [t_ps[b], t_GkT], [t_rkd[ti]])
                tt(rkd2[ti], psb[b][:, 0:256], GkT[:, tb * 2 + 1, :], ALU.mult, [t_ps[b], t_GkT], [t_rkd[ti]])
                b = nextbank1()
                mm(b, psb[b][:, :], [(X[:, kc, tb * 128:(tb + 1) * 128], W1[:, kc, 1280:1792]) for kc in range(8)],
                   [t_W1, tX])
                cp("dve", rvt[ti], psb[b][:, :], [t_ps[b]], [t_rvt[ti]])
                items = []
                for h in range(4):
                    items.append((psb[0][0:64, h * 128:(h + 1) * 128], rkd[ti][:, h * 64:(h + 1) * 64],
                                  rvt[ti][:, h * 128:(h + 1) * 128], (tb == 0 and h == 0), (tb == 3 and h == 3)))
                for h in range(4):
                    items.append((psb[1][0:64, h * 128:(h + 1) * 128], rkd2[ti][:, h * 64:(h + 1) * 64],
                                  rvt[ti][:, h * 128:(h + 1) * 128], (tb == 0 and h == 0), (tb == 3 and h == 3)))
                mml([0, 1], items, [t_rkd[ti], t_rvt[ti]])
            for h in range(4):
                stt(S2f[0:64, h, :], Sf[0:64, h, :], float(G511[h]), psb[1][0:64, h * 128:(h + 1) * 128], ALU.mult,
                    ALU.add, [t_Sf, t_ps[1]], [t_S2f])
            for h in range(4):
                stt(Sf[0:64, h, :], Sf[0:64, h, :], float(G512[h]), psb[0][0:64, h * 128:(h + 1) * 128], ALU.mult,
                    ALU.add, [t_Sf, t_ps[0]], [t_Sf])
    P.barrier()

    if upto < 2:
        P.stop = True
    r2 = Alloc(RB)
    Wq = r2.b16(8, 512)
    QT = r2.b16(4, 512)
    PTp = [r2.b16(1024) for _ in range(2)]
    PT = [[PTp[jj][:, m * 512:(m + 1) * 512] for jj in range(2)] for m in range(2)]
    sqb = r2.b16(512)
    OLc = [r2.f32(512) for _ in range(4)]
    Bf = OLc[3]
    R1 = r2.f32(512)
    Af = r2.f32(512)
    aw = r2.b16(4, 512)
    Cb = r2.b16(4, 512)
    BhT = r2.f32(NB, 8)
    identb = r2.b16(128)
    selI = [r2.b16(128) for _ in range(2)]
    t_ctab, t_bh, t_selI = Tk("ctab"), Tk("bh"), [Tk("selI0"), Tk("selI1")]
    selcnt = [0]
    t_Wq, t_QT = Tk("Wq"), Tk("QT")
    t_PT = [[Tk(f"PT{m}{j}") for j in range(2)] for m in range(2)]
    t_sqb = Tk("sqb")
    t_OLc = [Tk(f"OLc{i}") for i in range(4)]
    t_Bf = t_OLc[3]
    t_R1, t_Af, t_aw = Tk("R1"), Tk("Af"), Tk("aw")
    t_aTd, t_rTd, t_mTd, t_h1d, t_out = Tk("aTd"), Tk("rTd"), Tk("mTd"), Tk("h1d"), Tk("outd")
    P.dma("pool", Wq, w_in[:, :, 0:512], writes=[t_Wq])
    P.dma("pool", Cb, C_d, writes=[t_ctab], key="ctab")
    P.dma("pool", identb, I_d, writes=[t_ctab], key="ctab")
    P.dma("sp", BhT, Bh_d, writes=[t_bh])

    def post_evac(N):
        for i in range(4):
            cp("act" if i % 2 == 0 else "dve", OLc[i][:, 0:N], psb[i][:, 0:N], [t_ps[i]], [t_OLc[i]])

    def post_a(N):
        rc = recip_exact if N == 512 else recip
        rc(R1[:, 0:N], OLc[2][:, 0:N], [t_OLc[2]], [t_R1])
        tt(Af[:, 0:N], OLc[0][:, 0:N], R1[:, 0:N], ALU.mult, [t_OLc[0], t_R1], [t_Af])
        rc(R1[:, 0:N], OLc[3][:, 0:N], [t_OLc[3]], [t_R1])
        tt(Bf[:, 0:N], OLc[1][:, 0:N], R1[:, 0:N], ALU.mult, [t_OLc[1], t_R1], [t_Bf])
        stt(Af[:, 0:N], Bf[:, 0:N], neglam[:, 0:1], Af[:, 0:N], ALU.mult, ALU.add, [t_Bf, t_Af, t_lam], [t_Af])
        tt(sqb[:, 0:N], Af[:, 0:N], Af[:, 0:N], ALU.mult, [t_Af], [t_sqb])

    def post_b(N, h):
        mm(4, psb[4][:, 0:N], [(onesb128, sqb[:, 0:N])], [t_misc, t_sqb])
        rstd_from(4, psb[4][:, 0:N], R1[:, 0:N], t_R1, R1[:, 0:N], t_R1)
        tt(Af[:, 0:N], Af[:, 0:N], R1[:, 0:N], ALU.mult, [t_Af, t_R1], [t_Af])
        ts(aw[:, h, 0:N], Af[:, 0:N], gs[:, h:h + 1], None, ALU.mult, None, [t_Af, t_lam], [t_aw])

    def diff_post(N, h):
        post_evac(N)
        post_a(N)
        post_b(N, h)

    pend_a, pend_b = [], []

    for k in range(NSLOT):
        xi = k % 2
        X, tX = xb[xi], t_xb[xi]
        P.dma("pool", X, xq[:, :, k * 512:(k + 1) * 512], writes=[tX])
        if k == 1:
            for i in range(4):
                P.dma("pool", wu_b[:, :, i * 1408:(i + 1) * 1408], w_up[:, :, i * 1408:(i + 1) * 1408],
                      writes=[t_wub], key="wub")
            P.dma("pool", wd_b, w_dn, writes=[t_wdb], key="wdb")
        for h in range(4):
            b = nextbank()
            mm(b, psb[b][:, :], [(Wq[:, kc, h * 128:(h + 1) * 128], X[:, kc, :]) for kc in range(8)], [t_Wq, tX])
            cp("act", QT[:, h, :], psb[b][:, :], [t_ps[b]], [t_QT])
        for h in range(4):
            blocks = [(i, kb) for i in range(4 * k + 4) for kb in range(4)]
            n = len(blocks)

            def qk(j):
                i, kb = blocks[j]
                blk = i * 4 + kb
                pA, pB = 4 + 2 * (j % 2), 5 + 2 * (j % 2)
                ks = slice(blk * 128, (blk + 1) * 128)
                if i >= 4 * k:
                    if kb == 0:
                        selcnt[0] += 1
                        q_ = selcnt[0] % 2
                        ts(selI[q_], identb, pcol(PC_ACOL, h * 16 + k * 4 + (i - 4 * k)), None, ALU.mult, None,
                           [t_ctab, t_cst], [t_selI[q_]])
                    q_ = selcnt[0] % 2
                    mml([pA, pB], [(psb[pA][:, :], KT[0:64, h, ks], QT[0:64, h, :], True, False),
                                   (psb[pA][:, :], selI[q_], Cb[:, kb, :], False, True),
                                   (psb[pB][:, :], KT[64:128, h, ks], QT[64:128, h, :], True, False),
                                   (psb[pB][:, :], selI[q_], Cb[:, kb, :], False, True)],
                        [t_KT[h][i], t_QT, t_selI[q_], t_ctab])
                else:
                    mml([pA, pB], [(psb[pA][:, :], KT[0:64, h, ks], QT[0:64, h, :], True, True),
                                   (psb[pB][:, :], KT[64:128, h, ks], QT[64:128, h, :], True, True)],
                        [t_KT[h][i], t_QT])

            def ex(j):
                i, kb = blocks[j]
                blk = i * 4 + kb
                cidx = (h * NSLOT + k) * NB + blk
                kcol = kbW[:, cidx:cidx + 1]
                pA = 4 + 2 * (j % 2)
                act(PTp[j % 2], pspair[pA // 2][:, :], AF.Exp, [t_ps[pA], t_ps[pA + 1], t_tab],
                    [t_PT[0][j % 2], t_PT[1][j % 2]], bias=kcol, scale=0.125)

            def pv(j):
                i, kb = blocks[j]
                blk = i * 4 + kb
                vs = Vt[:, blk, h * 128:(h + 1) * 128]
                st, sp_ = (j == 0), (j == n - 1)
                items = []
                for m in range(2):
                    items.append((psb[m][:, :], vs, PT[m][j % 2], st, sp_))
                    items.append((psb[2 + m][:, :], ones_bf, PT[m][j % 2], st, sp_))
                mml([0, 1, 2, 3], items, [t_Vt[blk], t_PT[0][j % 2], t_PT[1][j % 2], t_misc])

            qk(0)
            for j in range(n):
                if j + 1 < n:
                    qk(j + 1)
                if j == 2 and pend_a:
                    pend_a.pop(0)()
                ex(j)
                if j == 12 and pend_b:
                    pend_b.pop(0)()
                pv(j)
            post_evac(512)
            pend_a.append(lambda: post_a(512))

            def fin(h=h, k=k):
                post_b(512, h)
                if h == 3:
                    P.dma("sp", aT_d[:, :, k * 512:(k + 1) * 512], aw, reads=[t_aw], writes=[t_aTd])
            pend_b.append(fin)
    while pend_a:
        pend_a.pop(0)()
    while pend_b:
        pend_b.pop(0)()

    X, tX = xb[0], t_xb[0]
    P.dma("pool", X[:, :, 0:8], xq[:, :, NQ:NQ + 8], writes=[tX])
    for h in range(4):
        b = nextbank()
        mm(b, psb[b][:, 0:8], [(Wq[:, kc, h * 128:(h + 1) * 128], X[:, kc, 0:8]) for kc in range(8)], [t_Wq, tX])
        cp("act", QT[:, h, 0:8], psb[b][:, 0:8], [t_ps[b]], [t_QT])
    NG = NB // 8
    for h in range(4):
        for g in range(NG):
            pA, pB = 4 + 2 * (g % 2), 5 + 2 * (g % 2)
            items = []
            rd = [t_QT]
            for bl in range(8):
                blk = g * 8 + bl
                ks = slice(blk * 128, (blk + 1) * 128)
                items.append((psb[pA][:, bl * 8:(bl + 1) * 8], KT[0:64, h, ks], QT[0:64, h, 0:8], True, True))
                items.append((psb[pB][:, bl * 8:(bl + 1) * 8], KT[64:128, h, ks], QT[64:128, h, 0:8], True, True))
                if t_KT[h][blk // 4] not in rd:
                    rd.append(t_KT[h][blk // 4])
            mml([pA, pB], items, rd)
            bh = BhT[:, g * 8:(g + 1) * 8, :]
            for m in range(2):
                pi = pA + m
                p3 = psb[pi][:, 0:64].rearrange("p (a b) -> p a b", b=8)
                stt(p3, bh, float(SLOPES[h]), p3, ALU.mult, ALU.add, [t_bh, t_ps[pi]], [t_ps[pi]])
                act(PT[m][g % 2][:, 0:64], psb[pi][:, 0:64], AF.Exp, [t_ps[pi]], [t_PT[m][g % 2]], scale=0.125)
            items = []
            rd = [t_PT[0][g % 2], t_PT[1][g % 2], t_misc]
            for bl in range(8):
                blk = g * 8 + bl
                st, sp_ = (g == 0 and bl == 0), (g == NG - 1 and bl == 7)
                for m in range(2):
                    rhs = PT[m][g % 2][:, bl * 8:(bl + 1) * 8]
                    items.append((psb[m][:, 0:8], Vt[:, blk, h * 128:(h + 1) * 128], rhs, st, sp_))
                    items.append((psb[2 + m][:, 0:8], ones_bf, rhs, st, sp_))
                rd.append(t_Vt[blk])
            mml([0, 1, 2, 3], items, rd)
        diff_post(8, h)
    P.dma("sp", aT_d[:, :, NQ:NQ + 8], aw[:, :, 0:8], reads=[t_aw], writes=[t_aTd])

    if upto < 3:
        P.stop = True
    P.barrier()
    a2 = Alloc(PBASE)
    Wp = a2.b16(8, 2048)
    Tt = a2.f32(4, 896)
    Gq = a2.f32(4, 512)
    xb2 = [a2.b16(8, 512)] * 2
    rqT = a2.b16(2, 512)
    rqd4 = a2.b16(4, 512)
    rkT = a2.b16(2, 512)
    rv = a2.b16(4, 512)
    sg = a2.b16(4, 512)
    mqT = a2.b16(4, 512)
    rw = a2.b16(4, 512)
    mw = a2.b16(4, 512)
    Sd = [a2.b16(512) for _ in range(2)]
    PT2 = [a2.b16(512) for _ in range(2)]
    rf = a2.f32(512)
    cen = a2.f32(512)
    rs = a2.f32(512)
    rfb = a2.b16(512)
    sqb2 = a2.b16(512)
    t_Wp4 = [Tk(f"Wp{i}") for i in range(4)]
    t_tab2 = Tk("tab2")
    t_xb2 = [Tk("xb20")] * 2
    t_rqT, t_rqd4, t_rkT, t_rv, t_sg, t_mqT = (Tk(x) for x in ("rqT", "rqd4", "rkT", "rv", "sg", "mqT"))
    t_rw, t_mw = Tk("rw"), Tk("mw")
    t_Sd = [Tk("Sd0"), Tk("Sd1")]
    t_PT2 = [Tk("PT20"), Tk("PT21")]
    t_rf, t_cen, t_rs = Tk("rf"), Tk("cen"), Tk("rs")
    t_rfb, t_sqb2 = Tk("rfb"), Tk("sqb2")
    for i in range(4):
        P.dma("pool", Wp[:, :, i * 512:(i + 1) * 512], w_in[:, :, 1536 + i * 512:1536 + (i + 1) * 512], writes=[t_Wp4[i]])
    P.dma("sp", Tt, Tt_d, writes=[t_tab2], key="tab2")
    P.dma("sp", Gq, Gq_d, writes=[t_tab2], key="tab2")
    TOP3 = TOTAL - (8 * 3072 + 12 * 1024 + 8 * 1024) * 2
    assert a2.off <= TOP3, f"P3a region {a2.off} overlaps P3b weight prefetch area {TOP3}"
    aw3 = Alloc(TOP3)
    Wg = aw3.b16(8, 3072)
    Wbr = aw3.b16(12, 1024)
    Wo = aw3.b16(8, 1024)
    t_Wg2 = [Tk("Wg0"), Tk("Wg1")]
    t_Wbr, t_Wo = Tk("Wbr"), Tk("Wo")
    Wg4 = Wg.rearrange("p k (j c) -> p k j c", j=3)
    wg4 = w_gate.rearrange("p k (j c) -> p k j c", j=3)
    prefetch3 = [lambda i=i: P.dma("pool", Wg4[:, :, :, i * 512:(i + 1) * 512], wg4[:, :, :, i * 512:(i + 1) * 512],
                                   writes=[t_Wg2[i]]) for i in range(2)]
    prefetch3.append(lambda: P.dma("pool", Wbr, w_br, writes=[t_Wbr]))
    prefetch3.append(lambda: P.dma("pool", Wo, w_out, writes=[t_Wo]))

    def group_norm_gate(N, src, h_or_none):
        cp("act", rfb[:, 0:N], src, [t_rf], [t_rfb])
        mm(1, psb[1][:, 0:N], [(onesb128, rfb[:, 0:N])], [t_misc, t_rfb])
        tt(cen[:, 0:N], src, psb[1][:, 0:N], ALU.subtract, [t_rf, t_ps[1]], [t_cen])
        tt(sqb2[:, 0:N], cen[:, 0:N], cen[:, 0:N], ALU.mult, [t_cen], [t_sqb2])
        mm(1, psb[1][:, 0:N], [(onesb128, sqb2[:, 0:N])], [t_misc, t_sqb2])
        rstd_from(1, psb[1][:, 0:N], rs[:, 0:N], t_rs, rs[:, 0:N], t_rs)
        tt(cen[:, 0:N], cen[:, 0:N], rs[:, 0:N], ALU.mult, [t_cen, t_rs], [t_cen])

    def mem_core(N, h):
        for blk in range(2):
            b = 4 + blk
            mm(b, psb[b][:, 0:N], [(mkT[:, h, blk * 128:(blk + 1) * 128], mqT[:, h, 0:N])], [t_mk, t_mqT])
            act(PT2[blk][:, 0:N], psb[b][:, 0:N], AF.Exp, [t_ps[b]], [t_PT2[blk]], scale=1.0 / math.sqrt(128.0))
            mml([6, 7], [(psb[6][:, 0:N], mv[:, blk, h * 128:(h + 1) * 128], PT2[blk][:, 0:N], blk == 0, blk == 1),
                         (psb[7][:, 0:N], ones_bf, PT2[blk][:, 0:N], blk == 0, blk == 1)],
                [t_mv, t_PT2[blk], t_misc])

    def mem_finish(N, h):
        recip(rs[:, 0:N], psb[7][:, 0:N], [t_ps[7]], [t_rs])
        tt(mw[:, h, 0:N], psb[6][:, 0:N], rs[:, 0:N], ALU.mult, [t_ps[6], t_rs], [t_mw])

    def mem_attn(N, h):
        mem_core(N, h)
        mem_finish(N, h)

    for k in range(NSLOT + 1):
        halo = (k == NSLOT)
        N = 8 if halo else 512
        c0 = NQ if halo else k * 512
        xi = k % 2
        X, tX = xb2[xi], t_xb2[xi]
        P.dma("pool", X[:, :, 0:N], xq[:, :, c0:c0 + N], writes=[tX])
        for _ in range(2):
            if prefetch3:
                prefetch3.pop(0)()
        if not halo:
            for c in range(2):
                b = nextbank()
                mm(b, psb[b][:, :], [(Wp[:, kc, c * 128:(c + 1) * 128], X[:, kc, :]) for kc in range(8)], [t_Wp4[0], tX])
                cp("act", rqT[:, c, :], psb[b][:, :], [t_ps[b]], [t_rqT])
            for c in range(2):
                b = nextbank()
                mm(b, psb[b][:, :], [(Wp[:, kc, 256 + c * 128:256 + (c + 1) * 128], X[:, kc, :]) for kc in range(8)],
                   [t_Wp4[0], tX])
                cp("act", rkT[:, c, :], psb[b][:, :], [t_ps[b]], [t_rkT])
            for tb in range(4):
                b = nextbank()
                mm(b, psb[b][:, :], [(X[:, kc, tb * 128:(tb + 1) * 128], Wp[:, kc, 512:1024]) for kc in range(8)],
                   [t_Wp4[1], tX])
                cp("act", rv[:, tb, :], psb[b][:, :], [t_ps[b]], [t_rv])
        for h in range(4):
            b = nextbank()
            mm(b, psb[b][0:64, 0:N], [(Wp[:, kc, h * 64:(h + 1) * 64], X[:, kc, 0:N]) for kc in range(8)], [t_Wp4[0], tX])
            if halo:
                cp("dve", rqd4[0:64, h, 0:N], psb[b][0:64, 0:N], [t_ps[b]], [t_rqd4])
            else:
                tt(rqd4[0:64, h, :], psb[b][0:64, :], Gq[0:64, h, :], ALU.mult, [t_ps[b], t_tab2], [t_rqd4])
        for h in range(4):
            b = nextbank()
            mm(b, psb[b][:, 0:N], [(Wp[:, kc, 1024 + h * 128:1024 + (h + 1) * 128], X[:, kc, 0:N]) for kc in range(8)],
               [t_Wp4[2], tX])
            act(sg[:, h, 0:N], psb[b][:, 0:N], AF.Silu, [t_ps[b]], [t_sg])
        for h in range(4):
            b = nextbank()
            mm(b, psb[b][:, 0:N], [(Wp[:, kc, 1536 + h * 128:1536 + (h + 1) * 128], X[:, kc, 0:N]) for kc in range(8)],
               [t_Wp4[3], tX])
            cp("act", mqT[:, h, 0:N], psb[b][:, 0:N], [t_ps[b]], [t_mqT])
        if not halo:
            def ret_core(h):
                c, r0 = h // 2, (h % 2) * 64
                rr = slice(r0, r0 + 64)
                mm(3, psb[3][:, :], [(snapS[0:64, k * 4 + h, :], rqd4[0:64, h, :])], [t_snap, t_rqd4])
                for kk in range(4):
                    b = 4 + (kk % 2)
                    mm(b, psb[b][:, :], [(rkT[rr, c, kk * 128:(kk + 1) * 128], rqT[rr, c, :])], [t_rkT, t_rqT])
                    tt(Sd[kk % 2], psb[b][:, :], Tt[:, h, 384 - 128 * kk:896 - 128 * kk], ALU.mult,
                       [t_ps[b], t_tab2], [t_Sd[kk % 2]])
                    mm(0, psb[0][:, :], [(rv[:, kk, h * 128:(h + 1) * 128], Sd[kk % 2])], [t_rv, t_Sd[kk % 2]],
                       start=(kk == 0), stop=(kk == 3))
                cp("act", rf, psb[0][:, :], [t_ps[0]], [t_rf])
                tt(rf, rf, psb[3][:, :], ALU.add, [t_rf, t_ps[3]], [t_rf])

            def ret_finish(h):
                group_norm_gate(512, rf, h)
                stt(rw[:, h, :], cen, col(C_RG, h), sg[:, h, :], ALU.mult, ALU.mult, [t_cen, t_cst, t_sg], [t_rw])
            for h in range(4):
                ret_core(h)
                if h > 0:
                    mem_finish(512, h - 1)
                mem_core(512, h)
                ret_finish(h)
            mem_finish(512, 3)
        else:
            items = []
            for h in range(4):
                for kq4 in range(4):
                    kq = min(kq4, NSLOT - 1)
                    for r in range(2):
                        e = 2 * kq4 + r
                        snap = snapS if r == 1 else snapS2
                        items.append((psb[3][:, h * 8 + e:h * 8 + e + 1], snap[0:64, kq * 4 + h, :],
                                      rqd4[0:64, h, e:e + 1], True, True))
            mml([3], items, [t_snap, t_rqd4])
            cp("act", rf[:, 0:32], psb[3][:, 0:32], [t_ps[3]], [t_rf])
            group_norm_gate(32, rf[:, 0:32], None)
            for h in range(4):
                stt(rw[:, h, 0:8], cen[:, h * 8:(h + 1) * 8], col(C_RG, h), sg[:, h, 0:8], ALU.mult, ALU.mult,
                    [t_cen, t_cst, t_sg], [t_rw])
        if halo:
            for h in range(4):
                mem_attn(N, h)
        P.dma("sp", rT_d[:, :, c0:c0 + N], rw[:, :, 0:N], reads=[t_rw], writes=[t_rTd])
        P.dma("sp", mT_d[:, :, c0:c0 + N], mw[:, :, 0:N], reads=[t_mw], writes=[t_mTd])

    while prefetch3:
        prefetch3.pop(0)()
    if upto < 4:
        P.stop = True
    P.barrier()
    a3 = Alloc(PBASE)
    xb3 = [a3.b16(8, 512) for _ in range(2)]
    xf = a3.f32(8, 512)
    br = a3.b16(12, 512)
    sig = [a3.f32(512) for _ in range(2)]
    acc = a3.f32(512)
    tmpm = a3.f32(512)
    mg = a3.b16(8, 512)
    y = a3.f32(8, 512)
    lnm = a3.f32(512)
    lnv = a3.f32(512)
    lnr = a3.f32(512)
    lnq = [a3.b16(512) for _ in range(2)]
    ybf = [a3.b16(512) for _ in range(2)]
    lnt = [a3.f32(512) for _ in range(2)]
    t_xb3 = [Tk("xb30"), Tk("xb31")]
    t_xf, t_br = Tk("xf"), Tk("br")
    t_sig = [Tk("sig0"), Tk("sig1")]
    t_acc, t_tmpm, t_mg, t_y = Tk("acc"), Tk("tmpm"), Tk("mg"), Tk("y")
    t_lnm, t_lnv, t_lnr = Tk("lnm"), Tk("lnv"), Tk("lnr")
    t_lnq = [Tk("lnq0"), Tk("lnq1")]
    t_ybf = [Tk("ybf0"), Tk("ybf1")]
    t_lnt = [Tk("lnt0"), Tk("lnt1")]
    assert a3.off <= TOP3, f"P3b buffers {a3.off} overlap weights {TOP3}"

    def layer_norm_steps(N, src, t_src, gofs, bofs, dst_fn, t_dst_list):
        steps = []

        def stat(c):
            q = c % 2
            cp("dve" if c % 2 == 0 else "act", ybf[q][:, 0:N], src[:, c, 0:N], [t_src], [t_ybf[q]])
            mm(0, psb[0][:, 0:N], [(onesb1024, ybf[q][:, 0:N])], [t_misc, t_ybf[q]], start=(c == 0), stop=(c == 7))
            act(lnq[q][:, 0:N], src[:, c, 0:N], AF.Square, [t_src], [t_lnq[q]])
            mm(1, psb[1][:, 0:N], [(onesb1024, lnq[q][:, 0:N])], [t_misc, t_lnq[q]], start=(c == 0), stop=(c == 7))

        def mid():
            cp("act", lnm[:, 0:N], psb[0][:, 0:N], [t_ps[0]], [t_lnm])
            tt(lnv[:, 0:N], lnm[:, 0:N], lnm[:, 0:N], ALU.mult, [t_lnm], [t_lnv])
            tt(lnv[:, 0:N], psb[1][:, 0:N], lnv[:, 0:N], ALU.subtract, [t_ps[1], t_lnv], [t_lnv])
            act(lnr[:, 0:N], lnv[:, 0:N], AF.Ln, [t_lnv, t_misc], [t_lnr], bias=epsc[:, 0:1], scale=1.0)
            act(lnr[:, 0:N], lnr[:, 0:N], AF.Exp, [t_lnr], [t_lnr], scale=-0.5)

        def norm(c):
            q = c % 2
            tt(lnt[q][:, 0:N], src[:, c, 0:N], lnm[:, 0:N], ALU.subtract, [t_src, t_lnm], [t_lnt[q]])
            tt(lnt[q][:, 0:N], lnt[q][:, 0:N], lnr[:, 0:N], ALU.mult, [t_lnt[q], t_lnr], [t_lnt[q]])
            act(dst_fn(c), lnt[q][:, 0:N], AF.Identity, [t_lnt[q], t_cst], t_dst_list, bias=col(bofs, c),
                scale=col(gofs, c))
        for c in range(8):
            steps.append(lambda c=c: stat(c))
        steps.append(mid)
        for c in range(8):
            steps.append(lambda c=c: norm(c))
        return steps

    def layer_norm(N, src, t_src, gofs, bofs, dst_fn, t_dst_list):
        for st in layer_norm_steps(N, src, t_src, gofs, bofs, dst_fn, t_dst_list):
            st()

    pending3 = []
    for k in range(NSLOT + 1):
        halo = (k == NSLOT)
        N = 8 if halo else 512
        c0 = NQ if halo else k * 512
        ws = slice(c0, c0 + N)
        xi = k % 2
        X, tX = xb3[xi], t_xb3[xi]

        def loads_a(kk):
            hal = (kk == NSLOT)
            n_ = 8 if hal else 512
            c_ = NQ if hal else kk * 512
            P.dma("pool", xb3[kk % 2][:, :, 0:n_], xq[:, :, c_:c_ + n_], writes=[t_xb3[kk % 2]])
            P.dma("sp", br[:, :, 0:n_], br_d[:, :, c_:c_ + n_], reads=[t_aTd, t_rTd, t_mTd], writes=[t_br], key="br")

        def loads_b(kk):
            hal = (kk == NSLOT)
            n_ = 8 if hal else 512
            c_ = NQ if hal else kk * 512
            P.dma("sp", xf[:, :, 0:n_], xq[:, :, c_:c_ + n_], writes=[t_xf])
        if k == 0:
            loads_a(0)
            loads_b(0)
        for c in range(8):
            for j in range(3):
                if (c, j) >= (1, 0) and pending3:
                    pending3.pop(0)()
                q = (c * 3 + j) % 2
                bG = 4 + q
                bB = 6 + q
                mm(bG, psb[bG][:, 0:N],
                   [(Wg[:, kc, j * 1024 + c * 128:j * 1024 + (c + 1) * 128], X[:, kc, 0:N]) for kc in range(8)],
                   [t_Wg2[c // 4], tX])
                act(sig[q][:, 0:N], psb[bG][:, 0:N], AF.Sigmoid, [t_ps[bG], t_cst], [t_sig[q]],
                    bias=col(C_BG, j * 8 + c), scale=1.0)
                mm(bB, psb[bB][:, 0:N],
                   [(Wbr[:, j * 4 + kc, c * 128:(c + 1) * 128], br[:, j * 4 + kc, 0:N]) for kc in range(4)],
                   [t_Wbr, t_br])
                if j == 0:
                    tt(acc[:, 0:N], sig[q][:, 0:N], psb[bB][:, 0:N], ALU.mult, [t_sig[q], t_ps[bB]], [t_acc])
                else:
                    tt(tmpm[:, 0:N], sig[q][:, 0:N], psb[bB][:, 0:N], ALU.mult, [t_sig[q], t_ps[bB]], [t_tmpm])
                    if j == 1:
                        tt(acc[:, 0:N], acc[:, 0:N], tmpm[:, 0:N], ALU.add, [t_acc, t_tmpm], [t_acc])
                    else:
                        tt(mg[:, c, 0:N], acc[:, 0:N], tmpm[:, 0:N], ALU.add, [t_acc, t_tmpm], [t_mg])
        if k + 1 <= NSLOT:
            loads_a(k + 1)
        for c in range(8):
            b = 2 + (c % 2)
            mm(b, psb[b][:, 0:N], [(Wo[:, kc, c * 128:(c + 1) * 128], mg[:, kc, 0:N]) for kc in range(8)],
               [t_Wo, t_mg])
            stt(y[:, c, 0:N], xf[:, c, 0:N], float(ALPHA), psb[b][:, 0:N], ALU.mult, ALU.add, [t_xf, t_ps[b]], [t_y])
        if k + 1 <= NSLOT:
            loads_b(k + 1)
        pending3 += layer_norm_steps(N, y, t_y, C_L1G, C_L1B, lambda c, N=N: y[:, c, 0:N], [t_y])

        def store(N=N, ws=ws, halo=halo):
            if halo:
                cp("dve", h1Hb, y[:, :, 0:8], [t_y], [t_h1H])
            P.dma("sp", h1_d[:, :, ws], y[:, :, 0:N], reads=[t_y], writes=[t_h1d])
        pending3.append(store)
    while pending3:
        pending3.pop(0)()

    if upto < 5:
        P.stop = True
    P.barrier()
    a4 = Alloc(PBASE)
    Wu = a4.b16(8, 5632)
    Wd = a4.b16(22, 1024)
    hb = [a4.b16(8, 512) for _ in range(2)]
    hf = a4.f32(8, 512)
    adead = Alloc(o_snap)
    tg = [a4.f32(512), adead.f32(512)]
    tv = [a4.f32(512), adead.f32(512)]
    bnd = adead.f32(44, 2)
    btmp = adead.f32(44)
    assert adead.off <= o_snap + 2 * max(NSLOT, 4) * 4 * 128 * 2, "P4 scratch overflows the dead snapshot area"
    t_bnd = Tk("bnd")
    o_actb = a4.off
    actb = a4.b16(22, 512)
    a4b = Alloc(o_actb)
    lnm = a4b.f32(512)
    lnv = a4b.f32(512)
    lnr = a4b.f32(512)
    lnq = [a4b.b16(512) for _ in range(2)]
    ybf = [a4b.b16(512) for _ in range(2)]
    lnt = [tg[0], tv[0]]
    t_Wu4 = [Tk(f"Wu{i}") for i in range(4)]
    t_Wd = Tk("Wd")
    t_hb = [Tk("hb0"), Tk("hb1")]
    t_hf = Tk("hf")
    t_tg = [Tk("tg0"), Tk("tg1")]
    t_tv = [Tk("tv0"), Tk("tv1")]
    t_actb = Tk("actb")
    t_lnm, t_lnv, t_lnr = Tk("lnm4"), Tk("lnv4"), Tk("lnr4")
    t_lnq = [Tk("lnq40"), Tk("lnq41")]
    t_ybf = [Tk("ybf40"), Tk("ybf41")]
    t_lnt = [t_tg[0], t_tv[0]]
    alias4 = [t_lnm, t_lnv, t_lnr] + t_lnq + t_ybf

    def inherit(dst, src):
        dst.w = src.w
        dst.r = dict(src.r)

    def absorb(dst, srcs):
        for t in srcs:
            for tok in ([t.w] if t.w else []) + list(t.r.values()):
                kk = id(tok[0])
                if kk not in dst.r or dst.r[kk][1] < tok[1]:
                    dst.r[kk] = tok
    for i in range(4):
        P.dma("sp", Wu[:, :, i * 1408:(i + 1) * 1408], wu_b[:, :, i * 1408:(i + 1) * 1408], reads=[t_wub],
              writes=[t_Wu4[i]])
    P.dma("sp", Wd, wd_b, reads=[t_wdb], writes=[t_Wd])
    for ch in range(44):
        b = 4 + (ch % 4)
        mm(b, psb[b][:, 0:8], [(Wu[:, kc, ch * 128:(ch + 1) * 128], h1Hb[:, kc, :]) for kc in range(8)],
           [t_Wu4[ch // 11], t_h1H])
        tt(uH[:, ch, :], psb[b][:, 0:8], pc[:, PC_HZ:PC_HZ + 8], ALU.mult, [t_ps[b], t_cst], [t_uH])
    P.dma("pool", hb[0], h1_d[:, :, 0:512], reads=[t_h1d], writes=[t_hb[0]])
    for k in range(NSLOT):
        xi = k % 2
        H, tH = hb[xi], t_hb[xi]
        ws = slice(k * 512, (k + 1) * 512)
        if k + 1 < NSLOT:
            P.dma("pool", hb[1 - xi], h1_d[:, :, (k + 1) * 512:(k + 2) * 512], reads=[t_h1d], writes=[t_hb[1 - xi]])
        P.dma("sp", hf, h1_d[:, :, ws], reads=[t_h1d], writes=[t_hf])
        w0v, w1v = cst[:, C_CW:C_CW + 44], cst[:, C_CW + 44:C_CW + 88]
        tt(bnd[:, :, 1], uH[:, :, 2 * k + 1], w0v, ALU.mult, [t_uH, t_cst], [t_bnd])
        tt(bnd[:, :, 0], uH[:, :, 2 * k], w0v, ALU.mult, [t_uH, t_cst], [t_bnd])
        tt(btmp, uH[:, :, 2 * k + 1], w1v, ALU.mult, [t_uH, t_cst], [t_bnd])
        tt(bnd[:, :, 0], bnd[:, :, 0], btmp, ALU.add, [t_bnd], [t_bnd])
        for c in range(22):
            ci = c % 2
            for br_i, (ch, T, tT) in enumerate(((c, tg[ci], t_tg[ci]), (22 + c, tv[ci], t_tv[ci]))):
                b = 4 + ((2 * c + br_i) % 4)
                w0, w1, w2 = (col(C_CW, t * 44 + ch) for t in range(3))
                mm(b, psb[b][:, :], [(Wu[:, kc, ch * 128:(ch + 1) * 128], H[:, kc, :]) for kc in range(8)],
                   [t_Wu4[ch // 11], tH])
                act(T, psb[b][:, :], AF.Identity, [t_ps[b], t_cst], [tT], bias=col(C_CB, ch), scale=w2)
                stt(T[:, 2:512], psb[b][:, 0:510], w0, T[:, 2:512], ALU.mult, ALU.add, [t_ps[b], t_cst, tT], [tT])
                stt(T[:, 1:512], psb[b][:, 0:511], w1, T[:, 1:512], ALU.mult, ALU.add, [t_ps[b], t_cst, tT], [tT])
                tt(T[:, 0:2], T[:, 0:2], bnd[:, ch, :], ALU.add, [tT, t_bnd], [tT])
            act(tg[ci], tg[ci], AF.Silu, [t_tg[ci]], [t_tg[ci]])
            tt(actb[:, c, :], tv[ci], tg[ci], ALU.mult, [t_tv[ci], t_tg[ci]], [t_actb], eng="pool")
        for c in range(8):
            b = 2 + (c % 2)
            mm(b, psb[b][:, :], [(Wd[:, kc, c * 128:(c + 1) * 128], actb[:, kc, :]) for kc in range(22)],
               [t_Wd, t_actb])
            stt(hf[:, c, :], hf[:, c, :], float(ALPHA), psb[b][:, :], ALU.mult, ALU.add, [t_hf, t_ps[b]], [t_hf])
        for t in alias4:
            inherit(t, t_actb)
        layer_norm(512, hf, t_hf, C_L2G, C_L2B, lambda c: hf[:, c, :], [t_hf])
        absorb(t_actb, alias4)
        P.dma("sp", outT[:, :, ws], hf, reads=[t_hf], writes=[t_out])
    P.final_wait("sp", [t_out, t_aTd, t_rTd, t_mTd, t_h1d])

    with nc.Block() as block:
        @block.tensor
        def _(e):
            for f in P.ops["pe"]:
                f(e)

        @block.scalar
        def _(e):
            for f in P.ops["act"]:
                f(e)

        @block.vector
        def _(e):
            for f in P.ops["dve"]:
                f(e)

        @block.gpsimd
        def _(e):
            for f in P.ops["pool"]:
                f(e)

        @block.sync
        def _(e):
            for f in P.ops["sp"]:
                f(e)
    es.close()
    return nc


def prep_shared(inp):
    f = np.float32
    d = {}
    d["w_in"] = fm(np.asarray(inp["w_in"], f)[0], 8)
    d["w_mkv"] = fm(np.asarray(inp["w_mem_kv"], f)[0], 8)
    d["w_br"] = np.ascontiguousarray(np.concatenate(
        [fm(np.asarray(inp[k], f)[0], 4) for k in ("w_diff_o", "w_ret_o", "w_mem_o")], axis=1))
    d["w_gate"] = fm(np.asarray(inp["w_gate"], f)[0], 8)
    d["w_out"] = fm(np.asarray(inp["w_mix_out"], f)[0], 8)
    d["w_up"] = fm(np.asarray(inp["w_up"], f)[0], 8)
    d["w_dn"] = fm(np.asarray(inp["w_down"], f)[0], 22)
    cs = np.zeros((128, NCONST), f)
    cs[:, C_BG:C_BG + 24] = vec_fm(np.asarray(inp["b_gate"], f)[0], 24)
    cs[:, C_L1G:C_L1G + 8] = vec_fm(np.asarray(inp["ln1_g"], f)[0], 8)
    cs[:, C_L1B:C_L1B + 8] = vec_fm(np.asarray(inp["ln1_b"], f)[0], 8)
    cs[:, C_L2G:C_L2G + 8] = vec_fm(np.asarray(inp["ln2_g"], f)[0], 8)
    cs[:, C_L2B:C_L2B + 8] = vec_fm(np.asarray(inp["ln2_b"], f)[0], 8)
    cw = np.asarray(inp["conv_w"], f)[0]
    for t in range(3):
        cs[:, C_CW + t * 44:C_CW + (t + 1) * 44] = vec_fm(cw[t], 44)
    cs[:, C_CB:C_CB + 44] = vec_fm(np.asarray(inp["conv_b"], f)[0], 44)
    cs[:, C_SG:C_SG + 4] = vec_fm(np.asarray(inp["diff_subln_g"], f)[0], 4)
    cs[:, C_RG:C_RG + 4] = vec_fm(np.asarray(inp["ret_norm_g"], f)[0], 4)
    d["consts"] = cs
    d["dl"] = np.ascontiguousarray(np.broadcast_to(np.asarray(inp["diff_lambda"], f)[0].reshape(1, 256), (128, 256)))
    d["Ctab"], d["Ttab"], d["Gq"], d["gk"] = shared_tables()
    gkT = np.zeros((128, 4, 2, 4, 64), np.float32)
    for tb in range(4):
        for v in range(2):
            for h in range(4):
                gkT[:, tb, v, h, :] = d["gk"][:, tb * 8 + v * 4 + h][:, None]
    d["gkT"] = gkT.reshape(128, 2048)
    d["ident"] = np.eye(128, dtype=np.float32)
    return d


def prep_core(inp, shared, b, j, NSB):
    NSLOT = NSB // 4
    S = NSB * 512
    f = np.float32
    x = np.asarray(inp["x"], f)[b, :S]
    d = dict(shared)
    xT = np.ascontiguousarray(x.T)
    d["xT"] = fm(xT, 8)
    cols = []
    for k in range(NSLOT):
        sb = 4 * k + j
        cols.append(np.arange(sb * 512, (sb + 1) * 512))
    hal = []
    for k in range(4):
        sb = 4 * k + j
        if k < NSLOT and sb > 0:
            hal += [sb * 512 - 2, sb * 512 - 1]
        else:
            hal += [0, 1]
    idx = np.concatenate(cols + [np.asarray(hal)])
    d["xq"] = fm(np.ascontiguousarray(xT[:, idx]), 8)
    d["memT"] = fm(np.ascontiguousarray(np.asarray(inp["mem"], f)[b].T), 8)
    d["kbW"], d["Bh"], d["pc"] = core_tables(j, NSB)
    return d


_NC_CACHE = {}


def run(inp, NSB=16, debug=False, upto=9):
    key = (NSB, debug, upto)
    if key not in _NC_CACHE:
        _NC_CACHE[key] = build(NSB, debug, upto)
    nc = _NC_CACHE[key]
    shared = prep_shared(inp)
    in_maps = [prep_core(inp, shared, c // 4, c % 4, NSB) for c in range(NCORES)]
    res = run_bass_kernel_spmd(nc, in_maps, core_ids=list(range(NCORES)))
    return res.results


def kernel(**inputs):
    NSB = 16
    NSLOT = NSB // 4
    S = NSB * 512
    r = run(inputs, NSB)
    out = np.empty((2, S, 1024), np.float32)
    for c in range(NCORES):
        b, j = c // 4, c % 4
        oT = np.asarray(r[c]["outT"])
        o = oT.transpose(2, 1, 0).reshape(NSLOT * 512, 1024)
        for k in range(NSLOT):
            sb = 4 * k + j
            out[b, sb * 512:(sb + 1) * 512] = o[k * 512:(k + 1) * 512]
    return out
```

```python
import math
import numpy as np
import concourse.bass as bass
import concourse.mybir as mybir
from concourse.bass_utils import run_bass_kernel_spmd

F32 = mybir.dt.float32
BF16 = mybir.dt.bfloat16
AF = mybir.ActivationFunctionType
ALU = mybir.AluOpType

ALPHA = 2.0 ** 0.25
LAMBDA_INIT = 0.2
EPS = 1e-5
BIGC = 1.0e7
NCORES = 8
SLOPES = [0.25 ** (i + 1) for i in range(4)]
SEM_ROT = 12000


class Tk:
    __slots__ = ("name", "w", "r")

    def __init__(self, name):
        self.name = name
        self.w = None
        self.r = {}


class Prog:
    ENG = ("pe", "act", "dve", "pool", "sp")

    def __init__(self, nc, sem_pool):
        self.nc = nc
        self.sem_pool = sem_pool
        self.ops = {e: [] for e in self.ENG}
        self.cur = {}
        self.cnt = {}
        self.known = {e: {} for e in self.ENG}
        for e in ("pe", "act", "dve", "pool"):
            self.cur[e] = self.sem_pool.pop()
            self.cnt[e] = 0
        self.dma_sem = {}
        self.dma_cnt = {}
        self.stop = False

    def _deps(self, eng, reads, writes):
        need = {}

        def add(tok):
            if tok is None:
                return
            s, v = tok
            if id(s) not in need or need[id(s)][1] < v:
                need[id(s)] = (s, v)

        for t in reads:
            add(t.w)
        for t in writes:
            add(t.w)
            for tok in t.r.values():
                add(tok)
        for s, v in need.values():
            if eng == "pe" and s is self.cur.get("pe"):
                continue
            k = self.known[eng].get(id(s), 0)
            if k >= v:
                continue
            self.known[eng][id(s)] = v
            self.ops[eng].append(lambda e, s=s, v=v: e.wait_ge(s, v))

    def _mark(self, tok, reads, writes):
        for t in reads:
            t.r[id(tok[0])] = tok
        for t in writes:
            t.w = tok
            t.r = {}

    def op(self, eng, fn, reads=(), writes=()):
        if self.stop:
            return
        self._deps(eng, reads, writes)
        if self.cnt[eng] >= SEM_ROT:
            self.cur[eng] = self.sem_pool.pop()
            self.cnt[eng] = 0
        self.cnt[eng] += 1
        s = self.cur[eng]
        v = self.cnt[eng]
        self.ops[eng].append(lambda e, s=s: fn(e).then_inc(s, 1))
        self._mark((s, v), reads, writes)

    def dma(self, q, out, in_, reads=(), writes=(), key=None):
        if self.stop:
            return
        self._deps(q, reads, writes)
        key = key if key is not None else (writes[0].name if writes else reads[0].name)
        if key not in self.dma_sem:
            self.dma_sem[key] = self.sem_pool.pop()
            self.dma_cnt[key] = 0
        self.dma_cnt[key] += 16
        s = self.dma_sem[key]
        v = self.dma_cnt[key]
        self.ops[q].append(lambda e, s=s: e.dma_start(out=out, in_=in_).then_inc(s, 16))
        self._mark((s, v), reads, writes)

    def barrier(self):
        if self.stop:
            return
        toks = [(self.cur[e], self.cnt[e]) for e in ("pe", "act", "dve", "pool") if self.cnt[e] > 0]
        toks += [(self.dma_sem[k], self.dma_cnt[k]) for k in self.dma_sem]
        for eng in self.ENG:
            for s, v in toks:
                if eng == "pe" and s is self.cur.get("pe"):
                    continue
                if self.known[eng].get(id(s), 0) >= v:
                    continue
                self.known[eng][id(s)] = v
                self.ops[eng].append(lambda e, s=s, v=v: e.wait_ge(s, v))

    def final_wait(self, eng, tiles):
        self._deps(eng, tiles, [])


C_BG, C_L1G, C_L1B, C_L2G, C_L2B, C_CW, C_CB, C_SG, C_RG = 0, 24, 32, 40, 48, 56, 188, 232, 236
NCONST = 240
NEGM = -30000.0


def _gammas():
    return [1.0 - 2.0 ** (-5.0 - h) for h in range(4)]


def shared_tables():
    C = np.zeros((128, 4, 512))
    n = np.arange(512)
    for k in range(4):
        s = 128 * k + np.arange(128)
        cs = (s // 64)[:, None]
        cn = (n // 64)[None, :]
        diff = (s[:, None] - n[None, :]).astype(np.float64)
        C[:, k, :] = np.where(cs > cn, -BIGC, np.where((cs == cn) & (diff > 0), -16.0 * diff, 0.0))
    lg = [math.log(x) for x in _gammas()]
    Tt = np.zeros((128, 4, 896))
    c = np.arange(896)
    for h in range(4):
        j = c[None, :] - 384 - np.arange(128)[:, None]
        Tt[:, h, :] = np.where(j >= 0, np.exp(lg[h] * np.maximum(j, 0)) / 8.0, 0.0)
    Gq = np.zeros((128, 4, 512))
    for h in range(4):
        Gq[:, h, :] = np.exp(lg[h] * (n + 1.0))[None, :]
    gk = np.zeros((128, 32))
    for tb in range(4):
        m = 128 * tb + np.arange(128)
        for h in range(4):
            gk[:, tb * 8 + h] = np.exp(lg[h] * (511.0 - m)) / 8.0
            gk[:, tb * 8 + 4 + h] = np.where(m <= 510, np.exp(lg[h] * np.maximum(510.0 - m, 0.0)) / 8.0, 0.0)
    return C.astype(np.float32), Tt.astype(np.float32), Gq.astype(np.float32), gk.astype(np.float32)


def core_tables(j, NSB):
    NSLOT = NSB // 4
    NB = NSB * 4
    p = np.arange(128, dtype=np.float64)
    kbW = np.zeros((128, 4, NSLOT, NB))
    acol = np.zeros((128, 4, NSLOT, 4))
    for k in range(NSLOT):
        sb = 4 * k + j
        for blk in range(NB):
            i = blk // 4
            for h in range(4):
                if i > sb:
                    kbW[:, h, k, blk] = NEGM
                else:
                    kbW[:, h, k, blk] = SLOPES[h] * (128.0 * blk + p - 512.0 * sb - 256.0)
        for ii in range(4):
            for h in range(4):
                acol[:, h, k, ii] = SLOPES[h] if (4 * k + ii == sb) else 0.0
    Bh = np.zeros((128, NB, 8))
    hz = np.zeros((128, 8))
    for k in range(NSLOT):
        sb = 4 * k + j
        t0 = 512 * sb
        if sb == 0:
            continue
        for r in range(2):
            tq = t0 - 2 + r
            e = 2 * k + r
            hz[:, e] = 1.0
            for blk in range(NB):
                s = 128.0 * blk + p
                Bh[:, blk, e] = np.where(s < t0, -8.0 * np.abs(tq - s), -BIGC)
    sel = np.zeros((128, 16))
    for k in range(NSLOT):
        sel[:, 4 * k + j] = 1.0
    pc = np.zeros((128, 64 + 16 + 8), np.float32)
    for h in range(4):
        for k in range(NSLOT):
            for ii in range(4):
                pc[:, h * 16 + k * 4 + ii] = acol[:, h, k, ii]
    pc[:, 64:80] = sel
    pc[:, 80:88] = hz
    return kbW.reshape(128, -1).astype(np.float32), Bh.astype(np.float32), pc


PC_ACOL, PC_SEL, PC_HZ = 0, 64, 80
NPC = 88


def fm(w, kc):
    return np.ascontiguousarray(w.reshape(kc, 128, -1).transpose(1, 0, 2))


def vec_fm(v, nch):
    return np.ascontiguousarray(v.reshape(nch, 128).T)


def build(NSB=16, debug=False, upto=9):
    NSLOT = NSB // 4
    S = NSB * 512
    NB = NSB * 4
    NQ = NSLOT * 512
    NQT = NQ + 8
    nc = bass.Bass("TRN2", target_bir_lowering=False)

    def din(name, shape, dt=F32):
        return nc.dram_tensor(name, list(shape), dt, kind="ExternalInput").ap()

    xT = din("xT", [128, 8, S])
    xq = din("xq", [128, 8, NQT])
    memT = din("memT", [128, 8, 256])
    w_in = din("w_in", [128, 8, 3584])
    w_mkv = din("w_mkv", [128, 8, 1024])
    w_br = din("w_br", [128, 12, 1024])
    w_gate = din("w_gate", [128, 8, 3072])
    w_out = din("w_out", [128, 8, 1024])
    w_up = din("w_up", [128, 8, 5632])
    w_dn = din("w_dn", [128, 22, 1024])
    consts_d = din("consts", [128, NCONST])
    dl_d = din("dl", [128, 256])
    kbW_d = din("kbW", [128, 4 * NSLOT * NB])
    Bh_d = din("Bh", [128, NB, 8])
    pc_d = din("pc", [128, NPC])
    gk_d = din("gk", [128, 32])
    gkT_d = din("gkT", [128, 2048])
    C_d = din("Ctab", [128, 4, 512])
    I_d = din("ident", [128, 128])
    Tt_d = din("Ttab", [128, 4, 896])
    Gq_d = din("Gq", [128, 4, 512])
    outT = nc.dram_tensor("outT", [128, 8, NQ], F32, kind="ExternalOutput").ap()
    skind = "ExternalOutput" if debug else "Internal"
    br_d = nc.dram_tensor("br_d", [128, 12, NQT], BF16, kind=skind).ap()
    aT_d, rT_d, mT_d = br_d[:, 0:4, :], br_d[:, 4:8, :], br_d[:, 8:12, :]
    h1_d = nc.dram_tensor("h1_d", [128, 8, NQT], F32, kind=skind).ap()
    wu_b = nc.dram_tensor("wu_b", [128, 8, 5632], BF16, kind="Internal").ap()
    wd_b = nc.dram_tensor("wd_b", [128, 22, 1024], BF16, kind="Internal").ap()
    t_wub, t_wdb = Tk("wub"), Tk("wdb")

    from contextlib import ExitStack
    es = ExitStack()
    big = es.enter_context(nc.sbuf_tensor("big", [128, 53200], F32))
    TOTAL = 53200 * 4
    bigb = big.bitcast(BF16)
    pspair = [es.enter_context(nc.psum_tensor(f"psp{i}", [128, 1024], F32)) for i in range(4)]
    psb = [pspair[i // 2][:, (i % 2) * 512:(i % 2 + 1) * 512] for i in range(8)]
    sem_pool = [es.enter_context(nc.semaphore(f"s{i}")) for i in range(72)]
    P = Prog(nc, sem_pool)

    class Alloc:
        def __init__(self, base):
            self.off = base

        def f32(self, *shape):
            n = int(np.prod(shape))
            o = (self.off + 3) // 4
            self.off = (o + n) * 4
            assert self.off <= TOTAL, f"SBUF overflow {self.off}"
            return self._shape(big[:, o:o + n], shape)

        def b16(self, *shape):
            n = int(np.prod(shape))
            o = (self.off + 3) // 4 * 2
            self.off = ((o + n) * 2 + 3) // 4 * 4
            assert self.off <= TOTAL, f"SBUF overflow {self.off}"
            return self._shape(bigb[:, o:o + n], shape)

        @staticmethod
        def _shape(v, shape):
            if len(shape) == 1:
                return v
            if len(shape) == 2:
                return v.rearrange("p (a b) -> p a b", b=shape[1])
            return v.rearrange("p (a b c) -> p a b c", b=shape[1], c=shape[2])

    pa = Alloc(0)
    cst = pa.f32(NCONST)
    pc = pa.f32(NPC)
    gk = pa.f32(32)
    dl = pa.f32(256)
    ones_bf = pa.b16(128)
    onesb128 = pa.b16(128)
    onesb1024 = pa.b16(128)
    epsc = pa.f32(1)
    neglam = pa.f32(1)
    lamtmp = pa.f32(128)
    lam2 = pa.f32(2)
    gs = pa.f32(4)
    mkT = pa.b16(4, 256)
    mv = pa.b16(2, 512)
    o_snap = pa.off
    snapS = pa.b16(max(NSLOT, 4) * 4, 128)
    snapS2 = pa.b16(max(NSLOT, 4) * 4, 128)
    h1Hb = pa.b16(8, 8)
    uH = pa.f32(44, 8)
    PBASE = pa.off

    t_cst, t_misc, t_mk, t_mv = Tk("cst"), Tk("misc"), Tk("mkT"), Tk("mv")
    t_snap, t_h1H, t_uH = Tk("snap"), Tk("h1H"), Tk("uH")
    t_ps = [Tk(f"ps{i}") for i in range(8)]

    def col(off, i=0):
        return cst[:, off + i:off + i + 1]

    def pcol(off, i=0):
        return pc[:, off + i:off + i + 1]

    def mm(ps_i, out_ap, pairs, reads, start=True, stop=True):
        n = len(pairs)

        def fn(e):
            ins = None
            for i, (l, r) in enumerate(pairs):
                ins = e.matmul(out_ap, l, r, start=(start and i == 0), stop=(stop and i == n - 1))
            return ins
        P.op("pe", fn, reads=reads, writes=[t_ps[ps_i]])

    def mml(ps_is, items, reads):
        def fn(e):
            ins = None
            for (o, l, r, st, sp_) in items:
                ins = e.matmul(o, l, r, start=st, stop=sp_)
            return ins
        P.op("pe", fn, reads=reads, writes=[t_ps[i] for i in ps_is])

    def act(out, in_, func, reads, writes, bias=None, scale=None):
        kw = {}
        if bias is not None:
            kw["bias"] = bias
        if scale is not None:
            kw["scale"] = scale
        P.op("act", lambda e: e.activation(out, in_, func, **kw), reads=reads, writes=writes)

    def tt(out, a, b, op, reads, writes, eng="dve"):
        P.op(eng, lambda e: e.tensor_tensor(out, a, b, op), reads=reads, writes=writes)

    def stt(out, a, sc, b, op0, op1, reads, writes):
        P.op("dve", lambda e: e.scalar_tensor_tensor(out, a, sc, b, op0, op1), reads=reads, writes=writes)

    def ts(out, a, s1, s2, op0, op1, reads, writes, eng="dve"):
        if op1 is None:
            P.op(eng, lambda e: e.tensor_scalar(out, a, s1, None, op0), reads=reads, writes=writes)
        else:
            P.op(eng, lambda e: e.tensor_scalar(out, a, s1, s2, op0, op1), reads=reads, writes=writes)

    def cp(eng, out, in_, reads, writes):
        if eng == "act":
            P.op("act", lambda e: e.copy(out, in_), reads=reads, writes=writes)
        else:
            P.op(eng, lambda e: e.tensor_copy(out, in_), reads=reads, writes=writes)

    def recip(out, in_, reads, writes):
        act(out, in_, AF.Ln, reads, writes)
        act(out, out, AF.Exp, writes, writes, scale=-1.0)

    def recip_exact(out, in_, reads, writes):
        P.op("dve", lambda e: e.reciprocal(out, in_), reads=reads, writes=writes)

    def rstd_from(ps_i, ps_ap, tmp, t_tmp, out, t_out):
        act(tmp, ps_ap, AF.Ln, [t_ps[ps_i], t_misc], [t_tmp], bias=epsc[:, 0:1], scale=1.0)
        act(out, tmp, AF.Exp, [t_tmp], [t_out], scale=-0.5)

    rot = [0]

    def nextbank():
        rot[0] = (rot[0] + 1) % 4
        return 4 + rot[0]

    P.dma("sp", cst, consts_d, writes=[t_cst])
    P.dma("sp", pc, pc_d, writes=[t_cst], key="cst")
    P.dma("sp", gk, gk_d, writes=[t_cst], key="cst")
    P.dma("sp", dl, dl_d, writes=[t_misc], key="miscld")
    P.op("pool", lambda e: e.memset(ones_bf, 1.0), writes=[t_misc])
    P.op("pool", lambda e: e.memset(onesb128, 1.0 / 128.0), writes=[t_misc])
    P.op("pool", lambda e: e.memset(onesb1024, 1.0 / 1024.0), writes=[t_misc])
    P.op("pool", lambda e: e.memset(epsc, EPS), writes=[t_misc])
    P.op("pool", lambda e: e.memset(snapS, 0.0), writes=[t_snap])
    P.op("pool", lambda e: e.memset(snapS2, 0.0), writes=[t_snap])
    dl4 = dl.rearrange("p (a b c) -> p a b c", b=2, c=64)
    lt = lamtmp.rearrange("p (a c) -> p a c", c=64)
    t_lam = Tk("lam")
    tt(lt, dl4[:, :, 0, :], dl4[:, :, 1, :], ALU.mult, [t_misc], [t_lam])
    P.op("dve", lambda e: e.tensor_reduce(lam2, lt, mybir.AxisListType.X, ALU.add), reads=[t_lam], writes=[t_lam])
    act(lam2, lam2, AF.Exp, [t_lam], [t_lam])
    tt(neglam, lam2[:, 1:2], lam2[:, 0:1], ALU.subtract, [t_lam], [t_lam])
    ts(neglam, neglam, -LAMBDA_INIT, None, ALU.add, None, [t_lam], [t_lam])
    ts(gs, cst[:, C_SG:C_SG + 4], 1.0 - LAMBDA_INIT, None, ALU.mult, None, [t_cst], [t_lam])

    a1 = Alloc(PBASE)
    KT = a1.b16(4, S)
    Vt = a1.b16(NB, 512)
    xb = [a1.b16(8, 512) for _ in range(2)]
    kbW = a1.f32(4 * NSLOT * NB)
    RB = a1.off
    t_KT = [[Tk(f"KT{h}_{s}") for s in range(NSB)] for h in range(4)]
    t_Vt = [Tk(f"Vt{b}") for b in range(NB)]
    t_xb = [Tk("xb0"), Tk("xb1")]
    t_tab = Tk("tab")
    P.dma("sp", kbW, kbW_d, writes=[t_tab], key="tab")

    r1 = Alloc(RB)
    W1 = r1.b16(8, 1792)
    rkd = [r1.b16(256) for _ in range(2)]
    rkd2 = [r1.b16(256) for _ in range(2)]
    rvt = [r1.b16(512) for _ in range(2)]
    Sf = r1.f32(4, 128)
    S2f = r1.f32(4, 128)
    GkT = r1.b16(8, 256)
    t_GkT = Tk("GkT")
    t_W1 = Tk("W1")
    t_rkd = [Tk("rkd0"), Tk("rkd1")]
    t_rvt = [Tk("rvt0"), Tk("rvt1")]
    t_Sf, t_S2f = Tk("Sf"), Tk("S2f")
    P.dma("pool", W1[:, :, 0:1024], w_in[:, :, 512:1536], writes=[t_W1], key="W1")
    P.dma("pool", W1[:, :, 1024:1792], w_in[:, :, 1792:2560], writes=[t_W1], key="W1")
    P.dma("pool", GkT, gkT_d, writes=[t_GkT])
    P.op("pool", lambda e: e.memset(Sf, 0.0), writes=[t_Sf])
    P.op("pool", lambda e: e.memset(S2f, 0.0), writes=[t_S2f])
    P.dma("pool", xb[0], xT[:, :, 0:512], writes=[t_xb[0]])
    a0 = Alloc(PBASE)
    memb = a0.b16(8, 256)
    Wm = a0.b16(8, 1024)
    t_Wm, t_memb = Tk("Wm"), Tk("memb")
    P.dma("pool", memb, memT, writes=[t_memb])
    P.dma("pool", Wm, w_mkv, writes=[t_Wm])
    for h in range(4):
        b = 4 + h
        mm(b, psb[b][:, 0:256], [(Wm[:, kc, h * 128:(h + 1) * 128], memb[:, kc, :]) for kc in range(8)],
           [t_Wm, t_memb])
        cp("act", mkT[:, h, :], psb[b][:, 0:256], [t_ps[b]], [t_mk])
    for blk in range(2):
        b = 4 + blk
        mm(b, psb[b][:, :], [(memb[:, kc, blk * 128:(blk + 1) * 128], Wm[:, kc, 512:1024]) for kc in range(8)],
           [t_Wm, t_memb])
        cp("dve", mv[:, blk, :], psb[b][:, :], [t_ps[b]], [t_mv])
    P.barrier()
    if upto < 1:
        P.stop = True

    G512 = [math.exp(math.log(g_) * 512.0) for g_ in _gammas()]
    G511 = [math.exp(math.log(g_) * 511.0) for g_ in _gammas()]
    for s in range(NSB):
        xi = s % 2
        X, tX = xb[xi], t_xb[xi]
        if s > 0:
            P.dma("pool", X, xT[:, :, s * 512:(s + 1) * 512], writes=[tX])
        for h in range(4):
            b = nextbank()
            mm(b, psb[b][:, :], [(W1[:, kc, h * 128:(h + 1) * 128], X[:, kc, :]) for kc in range(8)], [t_W1, tX])
            cp("act", KT[:, h, s * 512:(s + 1) * 512], psb[b][:, :], [t_ps[b]], [t_KT[h][s]])
        for tb in range(4):
            b = nextbank()
            mm(b, psb[b][:, :], [(X[:, kc, tb * 128:(tb + 1) * 128], W1[:, kc, 512:1024]) for kc in range(8)],
               [t_W1, tX])
            cp("dve", Vt[:, s * 4 + tb, :], psb[b][:, :], [t_ps[b]], [t_Vt[s * 4 + tb]])
        k = s // 4
        if k < NSLOT:
            stt(snapS[0:64, k * 4:(k + 1) * 4, :], Sf[0:64, :, :], pcol(PC_SEL, s)[0:64, :],
                snapS[0:64, k * 4:(k + 1) * 4, :], ALU.mult, ALU.add, [t_Sf, t_cst, t_snap], [t_snap])
            stt(snapS2[0:64, k * 4:(k + 1) * 4, :], S2f[0:64, :, :], pcol(PC_SEL, s)[0:64, :],
                snapS2[0:64, k * 4:(k + 1) * 4, :], ALU.mult, ALU.add, [t_S2f, t_cst, t_snap], [t_snap])
        if s < NSB - 1:
            for tb in range(4):
                ti = tb % 2
                b = nextbank()
                mm(b, psb[b][:, 0:256], [(X[:, kc, tb * 128:(tb + 1) * 128], W1[:, kc, 1024:1280]) for kc in range(8)],
                   [t_W1, tX])
                tt(rkd[ti], psb[b][:, 0:256], GkT[:, tb * 2, :], ALU.mult, [t_ps[b], t_GkT], [t_rkd[ti]])
                tt(rkd2[ti], psb[b][:, 0:256], GkT[:, tb * 2 + 1, :], ALU.mult, [t_ps[b], t_GkT], [t_rkd[ti]])
                b = nextbank()
                mm(b, psb[b][:, :], [(X[:, kc, tb * 128:(tb + 1) * 128], W1[:, kc, 1280:1792]) for kc in range(8)],
                   [t_W1, tX])
                cp("dve", rvt[ti], psb[b][:, :], [t_ps[b]], [t_rvt[ti]])
                items = []
                for h in range(4):
                    items.append((psb[0][0:64, h * 128:(h + 1) * 128], rkd[ti][:, h * 64:(h + 1) * 64],
                                  rvt[ti][:, h * 128:(h + 1) * 128], (tb == 0 and h == 0), (tb == 3 and h == 3)))
                for h in range(4):
                    items.append((psb[1][0:64, h * 128:(h + 1) * 128], rkd2[ti][:, h * 64:(h + 1) * 64],
                                  rvt[ti][:, h * 128:(h + 1) * 128], (tb == 0 and h == 0), (tb == 3 and h == 3)))
                mml([0, 1], items, [t_rkd[ti], t_rvt[ti]])
            for h in range(4):
                stt(S2f[0:64, h, :], Sf[0:64, h, :], float(G511[h]), psb[1][0:64, h * 128:(h + 1) * 128], ALU.mult,
                    ALU.add, [t_Sf, t_ps[1]], [t_S2f])
            for h in range(4):
                stt(Sf[0:64, h, :], Sf[0:64, h, :], float(G512[h]), psb[0][0:64, h * 128:(h + 1) * 128], ALU.mult,
                    ALU.add, [t_Sf, t_ps[0]], [t_Sf])
    P.barrier()

    if upto < 2:
        P.stop = True
    r2 = Alloc(RB)
    Wq = r2.b16(8, 512)
    QT = r2.b16(4, 512)
    QT1 = r2.b16(4, 512)
    PTp = [r2.b16(1024) for _ in range(2)]
    PT = [[PTp[jj][:, m * 512:(m + 1) * 512] for jj in range(2)] for m in range(2)]
    sqb = r2.b16(512)
    OLc = [r2.f32(512) for _ in range(4)]
    Bf = OLc[3]
    R1 = r2.f32(512)
    Af = r2.f32(512)
    aw = r2.b16(4, 512)
    o_cb = r2.off
    Cb = r2.b16(4, 512)
    BhT = Alloc(o_cb).f32(NB, 8)
    identb = r2.b16(128)
    selI = [r2.b16(128) for _ in range(2)]
    t_ctab, t_bh, t_selI = Tk("ctab"), Tk("bh"), [Tk("selI0"), Tk("selI1")]
    selcnt = [0]
    t_Wq, t_QT = Tk("Wq"), Tk("QT")
    t_PT = [[Tk(f"PT{m}{j}") for j in range(2)] for m in range(2)]
    t_sqb = Tk("sqb")
    t_OLc = [Tk(f"OLc{i}") for i in range(4)]
    t_Bf = t_OLc[3]
    t_R1, t_Af, t_aw = Tk("R1"), Tk("Af"), Tk("aw")
    t_aTd, t_rTd, t_mTd, t_h1d, t_out = Tk("aTd"), Tk("rTd"), Tk("mTd"), Tk("h1d"), Tk("outd")
    P.dma("pool", Wq, w_in[:, :, 0:512], writes=[t_Wq])
    P.op("pool", lambda e: e.memset(QT, 0.0), writes=[t_QT])
    P.op("pool", lambda e: e.memset(QT1, 0.0), writes=[t_QT])
    P.dma("pool", Cb, C_d, writes=[t_ctab], key="ctab")
    P.dma("pool", identb, I_d, writes=[t_ctab], key="ctab")

    def post_evac(N):
        for i in range(4):
            cp("act" if i % 2 == 0 else "dve", OLc[i][:, 0:N], psb[i][:, 0:N], [t_ps[i]], [t_OLc[i]])

    def post_a(N):
        rc = recip_exact if N == 512 else recip
        rc(R1[:, 0:N], OLc[2][:, 0:N], [t_OLc[2]], [t_R1])
        tt(Af[:, 0:N], OLc[0][:, 0:N], R1[:, 0:N], ALU.mult, [t_OLc[0], t_R1], [t_Af])
        rc(R1[:, 0:N], OLc[3][:, 0:N], [t_OLc[3]], [t_R1])
        tt(Bf[:, 0:N], OLc[1][:, 0:N], R1[:, 0:N], ALU.mult, [t_OLc[1], t_R1], [t_Bf])
        stt(Af[:, 0:N], Bf[:, 0:N], neglam[:, 0:1], Af[:, 0:N], ALU.mult, ALU.add, [t_Bf, t_Af, t_lam], [t_Af])
        tt(sqb[:, 0:N], Af[:, 0:N], Af[:, 0:N], ALU.mult, [t_Af], [t_sqb])

    def post_b(N, h):
        mm(4, psb[4][:, 0:N], [(onesb128, sqb[:, 0:N])], [t_misc, t_sqb])
        rstd_from(4, psb[4][:, 0:N], R1[:, 0:N], t_R1, R1[:, 0:N], t_R1)
        tt(Af[:, 0:N], Af[:, 0:N], R1[:, 0:N], ALU.mult, [t_Af, t_R1], [t_Af])
        ts(aw[:, h, 0:N], Af[:, 0:N], gs[:, h:h + 1], None, ALU.mult, None, [t_Af, t_lam], [t_aw])

    def diff_post(N, h):
        post_evac(N)
        post_a(N)
        post_b(N, h)

    pend_a, pend_b = [], []

    for k in range(NSLOT):
        xi = k % 2
        X, tX = xb[xi], t_xb[xi]
        P.dma("pool", X, xq[:, :, k * 512:(k + 1) * 512], writes=[tX])
        if k == 1:
            for i in range(4):
                P.dma("pool", wu_b[:, :, i * 1408:(i + 1) * 1408], w_up[:, :, i * 1408:(i + 1) * 1408],
                      writes=[t_wub], key="wub")
            P.dma("pool", wd_b, w_dn, writes=[t_wdb], key="wdb")
        for h in range(4):
            b = nextbank()
            mm(b, psb[b][:, :], [(Wq[:, kc, h * 128:(h + 1) * 128], X[:, kc, :]) for kc in range(8)], [t_Wq, tX])
            cp("act", QT[0:64, h, :], psb[b][0:64, :], [t_ps[b]], [t_QT])
            cp("dve", QT1[64:128, h, :], psb[b][64:128, :], [t_ps[b]], [t_QT])
        for h in range(4):
            blocks = [(i, kb) for i in range(4 * k + 4) for kb in range(4)]
            n = len(blocks)

            def qk(j):
                i, kb = blocks[j]
                blk = i * 4 + kb
                pA, pB = 4 + 2 * (j % 2), 5 + 2 * (j % 2)
                ks = slice(blk * 128, (blk + 1) * 128)
                if i >= 4 * k:
                    if kb == 0:
                        selcnt[0] += 1
                        q_ = selcnt[0] % 2
                        ts(selI[q_], identb, pcol(PC_ACOL, h * 16 + k * 4 + (i - 4 * k)), None, ALU.mult, None,
                           [t_ctab, t_cst], [t_selI[q_]])
                    q_ = selcnt[0] % 2
                    mml([pA, pB], [(psb[pA][:, :], KT[:, h, ks], QT[:, h, :], True, False),
                                   (psb[pA][:, :], selI[q_], Cb[:, kb, :], False, True),
                                   (psb[pB][:, :], KT[:, h, ks], QT1[:, h, :], True, False),
                                   (psb[pB][:, :], selI[q_], Cb[:, kb, :], False, True)],
                        [t_KT[h][i], t_QT, t_selI[q_], t_ctab])
                else:
                    mml([pA, pB], [(psb[pA][:, :], KT[:, h, ks], QT[:, h, :], True, True),
                                   (psb[pB][:, :], KT[:, h, ks], QT1[:, h, :], True, True)],
                        [t_KT[h][i], t_QT])

            def ex(j):
                i, kb = blocks[j]
                blk = i * 4 + kb
                cidx = (h * NSLOT + k) * NB + blk
                kcol = kbW[:, cidx:cidx + 1]
                pA = 4 + 2 * (j % 2)
                act(PTp[j % 2], pspair[pA // 2][:, :], AF.Exp, [t_ps[pA], t_ps[pA + 1], t_tab],
                    [t_PT[0][j % 2], t_PT[1][j % 2]], bias=kcol, scale=0.125)

            def pv(j):
                i, kb = blocks[j]
                blk = i * 4 + kb
                vs = Vt[:, blk, h * 128:(h + 1) * 128]
                st, sp_ = (j == 0), (j == n - 1)
                items = []
                for m in range(2):
                    items.append((psb[m][:, :], vs, PT[m][j % 2], st, sp_))
                    items.append((psb[2 + m][:, :], ones_bf, PT[m][j % 2], st, sp_))
                mml([0, 1, 2, 3], items, [t_Vt[blk], t_PT[0][j % 2], t_PT[1][j % 2], t_misc])

            qk(0)
            for j in range(n):
                if j + 1 < n:
                    qk(j + 1)
                if j == 2 and pend_a:
                    pend_a.pop(0)()
                ex(j)
                if j == 12 and pend_b:
                    pend_b.pop(0)()
                pv(j)
            post_evac(512)
            pend_a.append(lambda: post_a(512))

            def fin(h=h, k=k):
                post_b(512, h)
                if h == 3:
                    P.dma("sp", aT_d[:, :, k * 512:(k + 1) * 512], aw, reads=[t_aw], writes=[t_aTd])
            pend_b.append(fin)
    while pend_a:
        pend_a.pop(0)()
    while pend_b:
        pend_b.pop(0)()

    X, tX = xb[0], t_xb[0]
    P.dma("sp", BhT, Bh_d, writes=[t_ctab, t_bh])
    P.dma("pool", X[:, :, 0:8], xq[:, :, NQ:NQ + 8], writes=[tX])
    for h in range(4):
        b = nextbank()
        mm(b, psb[b][:, 0:8], [(Wq[:, kc, h * 128:(h + 1) * 128], X[:, kc, 0:8]) for kc in range(8)], [t_Wq, tX])
        cp("act", QT[0:64, h, 0:8], psb[b][0:64, 0:8], [t_ps[b]], [t_QT])
        cp("dve", QT1[64:128, h, 0:8], psb[b][64:128, 0:8], [t_ps[b]], [t_QT])
    NG = NB // 8
    for h in range(4):
        for g in range(NG):
            pA, pB = 4 + 2 * (g % 2), 5 + 2 * (g % 2)
            items = []
            rd = [t_QT]
            for bl in range(8):
                blk = g * 8 + bl
                ks = slice(blk * 128, (blk + 1) * 128)
                items.append((psb[pA][:, bl * 8:(bl + 1) * 8], KT[0:64, h, ks], QT[0:64, h, 0:8], True, True))
                items.append((psb[pB][:, bl * 8:(bl + 1) * 8], KT[64:128, h, ks], QT1[64:128, h, 0:8], True, True))
                if t_KT[h][blk // 4] not in rd:
                    rd.append(t_KT[h][blk // 4])
            mml([pA, pB], items, rd)
            bh = BhT[:, g * 8:(g + 1) * 8, :]
            for m in range(2):
                pi = pA + m
                p3 = psb[pi][:, 0:64].rearrange("p (a b) -> p a b", b=8)
                stt(p3, bh, float(SLOPES[h]), p3, ALU.mult, ALU.add, [t_bh, t_ps[pi]], [t_ps[pi]])
                act(PT[m][g % 2][:, 0:64], psb[pi][:, 0:64], AF.Exp, [t_ps[pi]], [t_PT[m][g % 2]], scale=0.125)
            items = []
            rd = [t_PT[0][g % 2], t_PT[1][g % 2], t_misc]
            for bl in range(8):
                blk = g * 8 + bl
                st, sp_ = (g == 0 and bl == 0), (g == NG - 1 and bl == 7)
                for m in range(2):
                    rhs = PT[m][g % 2][:, bl * 8:(bl + 1) * 8]
                    items.append((psb[m][:, 0:8], Vt[:, blk, h * 128:(h + 1) * 128], rhs, st, sp_))
                    items.append((psb[2 + m][:, 0:8], ones_bf, rhs, st, sp_))
                rd.append(t_Vt[blk])
            mml([0, 1, 2, 3], items, rd)
        diff_post(8, h)
    P.dma("sp", aT_d[:, :, NQ:NQ + 8], aw[:, :, 0:8], reads=[t_aw], writes=[t_aTd])

    if upto < 3:
        P.stop = True
    P.barrier()
    a2 = Alloc(PBASE)
    Wp = a2.b16(8, 2048)
    Tt = a2.f32(4, 896)
    Gq = a2.f32(4, 512)
    xb2 = [a2.b16(8, 512)] * 2
    rqT = a2.b16(2, 512)
    rqd4 = a2.b16(4, 512)
    rkT = a2.b16(2, 512)
    rv = a2.b16(4, 512)
    sg = a2.b16(4, 512)
    mqT = a2.b16(4, 512)
    rw = a2.b16(4, 512)
    mw = a2.b16(4, 512)
    Sd = [a2.b16(512) for _ in range(2)]
    PT2 = [a2.b16(512) for _ in range(2)]
    rf = a2.f32(512)
    cen = a2.f32(512)
    rs = a2.f32(512)
    rfb = a2.b16(512)
    sqb2 = a2.b16(512)
    t_Wp4 = [Tk(f"Wp{i}") for i in range(4)]
    t_tab2 = Tk("tab2")
    t_xb2 = [Tk("xb20")] * 2
    t_rqT, t_rqd4, t_rkT, t_rv, t_sg, t_mqT = (Tk(x) for x in ("rqT", "rqd4", "rkT", "rv", "sg", "mqT"))
    t_rw, t_mw = Tk("rw"), Tk("mw")
    t_Sd = [Tk("Sd0"), Tk("Sd1")]
    t_PT2 = [Tk("PT20"), Tk("PT21")]
    t_rf, t_cen, t_rs = Tk("rf"), Tk("cen"), Tk("rs")
    t_rfb, t_sqb2 = Tk("rfb"), Tk("sqb2")
    for i in range(4):
        P.dma("pool", Wp[:, :, i * 512:(i + 1) * 512], w_in[:, :, 1536 + i * 512:1536 + (i + 1) * 512], writes=[t_Wp4[i]])
    P.dma("sp", Tt, Tt_d, writes=[t_tab2], key="tab2")
    P.dma("sp", Gq, Gq_d, writes=[t_tab2], key="tab2")
    TOP3 = TOTAL - (8 * 3072 + 12 * 1024 + 8 * 1024) * 2
    assert a2.off <= TOP3, f"P3a region {a2.off} overlaps P3b weight prefetch area {TOP3}"
    aw3 = Alloc(TOP3)
    Wg = aw3.b16(8, 3072)
    Wbr = aw3.b16(12, 1024)
    Wo = aw3.b16(8, 1024)
    t_Wg2 = [Tk("Wg0"), Tk("Wg1")]
    t_Wbr, t_Wo = Tk("Wbr"), Tk("Wo")
    Wg4 = Wg.rearrange("p k (j c) -> p k j c", j=3)
    wg4 = w_gate.rearrange("p k (j c) -> p k j c", j=3)
    prefetch3 = [lambda i=i: P.dma("pool", Wg4[:, :, :, i * 512:(i + 1) * 512], wg4[:, :, :, i * 512:(i + 1) * 512],
                                   writes=[t_Wg2[i]]) for i in range(2)]
    prefetch3.append(lambda: P.dma("pool", Wbr, w_br, writes=[t_Wbr]))
    prefetch3.append(lambda: P.dma("pool", Wo, w_out, writes=[t_Wo]))

    def group_norm_gate(N, src, h_or_none):
        cp("act", rfb[:, 0:N], src, [t_rf], [t_rfb])
        mm(1, psb[1][:, 0:N], [(onesb128, rfb[:, 0:N])], [t_misc, t_rfb])
        tt(cen[:, 0:N], src, psb[1][:, 0:N], ALU.subtract, [t_rf, t_ps[1]], [t_cen])
        tt(sqb2[:, 0:N], cen[:, 0:N], cen[:, 0:N], ALU.mult, [t_cen], [t_sqb2])
        mm(1, psb[1][:, 0:N], [(onesb128, sqb2[:, 0:N])], [t_misc, t_sqb2])
        rstd_from(1, psb[1][:, 0:N], rs[:, 0:N], t_rs, rs[:, 0:N], t_rs)
        tt(cen[:, 0:N], cen[:, 0:N], rs[:, 0:N], ALU.mult, [t_cen, t_rs], [t_cen])

    def mem_core(N, h):
        for blk in range(2):
            b = 4 + blk
            mm(b, psb[b][:, 0:N], [(mkT[:, h, blk * 128:(blk + 1) * 128], mqT[:, h, 0:N])], [t_mk, t_mqT])
            act(PT2[blk][:, 0:N], psb[b][:, 0:N], AF.Exp, [t_ps[b]], [t_PT2[blk]], scale=1.0 / math.sqrt(128.0))
            mml([6, 7], [(psb[6][:, 0:N], mv[:, blk, h * 128:(h + 1) * 128], PT2[blk][:, 0:N], blk == 0, blk == 1),
                         (psb[7][:, 0:N], ones_bf, PT2[blk][:, 0:N], blk == 0, blk == 1)],
                [t_mv, t_PT2[blk], t_misc])

    def mem_finish(N, h):
        recip(rs[:, 0:N], psb[7][:, 0:N], [t_ps[7]], [t_rs])
        tt(mw[:, h, 0:N], psb[6][:, 0:N], rs[:, 0:N], ALU.mult, [t_ps[6], t_rs], [t_mw])

    def mem_attn(N, h):
        mem_core(N, h)
        mem_finish(N, h)

    for k in range(NSLOT + 1):
        halo = (k == NSLOT)
        N = 8 if halo else 512
        c0 = NQ if halo else k * 512
        xi = k % 2
        X, tX = xb2[xi], t_xb2[xi]
        P.dma("pool", X[:, :, 0:N], xq[:, :, c0:c0 + N], writes=[tX])
        for _ in range(2):
            if prefetch3:
                prefetch3.pop(0)()
        if not halo:
            for c in range(2):
                b = nextbank()
                mm(b, psb[b][:, :], [(Wp[:, kc, c * 128:(c + 1) * 128], X[:, kc, :]) for kc in range(8)], [t_Wp4[0], tX])
                cp("act", rqT[:, c, :], psb[b][:, :], [t_ps[b]], [t_rqT])
            for c in range(2):
                b = nextbank()
                mm(b, psb[b][:, :], [(Wp[:, kc, 256 + c * 128:256 + (c + 1) * 128], X[:, kc, :]) for kc in range(8)],
                   [t_Wp4[0], tX])
                cp("act", rkT[:, c, :], psb[b][:, :], [t_ps[b]], [t_rkT])
            for tb in range(4):
                b = nextbank()
                mm(b, psb[b][:, :], [(X[:, kc, tb * 128:(tb + 1) * 128], Wp[:, kc, 512:1024]) for kc in range(8)],
                   [t_Wp4[1], tX])
                cp("act", rv[:, tb, :], psb[b][:, :], [t_ps[b]], [t_rv])
        for h in range(4):
            b = nextbank()
            mm(b, psb[b][0:64, 0:N], [(Wp[:, kc, h * 64:(h + 1) * 64], X[:, kc, 0:N]) for kc in range(8)], [t_Wp4[0], tX])
            if halo:
                cp("dve", rqd4[0:64, h, 0:N], psb[b][0:64, 0:N], [t_ps[b]], [t_rqd4])
            else:
                tt(rqd4[0:64, h, :], psb[b][0:64, :], Gq[0:64, h, :], ALU.mult, [t_ps[b], t_tab2], [t_rqd4])
        for h in range(4):
            b = nextbank()
            mm(b, psb[b][:, 0:N], [(Wp[:, kc, 1024 + h * 128:1024 + (h + 1) * 128], X[:, kc, 0:N]) for kc in range(8)],
               [t_Wp4[2], tX])
            act(sg[:, h, 0:N], psb[b][:, 0:N], AF.Silu, [t_ps[b]], [t_sg])
        for h in range(4):
            b = nextbank()
            mm(b, psb[b][:, 0:N], [(Wp[:, kc, 1536 + h * 128:1536 + (h + 1) * 128], X[:, kc, 0:N]) for kc in range(8)],
               [t_Wp4[3], tX])
            cp("act", mqT[:, h, 0:N], psb[b][:, 0:N], [t_ps[b]], [t_mqT])
        if not halo:
            def ret_core(h):
                c, r0 = h // 2, (h % 2) * 64
                rr = slice(r0, r0 + 64)
                mm(3, psb[3][:, :], [(snapS[0:64, k * 4 + h, :], rqd4[0:64, h, :])], [t_snap, t_rqd4])
                for kk in range(4):
                    b = 4 + (kk % 2)
                    mm(b, psb[b][:, :], [(rkT[rr, c, kk * 128:(kk + 1) * 128], rqT[rr, c, :])], [t_rkT, t_rqT])
                    tt(Sd[kk % 2], psb[b][:, :], Tt[:, h, 384 - 128 * kk:896 - 128 * kk], ALU.mult,
                       [t_ps[b], t_tab2], [t_Sd[kk % 2]])
                    mm(0, psb[0][:, :], [(rv[:, kk, h * 128:(h + 1) * 128], Sd[kk % 2])], [t_rv, t_Sd[kk % 2]],
                       start=(kk == 0), stop=(kk == 3))
                cp("act", rf, psb[0][:, :], [t_ps[0]], [t_rf])
                tt(rf, rf, psb[3][:, :], ALU.add, [t_rf, t_ps[3]], [t_rf])

            def ret_finish(h):
                group_norm_gate(512, rf, h)
                stt(rw[:, h, :], cen, col(C_RG, h), sg[:, h, :], ALU.mult, ALU.mult, [t_cen, t_cst, t_sg], [t_rw])
            for h in range(4):
                ret_core(h)
                if h > 0:
                    mem_finish(512, h - 1)
                mem_core(512, h)
                ret_finish(h)
            mem_finish(512, 3)
        else:
            items = []
            for h in range(4):
                for kq4 in range(4):
                    kq = min(kq4, NSLOT - 1)
                    for r in range(2):
                        e = 2 * kq4 + r
                        snap = snapS if r == 1 else snapS2
                        items.append((psb[3][:, h * 8 + e:h * 8 + e + 1], snap[0:64, kq * 4 + h, :],
                                      rqd4[0:64, h, e:e + 1], True, True))
            mml([3], items, [t_snap, t_rqd4])
            cp("act", rf[:, 0:32], psb[3][:, 0:32], [t_ps[3]], [t_rf])
            group_norm_gate(32, rf[:, 0:32], None)
            for h in range(4):
                stt(rw[:, h, 0:8], cen[:, h * 8:(h + 1) * 8], col(C_RG, h), sg[:, h, 0:8], ALU.mult, ALU.mult,
                    [t_cen, t_cst, t_sg], [t_rw])
        if halo:
            for h in range(4):
                mem_attn(N, h)
        P.dma("sp", rT_d[:, :, c0:c0 + N], rw[:, :, 0:N], reads=[t_rw], writes=[t_rTd])
        P.dma("sp", mT_d[:, :, c0:c0 + N], mw[:, :, 0:N], reads=[t_mw], writes=[t_mTd])

    while prefetch3:
        prefetch3.pop(0)()
    if upto < 4:
        P.stop = True
    P.barrier()
    a3 = Alloc(PBASE)
    xb3 = [a3.b16(8, 512) for _ in range(2)]
    xf = a3.f32(8, 512)
    br = a3.b16(12, 512)
    sig = [a3.f32(512) for _ in range(2)]
    acc = a3.f32(512)
    tmpm = a3.f32(512)
    mg = a3.b16(8, 512)
    y = a3.f32(8, 512)
    lnm = a3.f32(512)
    lnv = a3.f32(512)
    lnr = a3.f32(512)
    lnq = [a3.b16(512) for _ in range(2)]
    ybf = [a3.b16(512) for _ in range(2)]
    lnt = [a3.f32(512) for _ in range(2)]
    t_xb3 = [Tk("xb30"), Tk("xb31")]
    t_xf, t_br = Tk("xf"), Tk("br")
    t_sig = [Tk("sig0"), Tk("sig1")]
    t_acc, t_tmpm, t_mg, t_y = Tk("acc"), Tk("tmpm"), Tk("mg"), Tk("y")
    t_lnm, t_lnv, t_lnr = Tk("lnm"), Tk("lnv"), Tk("lnr")
    t_lnq = [Tk("lnq0"), Tk("lnq1")]
    t_ybf = [Tk("ybf0"), Tk("ybf1")]
    t_lnt = [Tk("lnt0"), Tk("lnt1")]
    assert a3.off <= TOP3, f"P3b buffers {a3.off} overlap weights {TOP3}"

    def layer_norm_steps(N, src, t_src, gofs, bofs, dst_fn, t_dst_list):
        steps = []

        def stat(c):
            q = c % 2
            cp("dve" if c % 2 == 0 else "act", ybf[q][:, 0:N], src[:, c, 0:N], [t_src], [t_ybf[q]])
            mm(0, psb[0][:, 0:N], [(onesb1024, ybf[q][:, 0:N])], [t_misc, t_ybf[q]], start=(c == 0), stop=(c == 7))
            act(lnq[q][:, 0:N], src[:, c, 0:N], AF.Square, [t_src], [t_lnq[q]])
            mm(1, psb[1][:, 0:N], [(onesb1024, lnq[q][:, 0:N])], [t_misc, t_lnq[q]], start=(c == 0), stop=(c == 7))

        def mid():
            cp("act", lnm[:, 0:N], psb[0][:, 0:N], [t_ps[0]], [t_lnm])
            tt(lnv[:, 0:N], lnm[:, 0:N], lnm[:, 0:N], ALU.mult, [t_lnm], [t_lnv])
            tt(lnv[:, 0:N], psb[1][:, 0:N], lnv[:, 0:N], ALU.subtract, [t_ps[1], t_lnv], [t_lnv])
            act(lnr[:, 0:N], lnv[:, 0:N], AF.Ln, [t_lnv, t_misc], [t_lnr], bias=epsc[:, 0:1], scale=1.0)
            act(lnr[:, 0:N], lnr[:, 0:N], AF.Exp, [t_lnr], [t_lnr], scale=-0.5)

        def norm(c):
            q = c % 2
            tt(lnt[q][:, 0:N], src[:, c, 0:N], lnm[:, 0:N], ALU.subtract, [t_src, t_lnm], [t_lnt[q]])
            tt(lnt[q][:, 0:N], lnt[q][:, 0:N], lnr[:, 0:N], ALU.mult, [t_lnt[q], t_lnr], [t_lnt[q]])
            act(dst_fn(c), lnt[q][:, 0:N], AF.Identity, [t_lnt[q], t_cst], t_dst_list, bias=col(bofs, c),
                scale=col(gofs, c))
        for c in range(8):
            steps.append(lambda c=c: stat(c))
        steps.append(mid)
        for c in range(8):
            steps.append(lambda c=c: norm(c))
        return steps

    def layer_norm(N, src, t_src, gofs, bofs, dst_fn, t_dst_list):
        for st in layer_norm_steps(N, src, t_src, gofs, bofs, dst_fn, t_dst_list):
            st()

    pending3 = []
    for k in range(NSLOT + 1):
        halo = (k == NSLOT)
        N = 8 if halo else 512
        c0 = NQ if halo else k * 512
        ws = slice(c0, c0 + N)
        xi = k % 2
        X, tX = xb3[xi], t_xb3[xi]

        def loads_a(kk):
            hal = (kk == NSLOT)
            n_ = 8 if hal else 512
            c_ = NQ if hal else kk * 512
            P.dma("pool", xb3[kk % 2][:, :, 0:n_], xq[:, :, c_:c_ + n_], writes=[t_xb3[kk % 2]])
            P.dma("sp", br[:, :, 0:n_], br_d[:, :, c_:c_ + n_], reads=[t_aTd, t_rTd, t_mTd], writes=[t_br], key="br")

        def loads_b(kk):
            hal = (kk == NSLOT)
            n_ = 8 if hal else 512
            c_ = NQ if hal else kk * 512
            P.dma("sp", xf[:, :, 0:n_], xq[:, :, c_:c_ + n_], writes=[t_xf])
        if k == 0:
            loads_a(0)
            loads_b(0)
        for c in range(8):
            for j in range(3):
                if (c, j) >= (1, 0) and pending3:
                    pending3.pop(0)()
                q = (c * 3 + j) % 2
                bG = 4 + q
                bB = 6 + q
                mm(bG, psb[bG][:, 0:N],
                   [(Wg[:, kc, j * 1024 + c * 128:j * 1024 + (c + 1) * 128], X[:, kc, 0:N]) for kc in range(8)],
                   [t_Wg2[c // 4], tX])
                act(sig[q][:, 0:N], psb[bG][:, 0:N], AF.Sigmoid, [t_ps[bG], t_cst], [t_sig[q]],
                    bias=col(C_BG, j * 8 + c), scale=1.0)
                mm(bB, psb[bB][:, 0:N],
                   [(Wbr[:, j * 4 + kc, c * 128:(c + 1) * 128], br[:, j * 4 + kc, 0:N]) for kc in range(4)],
                   [t_Wbr, t_br])
                if j == 0:
                    tt(acc[:, 0:N], sig[q][:, 0:N], psb[bB][:, 0:N], ALU.mult, [t_sig[q], t_ps[bB]], [t_acc])
                else:
                    tt(tmpm[:, 0:N], sig[q][:, 0:N], psb[bB][:, 0:N], ALU.mult, [t_sig[q], t_ps[bB]], [t_tmpm])
                    if j == 1:
                        tt(acc[:, 0:N], acc[:, 0:N], tmpm[:, 0:N], ALU.add, [t_acc, t_tmpm], [t_acc])
                    else:
                        tt(mg[:, c, 0:N], acc[:, 0:N], tmpm[:, 0:N], ALU.add, [t_acc, t_tmpm], [t_mg])
        if k + 1 <= NSLOT:
            loads_a(k + 1)
        for c in range(8):
            b = 2 + (c % 2)
            mm(b, psb[b][:, 0:N], [(Wo[:, kc, c * 128:(c + 1) * 128], mg[:, kc, 0:N]) for kc in range(8)],
               [t_Wo, t_mg])
            stt(y[:, c, 0:N], xf[:, c, 0:N], float(ALPHA), psb[b][:, 0:N], ALU.mult, ALU.add, [t_xf, t_ps[b]], [t_y])
        if k + 1 <= NSLOT:
            loads_b(k + 1)
        pending3 += layer_norm_steps(N, y, t_y, C_L1G, C_L1B, lambda c, N=N: y[:, c, 0:N], [t_y])

        def store(N=N, ws=ws, halo=halo):
            if halo:
                cp("dve", h1Hb, y[:, :, 0:8], [t_y], [t_h1H])
            P.dma("sp", h1_d[:, :, ws], y[:, :, 0:N], reads=[t_y], writes=[t_h1d])
        pending3.append(store)
    while pending3:
        pending3.pop(0)()

    if upto < 5:
        P.stop = True
    P.barrier()
    a4 = Alloc(PBASE)
    Wu = a4.b16(8, 5632)
    Wd = a4.b16(22, 1024)
    hb = [a4.b16(8, 512) for _ in range(2)]
    hf = a4.f32(8, 512)
    adead = Alloc(o_snap)
    tg = [a4.f32(512), adead.f32(512)]
    tv = [a4.f32(512), adead.f32(512)]
    bnd = adead.f32(44, 2)
    btmp = adead.f32(44)
    assert adead.off <= o_snap + 2 * max(NSLOT, 4) * 4 * 128 * 2, "P4 scratch overflows the dead snapshot area"
    t_bnd = Tk("bnd")
    o_actb = a4.off
    actb = a4.b16(22, 512)
    a4b = Alloc(o_actb)
    lnm = a4b.f32(512)
    lnv = a4b.f32(512)
    lnr = a4b.f32(512)
    lnq = [a4b.b16(512) for _ in range(2)]
    ybf = [a4b.b16(512) for _ in range(2)]
    lnt = [tg[0], tv[0]]
    t_Wu4 = [Tk(f"Wu{i}") for i in range(4)]
    t_Wd = Tk("Wd")
    t_hb = [Tk("hb0"), Tk("hb1")]
    t_hf = Tk("hf")
    t_tg = [Tk("tg0"), Tk("tg1")]
    t_tv = [Tk("tv0"), Tk("tv1")]
    t_actb = Tk("actb")
    t_lnm, t_lnv, t_lnr = Tk("lnm4"), Tk("lnv4"), Tk("lnr4")
    t_lnq = [Tk("lnq40"), Tk("lnq41")]
    t_ybf = [Tk("ybf40"), Tk("ybf41")]
    t_lnt = [t_tg[0], t_tv[0]]
    alias4 = [t_lnm, t_lnv, t_lnr] + t_lnq + t_ybf

    def inherit(dst, src):
        dst.w = src.w
        dst.r = dict(src.r)

    def absorb(dst, srcs):
        for t in srcs:
            for tok in ([t.w] if t.w else []) + list(t.r.values()):
                kk = id(tok[0])
                if kk not in dst.r or dst.r[kk][1] < tok[1]:
                    dst.r[kk] = tok
    for i in range(4):
        P.dma("sp", Wu[:, :, i * 1408:(i + 1) * 1408], wu_b[:, :, i * 1408:(i + 1) * 1408], reads=[t_wub],
              writes=[t_Wu4[i]])
    P.dma("sp", Wd, wd_b, reads=[t_wdb], writes=[t_Wd])
    for ch in range(44):
        b = 4 + (ch % 4)
        mm(b, psb[b][:, 0:8], [(Wu[:, kc, ch * 128:(ch + 1) * 128], h1Hb[:, kc, :]) for kc in range(8)],
           [t_Wu4[ch // 11], t_h1H])
        tt(uH[:, ch, :], psb[b][:, 0:8], pc[:, PC_HZ:PC_HZ + 8], ALU.mult, [t_ps[b], t_cst], [t_uH])
    P.dma("pool", hb[0], h1_d[:, :, 0:512], reads=[t_h1d], writes=[t_hb[0]])
    for k in range(NSLOT):
        xi = k % 2
        H, tH = hb[xi], t_hb[xi]
        ws = slice(k * 512, (k + 1) * 512)
        if k + 1 < NSLOT:
            P.dma("pool", hb[1 - xi], h1_d[:, :, (k + 1) * 512:(k + 2) * 512], reads=[t_h1d], writes=[t_hb[1 - xi]])
        P.dma("sp", hf, h1_d[:, :, ws], reads=[t_h1d], writes=[t_hf])
        w0v, w1v = cst[:, C_CW:C_CW + 44], cst[:, C_CW + 44:C_CW + 88]
        tt(bnd[:, :, 1], uH[:, :, 2 * k + 1], w0v, ALU.mult, [t_uH, t_cst], [t_bnd])
        tt(bnd[:, :, 0], uH[:, :, 2 * k], w0v, ALU.mult, [t_uH, t_cst], [t_bnd])
        tt(btmp, uH[:, :, 2 * k + 1], w1v, ALU.mult, [t_uH, t_cst], [t_bnd])
        tt(bnd[:, :, 0], bnd[:, :, 0], btmp, ALU.add, [t_bnd], [t_bnd])
        for c in range(22):
            ci = c % 2
            for br_i, (ch, T, tT) in enumerate(((c, tg[ci], t_tg[ci]), (22 + c, tv[ci], t_tv[ci]))):
                b = 4 + ((2 * c + br_i) % 4)
                w0, w1, w2 = (col(C_CW, t * 44 + ch) for t in range(3))
                mm(b, psb[b][:, :], [(Wu[:, kc, ch * 128:(ch + 1) * 128], H[:, kc, :]) for kc in range(8)],
                   [t_Wu4[ch // 11], tH])
                act(T, psb[b][:, :], AF.Identity, [t_ps[b], t_cst], [tT], bias=col(C_CB, ch), scale=w2)
                stt(T[:, 2:512], psb[b][:, 0:510], w0, T[:, 2:512], ALU.mult, ALU.add, [t_ps[b], t_cst, tT], [tT])
                stt(T[:, 1:512], psb[b][:, 0:511], w1, T[:, 1:512], ALU.mult, ALU.add, [t_ps[b], t_cst, tT], [tT])
                tt(T[:, 0:2], T[:, 0:2], bnd[:, ch, :], ALU.add, [tT, t_bnd], [tT])
            act(tg[ci], tg[ci], AF.Silu, [t_tg[ci]], [t_tg[ci]])
            tt(actb[:, c, :], tv[ci], tg[ci], ALU.mult, [t_tv[ci], t_tg[ci]], [t_actb], eng="pool")
        for c in range(8):
            b = 2 + (c % 2)
            mm(b, psb[b][:, :], [(Wd[:, kc, c * 128:(c + 1) * 128], actb[:, kc, :]) for kc in range(22)],
               [t_Wd, t_actb])
            stt(hf[:, c, :], hf[:, c, :], float(ALPHA), psb[b][:, :], ALU.mult, ALU.add, [t_hf, t_ps[b]], [t_hf])
        for t in alias4:
            inherit(t, t_actb)
        layer_norm(512, hf, t_hf, C_L2G, C_L2B, lambda c: hf[:, c, :], [t_hf])
        absorb(t_actb, alias4)
        P.dma("sp", outT[:, :, ws], hf, reads=[t_hf], writes=[t_out])
    P.final_wait("sp", [t_out, t_aTd, t_rTd, t_mTd, t_h1d])

    with nc.Block() as block:
        @block.tensor
        def _(e):
            for f in P.ops["pe"]:
                f(e)

        @block.scalar
        def _(e):
            for f in P.ops["act"]:
                f(e)

        @block.vector
        def _(e):
            for f in P.ops["dve"]:
                f(e)

        @block.gpsimd
        def _(e):
            for f in P.ops["pool"]:
                f(e)

        @block.sync
        def _(e):
            for f in P.ops["sp"]:
                f(e)
    es.close()
    return nc


def prep_shared(inp):
    f = np.float32
    d = {}
    d["w_in"] = fm(np.asarray(inp["w_in"], f)[0], 8)
    d["w_mkv"] = fm(np.asarray(inp["w_mem_kv"], f)[0], 8)
    d["w_br"] = np.ascontiguousarray(np.concatenate(
        [fm(np.asarray(inp[k], f)[0], 4) for k in ("w_diff_o", "w_ret_o", "w_mem_o")], axis=1))
    d["w_gate"] = fm(np.asarray(inp["w_gate"], f)[0], 8)
    d["w_out"] = fm(np.asarray(inp["w_mix_out"], f)[0], 8)
    d["w_up"] = fm(np.asarray(inp["w_up"], f)[0], 8)
    d["w_dn"] = fm(np.asarray(inp["w_down"], f)[0], 22)
    cs = np.zeros((128, NCONST), f)
    cs[:, C_BG:C_BG + 24] = vec_fm(np.asarray(inp["b_gate"], f)[0], 24)
    cs[:, C_L1G:C_L1G + 8] = vec_fm(np.asarray(inp["ln1_g"], f)[0], 8)
    cs[:, C_L1B:C_L1B + 8] = vec_fm(np.asarray(inp["ln1_b"], f)[0], 8)
    cs[:, C_L2G:C_L2G + 8] = vec_fm(np.asarray(inp["ln2_g"], f)[0], 8)
    cs[:, C_L2B:C_L2B + 8] = vec_fm(np.asarray(inp["ln2_b"], f)[0], 8)
    cw = np.asarray(inp["conv_w"], f)[0]
    for t in range(3):
        cs[:, C_CW + t * 44:C_CW + (t + 1) * 44] = vec_fm(cw[t], 44)
    cs[:, C_CB:C_CB + 44] = vec_fm(np.asarray(inp["conv_b"], f)[0], 44)
    cs[:, C_SG:C_SG + 4] = vec_fm(np.asarray(inp["diff_subln_g"], f)[0], 4)
    cs[:, C_RG:C_RG + 4] = vec_fm(np.asarray(inp["ret_norm_g"], f)[0], 4)
    d["consts"] = cs
    d["dl"] = np.ascontiguousarray(np.broadcast_to(np.asarray(inp["diff_lambda"], f)[0].reshape(1, 256), (128, 256)))
    d["Ctab"], d["Ttab"], d["Gq"], d["gk"] = shared_tables()
    gkT = np.zeros((128, 4, 2, 4, 64), np.float32)
    for tb in range(4):
        for v in range(2):
            for h in range(4):
                gkT[:, tb, v, h, :] = d["gk"][:, tb * 8 + v * 4 + h][:, None]
    d["gkT"] = gkT.reshape(128, 2048)
    d["ident"] = np.eye(128, dtype=np.float32)
    return d


def prep_core(inp, shared, b, j, NSB):
    NSLOT = NSB // 4
    S = NSB * 512
    f = np.float32
    x = np.asarray(inp["x"], f)[b, :S]
    d = dict(shared)
    xT = np.ascontiguousarray(x.T)
    d["xT"] = fm(xT, 8)
    cols = []
    for k in range(NSLOT):
        sb = 4 * k + j
        cols.append(np.arange(sb * 512, (sb + 1) * 512))
    hal = []
    for k in range(4):
        sb = 4 * k + j
        if k < NSLOT and sb > 0:
            hal += [sb * 512 - 2, sb * 512 - 1]
        else:
            hal += [0, 1]
    idx = np.concatenate(cols + [np.asarray(hal)])
    d["xq"] = fm(np.ascontiguousarray(xT[:, idx]), 8)
    d["memT"] = fm(np.ascontiguousarray(np.asarray(inp["mem"], f)[b].T), 8)
    d["kbW"], d["Bh"], d["pc"] = core_tables(j, NSB)
    return d


_NC_CACHE = {}


def run(inp, NSB=16, debug=False, upto=9):
    key = (NSB, debug, upto)
    if key not in _NC_CACHE:
        _NC_CACHE[key] = build(NSB, debug, upto)
    nc = _NC_CACHE[key]
    shared = prep_shared(inp)
    in_maps = [prep_core(inp, shared, c // 4, c % 4, NSB) for c in range(NCORES)]
    res = run_bass_kernel_spmd(nc, in_maps, core_ids=list(range(NCORES)))
    return res.results


def kernel(**inputs):
    NSB = 16
    NSLOT = NSB // 4
    S = NSB * 512
    r = run(inputs, NSB)
    out = np.empty((2, S, 1024), np.float32)
    for c in range(NCORES):
        b, j = c // 4, c % 4
        oT = np.asarray(r[c]["outT"])
        o = oT.transpose(2, 1, 0).reshape(NSLOT * 512, 1024)
        for k in range(NSLOT):
            sb = 4 * k + j
            out[b, sb * 512:(sb + 1) * 512] = o[k * 512:(k + 1) * 512]
    return out
```
